# Optimizing a Trainium2 kernel written in Bass

```python
import math
import jax, jax.numpy as jnp
from jax import lax
import numpy as np

D_MODEL = 1024
BATCH = 8
SEQ = 2048
DEPTH = 4

N_A_LAYERS = DEPTH // 2
N_B_LAYERS = DEPTH - N_A_LAYERS
D_FF = -(-8 * D_MODEL // (3 * 256)) * 256

ML_HEADS = 4
ML_DQK = D_MODEL // (2 * ML_HEADS)
ML_DV = D_MODEL // ML_HEADS
ML_CHUNK = 64
ML_PROJ = 2 * ML_HEADS * ML_DQK + 2 * ML_HEADS * ML_DV + 2 * ML_HEADS

NSA_HEADS = D_MODEL // 64
NSA_GROUPS = 4
NSA_HPG = NSA_HEADS // NSA_GROUPS
NSA_HD = 64
CMP_BLOCK = 32
CMP_STRIDE = 16
CMP_HIDDEN = 4 * NSA_HD
SLC_BLOCK = 64
SLC_TOPK = 8
WINDOW = 512
NSA_QBLOCK = 64
FORCE_SCORE = 1e4
NSA_Q_PROJ = NSA_HEADS * NSA_HD + 3 * NSA_HEADS
NSA_KV_PROJ = 6 * NSA_GROUPS * NSA_HD

REL_BUCKETS = 32
REL_MAX_DIST = 128

kernel_name = "hybrid_mlstm_nsa_yoco"


def rms_norm(x, g, eps=1e-6):
    xf = x.astype(jnp.float32)
    y = xf * lax.rsqrt(jnp.mean(xf * xf, axis=-1, keepdims=True) + eps)
    return (y * g.astype(jnp.float32)).astype(x.dtype)


def modulate(h, shift, scale):
    return h * (1 + scale[:, None, :]) + shift[:, None, :]


def swiglu(h, w_in, w_out):
    g, u = jnp.split(h @ w_in, 2, axis=-1)
    return (jax.nn.silu(g) * u) @ w_out


def masked_softmax(s, valid):
    s = jnp.where(valid, s.astype(jnp.float32), -jnp.inf)
    m = jnp.max(s, axis=-1, keepdims=True)
    m = jnp.where(jnp.isfinite(m), m, 0.0)
    e = jnp.exp(s - m)
    return e / jnp.maximum(jnp.sum(e, axis=-1, keepdims=True), jnp.finfo(jnp.float32).tiny)


def t5_bucket(dist):
    n = jnp.maximum(dist, 0)
    max_exact = REL_BUCKETS // 2
    nf = jnp.maximum(n, 1).astype(jnp.float32)
    large = max_exact + (jnp.log(nf / max_exact) / math.log(REL_MAX_DIST / max_exact)
                         * (REL_BUCKETS - max_exact)).astype(jnp.int32)
    large = jnp.minimum(large, REL_BUCKETS - 1)
    return jnp.where(n < max_exact, n, large)


def dense_bias(dist, rel_table):
    b = rel_table[t5_bucket(dist)]
    q, k = dist.shape
    return b.reshape(q, k, NSA_GROUPS, NSA_HPG).transpose(2, 3, 0, 1).astype(jnp.float32)


def mlstm_mixer(h, w_in, b_if, g_out, w_out):
    B, S, _ = h.shape
    NC = S // ML_CHUNK
    L = ML_CHUNK
    qk = ML_HEADS * ML_DQK
    vd = ML_HEADS * ML_DV
    proj = h @ w_in
    q, k, v, o, gif = jnp.split(proj, [qk, 2 * qk, 2 * qk + vd, 2 * qk + 2 * vd], axis=-1)
    gif = (gif + b_if).astype(jnp.float32)

    def heads(t, d):
        return t.reshape(B, NC, L, ML_HEADS, d).transpose(0, 3, 1, 2, 4)

    def gate_heads(t):
        return t.reshape(B, NC, L, ML_HEADS).transpose(0, 3, 1, 2)

    q = heads(q, ML_DQK) * (ML_DQK ** -0.5)
    k = heads(k, ML_DQK)
    v = heads(v, ML_DV)
    logi = gate_heads(gif[..., :ML_HEADS])
    logf = jax.nn.log_sigmoid(gate_heads(gif[..., ML_HEADS:]))
    b = jnp.cumsum(logf, axis=-1)
    b_last = b[..., -1]

    w_state = b_last[..., None] - b + logi
    m_loc = jnp.max(w_state, axis=-1)
    e_state = jnp.exp(w_state - m_loc[..., None])
    kv_loc = jnp.einsum('bhcs,bhcsd,bhcsv->bhcdv', e_state, k, v)
    n_loc = jnp.einsum('bhcs,bhcsd->bhcd', e_state, k)

    def step(carry, xs):
        C, n, m = carry
        bl, ml, kvl, nl = xs
        m_new = jnp.maximum(bl + m, ml)
        a = jnp.exp(bl + m - m_new)
        bb = jnp.exp(ml - m_new)
        C_new = a[..., None, None] * C + bb[..., None, None] * kvl
        n_new = a[..., None] * n + bb[..., None] * nl
        return (C_new, n_new, m_new), (C, n, m)

    init = (jnp.zeros((B, ML_HEADS, ML_DQK, ML_DV), jnp.float32),
            jnp.zeros((B, ML_HEADS, ML_DQK), jnp.float32),
            jnp.zeros((B, ML_HEADS), jnp.float32))
    xs = (jnp.moveaxis(b_last, 2, 0), jnp.moveaxis(m_loc, 2, 0),
          jnp.moveaxis(kv_loc, 2, 0).astype(jnp.float32), jnp.moveaxis(n_loc, 2, 0).astype(jnp.float32))
    _, (C_prev, n_prev, m_prev) = lax.scan(step, init, xs)
    C_prev = jnp.moveaxis(C_prev, 0, 2)
    n_prev = jnp.moveaxis(n_prev, 0, 2)
    m_prev = jnp.moveaxis(m_prev, 0, 2)

    causal = np.tril(np.ones((L, L), dtype=bool))
    d_intra = jnp.where(causal, b[..., :, None] - b[..., None, :] + logi[..., None, :], -jnp.inf)
    inter_log = b + m_prev[..., None]
    m = jnp.maximum(inter_log, jnp.max(d_intra, axis=-1))
    w_intra = jnp.exp(d_intra - m[..., None])
    w_inter = jnp.exp(inter_log - m)
    s = jnp.einsum('bhcjd,bhcsd->bhcjs', q, k) * w_intra
    num = jnp.einsum('bhcjs,bhcsv->bhcjv', s, v) \
        + w_inter[..., None] * jnp.einsum('bhcjd,bhcdv->bhcjv', q, C_prev)
    qn = jnp.sum(s, axis=-1) + w_inter * jnp.einsum('bhcjd,bhcd->bhcj', q, n_prev)
    denom = jnp.maximum(jnp.abs(qn), jnp.exp(-m))
    hout = num / denom[..., None]
    hout = hout.transpose(0, 2, 3, 1, 4).reshape(B, S, ML_HEADS, ML_DV)
    hout = rms_norm(hout, g_out.reshape(ML_HEADS, ML_DV)).reshape(B, S, vd)
    return (hout * jax.nn.sigmoid(o)) @ w_out


def nsa_shared_kv(h, w_kv, pos_cmp_k, w_cmp_k1, w_cmp_k2, pos_cmp_v, w_cmp_v1, w_cmp_v2, g_knorm):
    B, S, _ = h.shape
    G = NSA_GROUPS
    kv = (h @ w_kv).reshape(B, S, 6, G, NSA_HD).transpose(2, 0, 3, 1, 4)
    kc, vc, ks, vs, kw, vw = kv[0], kv[1], kv[2], kv[3], kv[4], kv[5]
    ncmp = (S - CMP_BLOCK) // CMP_STRIDE + 1
    idx = np.arange(ncmp)[:, None] * CMP_STRIDE + np.arange(CMP_BLOCK)[None, :]

    def compress(t, pos, w1, w2):
        blk = t[:, :, idx] + pos
        return jax.nn.gelu(blk.reshape(B, G, ncmp, CMP_BLOCK * NSA_HD) @ w1) @ w2

    kc_cmp = rms_norm(compress(kc, pos_cmp_k, w_cmp_k1, w_cmp_k2), g_knorm[0])
    vc_cmp = compress(vc, pos_cmp_v, w_cmp_v1, w_cmp_v2)
    nsb = S // SLC_BLOCK
    ks_blk = rms_norm(ks, g_knorm[1]).reshape(B, G, nsb, SLC_BLOCK, NSA_HD)
    vs_blk = vs.reshape(B, G, nsb, SLC_BLOCK, NSA_HD)
    pad = ((0, 0), (0, 0), (WINDOW, 0), (0, 0))
    kw_pad = jnp.pad(rms_norm(kw, g_knorm[2]), pad)
    vw_pad = jnp.pad(vw, pad)
    return kc_cmp, vc_cmp, ks_blk, vs_blk, kw_pad, vw_pad


def cmp_to_slc_overlap(ncmp, nsb):
    start = np.arange(ncmp) * CMP_STRIDE
    sj = np.arange(nsb) * SLC_BLOCK
    ov = np.minimum(start[:, None] + CMP_BLOCK, sj[None, :] + SLC_BLOCK) - np.maximum(start[:, None], sj[None, :])
    return (np.clip(ov, 0, None) / CMP_BLOCK).astype(np.float32)


def nsa_mixer(h, kc, vc, ks_blk, vs_blk, kw_pad, vw_pad, w_q, b_gate, g_qnorm, rel_table, w_out):
    B, S, _ = h.shape
    G, HPG, HD = NSA_GROUPS, NSA_HPG, NSA_HD
    QB = NSA_QBLOCK
    proj = h @ w_q
    q = rms_norm(proj[..., :NSA_HEADS * HD].reshape(B, S, G, HPG, HD), g_qnorm) * (HD ** -0.5)
    q = q.transpose(0, 2, 3, 1, 4)
    gates = jax.nn.sigmoid((proj[..., NSA_HEADS * HD:] + b_gate).astype(jnp.float32))
    gates = gates.reshape(B, S, G, HPG, 3).transpose(0, 2, 3, 1, 4)
    ncmp = kc.shape[2]
    nsb = S // SLC_BLOCK
    topk = min(SLC_TOPK, nsb)
    overlap = cmp_to_slc_overlap(ncmp, nsb)
    cmp_end = np.arange(ncmp) * CMP_STRIDE + CMP_BLOCK - 1
    table_g = rel_table.reshape(REL_BUCKETS, G, HPG).transpose(1, 0, 2)
    bi = jnp.arange(B)[:, None, None]
    gi = jnp.arange(G)[None, :, None]

    def block_step(qb):
        t0 = qb * QB
        qt = t0 + jnp.arange(QB)
        qblk = lax.dynamic_slice_in_dim(q, t0, QB, axis=3)
        gblk = lax.dynamic_slice_in_dim(gates, t0, QB, axis=3)

        dist_c = qt[:, None] - cmp_end[None, :]
        s_c = jnp.einsum('bghqd,bgkd->bghqk', qblk, kc) + dense_bias(dist_c, rel_table)
        p_c = masked_softmax(s_c, dist_c >= 0)
        o_c = jnp.einsum('bghqk,bgkd->bghqd', p_c.astype(vc.dtype), vc)

        imp = jnp.einsum('bghqk,kj->bgqj', p_c, overlap)
        jb = jnp.arange(nsb)[None, :]
        qblk_id = (qt // SLC_BLOCK)[:, None]
        forced = (jb == 0) | (jb == qblk_id) | (jb == qblk_id - 1)
        score = jnp.where(forced, FORCE_SCORE, imp)
        score = jnp.where(jb <= qblk_id, score, -jnp.inf)
        top_s, top_idx = lax.top_k(score, topk)
        sel_valid = jnp.isfinite(top_s)

        flat_idx = top_idx.reshape(B, G, QB * topk)
        k_sel = ks_blk[bi, gi, flat_idx].reshape(B, G, QB, topk * SLC_BLOCK, HD)
        v_sel = vs_blk[bi, gi, flat_idx].reshape(B, G, QB, topk * SLC_BLOCK, HD)
        pos_s = (top_idx[..., None] * SLC_BLOCK + jnp.arange(SLC_BLOCK)).reshape(B, G, QB, topk * SLC_BLOCK)
        dist_s = qt[None, None, :, None] - pos_s
        valid_s = (dist_s >= 0) & jnp.repeat(sel_valid, SLC_BLOCK, axis=-1)
        bias_s = table_g[jnp.arange(G)[None, :, None, None], t5_bucket(dist_s)]
        s_s = jnp.einsum('bghqd,bgqkd->bghqk', qblk, k_sel) + bias_s.transpose(0, 1, 4, 2, 3).astype(jnp.float32)
        p_s = masked_softmax(s_s, valid_s[:, :, None])
        o_s = jnp.einsum('bghqk,bgqkd->bghqd', p_s.astype(v_sel.dtype), v_sel)

        k_w = lax.dynamic_slice_in_dim(kw_pad, t0, WINDOW + QB, axis=2)
        v_w = lax.dynamic_slice_in_dim(vw_pad, t0, WINDOW + QB, axis=2)
        pos_w = t0 - WINDOW + jnp.arange(WINDOW + QB)
        dist_w = qt[:, None] - pos_w[None, :]
        valid_w = (dist_w >= 0) & (dist_w < WINDOW) & (pos_w[None, :] >= 0)
        s_w = jnp.einsum('bghqd,bgkd->bghqk', qblk, k_w) + dense_bias(dist_w, rel_table)
        p_w = masked_softmax(s_w, valid_w)
        o_w = jnp.einsum('bghqk,bgkd->bghqd', p_w.astype(v_w.dtype), v_w)

        return gblk[..., 0:1] * o_c + gblk[..., 1:2] * o_s + gblk[..., 2:3] * o_w

    out = lax.map(block_step, jnp.arange(S // QB))
    out = out.transpose(1, 0, 4, 2, 3, 5).reshape(B, S, NSA_HEADS * HD)
    return out @ w_out


def setup_inputs(seed: int = 0) -> dict:
    key = jax.random.key(seed)
    ks = jax.random.split(key, 32)
    D = D_MODEL

    def nrm(k, shape, scale):
        return jax.random.normal(k, shape, jnp.float32) * scale

    def gain(k, shape):
        return 1.0 + 0.02 * jax.random.normal(k, shape, jnp.float32)

    forget_bias = 3.0 + jnp.linspace(0.0, 3.0, ML_HEADS, dtype=jnp.float32)
    b_a_if = jnp.concatenate([nrm(ks[9], (N_A_LAYERS, ML_HEADS), 0.1),
                              forget_bias[None, :] + nrm(ks[10], (N_A_LAYERS, ML_HEADS), 0.1)], axis=-1)
    return {
        "x": nrm(ks[0], (BATCH, SEQ, D), 1.0),
        "c": nrm(ks[1], (BATCH, D), 1.0),
        "w_ada": nrm(ks[2], (DEPTH, D, 6 * D), 0.5 * D ** -0.5),
        "b_ada": nrm(ks[3], (DEPTH, 6 * D), 0.02),
        "g_norm_mix": gain(ks[4], (DEPTH, D)),
        "g_norm_ffn": gain(ks[5], (DEPTH, D)),
        "w_ffn_in": nrm(ks[6], (DEPTH, D, 2 * D_FF), D ** -0.5),
        "w_ffn_out": nrm(ks[7], (DEPTH, D_FF, D), D_FF ** -0.5),
        "w_a_in": nrm(ks[8], (N_A_LAYERS, D, ML_PROJ), D ** -0.5),
        "b_a_if": b_a_if,
        "g_a_out": gain(ks[11], (N_A_LAYERS, ML_HEADS * ML_DV)),
        "w_a_out": nrm(ks[12], (N_A_LAYERS, ML_HEADS * ML_DV, D), (ML_HEADS * ML_DV) ** -0.5),
        "w_kv_ada": nrm(ks[13], (D, 2 * D), 0.5 * D ** -0.5),
        "b_kv_ada": nrm(ks[14], (2 * D,), 0.02),
        "g_kv_norm": gain(ks[15], (D,)),
        "w_kv": nrm(ks[16], (D, NSA_KV_PROJ), D ** -0.5),
        "pos_cmp_k": nrm(ks[17], (CMP_BLOCK, NSA_HD), 0.1),
        "w_cmp_k1": nrm(ks[18], (CMP_BLOCK * NSA_HD, CMP_HIDDEN), (CMP_BLOCK * NSA_HD) ** -0.5),
        "w_cmp_k2": nrm(ks[19], (CMP_HIDDEN, NSA_HD), CMP_HIDDEN ** -0.5),
        "pos_cmp_v": nrm(ks[20], (CMP_BLOCK, NSA_HD), 0.1),
        "w_cmp_v1": nrm(ks[21], (CMP_BLOCK * NSA_HD, CMP_HIDDEN), (CMP_BLOCK * NSA_HD) ** -0.5),
        "w_cmp_v2": nrm(ks[22], (CMP_HIDDEN, NSA_HD), CMP_HIDDEN ** -0.5),
        "g_knorm": gain(ks[23], (3, NSA_HD)),
        "w_b_q": nrm(ks[24], (N_B_LAYERS, D, NSA_Q_PROJ), D ** -0.5),
        "b_b_gate": nrm(ks[25], (N_B_LAYERS, 3 * NSA_HEADS), 0.1),
        "g_qnorm": gain(ks[26], (N_B_LAYERS, NSA_HD)),
        "w_b_out": nrm(ks[27], (N_B_LAYERS, NSA_HEADS * NSA_HD, D), (NSA_HEADS * NSA_HD) ** -0.5),
        "rel_table": nrm(ks[28], (REL_BUCKETS, NSA_HEADS), 0.5),
    }


def reference(x, c, w_ada, b_ada, g_norm_mix, g_norm_ffn, w_ffn_in, w_ffn_out,
              w_a_in, b_a_if, g_a_out, w_a_out,
              w_kv_ada, b_kv_ada, g_kv_norm, w_kv, pos_cmp_k, w_cmp_k1, w_cmp_k2,
              pos_cmp_v, w_cmp_v1, w_cmp_v2, g_knorm,
              w_b_q, b_b_gate, g_qnorm, w_b_out, rel_table):
    c_act = jax.nn.silu(c)
    shared = None
    for layer in range(DEPTH):
        mod = c_act @ w_ada[layer] + b_ada[layer]
        sh1, sc1, ga1, sh2, sc2, ga2 = jnp.split(mod, 6, axis=-1)
        h = modulate(rms_norm(x, g_norm_mix[layer]), sh1, sc1)
        if layer < N_A_LAYERS:
            y = mlstm_mixer(h, w_a_in[layer], b_a_if[layer], g_a_out[layer], w_a_out[layer])
        else:
            if layer == N_A_LAYERS:
                kv_sh, kv_sc = jnp.split(c_act @ w_kv_ada + b_kv_ada, 2, axis=-1)
                h_kv = modulate(rms_norm(x, g_kv_norm), kv_sh, kv_sc)
                shared = nsa_shared_kv(h_kv, w_kv, pos_cmp_k, w_cmp_k1, w_cmp_k2,
                                       pos_cmp_v, w_cmp_v1, w_cmp_v2, g_knorm)
            j = layer - N_A_LAYERS
            kc, vc, ks_blk, vs_blk, kw_pad, vw_pad = shared
            y = nsa_mixer(h, kc, vc, ks_blk, vs_blk, kw_pad, vw_pad,
                          w_b_q[j], b_b_gate[j], g_qnorm[j], rel_table, w_b_out[j])
        x = x + (ga1[:, None, :] * y).astype(x.dtype)
        h = modulate(rms_norm(x, g_norm_ffn[layer]), sh2, sc2)
        x = x + (ga2[:, None, :] * swiglu(h, w_ffn_in[layer], w_ffn_out[layer])).astype(x.dtype)
    return x
```

```python
import math
import numpy as np
import ml_dtypes
import concourse.bass as bass
import concourse.mybir as mybir
from concourse.bass_utils import run_bass_kernel_spmd

F32 = mybir.dt.float32
BF16 = mybir.dt.bfloat16
AF = mybir.ActivationFunctionType
ALU = mybir.AluOpType
AX = mybir.AxisListType

D = 1024
S = 2048
DEPTH = 4
DFF = 2816
NT = S // 128
NB = S // 512
KC = D // 128

EPOCH = 4000
COMPUTE = ("pe", "act", "dve", "pool")
NDMA = {"sp": 20, "pool": 8, "act": 8}


class Prog:
    def __init__(self, nc, same_engine_sync=True):
        self.nc = nc
        self.same_sync = same_engine_sync
        self.ops = {e: [] for e in ("pe", "act", "dve", "pool", "sp")}
        self.cnt = {e: 0 for e in COMPUTE}
        self.clock = {e: {} for e in self.ops}
        self.iclock = {e: [] for e in COMPUTE}
        self.sems = {e: [] for e in COMPUTE}
        self.dma_sems = {q: [nc.alloc_semaphore(f"dq_{q}_{i}") for i in range(n)] for q, n in NDMA.items()}
        self.dma_n = {q: 0 for q in NDMA}
        self.dma_done = {e: set() for e in self.ops}
        self.dma_info = {}
        self.lastw = {}
        self.reads = {}
        self.nwaits = 0

    def _sem(self, e, n):
        ep = n // EPOCH
        while len(self.sems[e]) <= ep:
            self.sems[e].append(self.nc.alloc_semaphore(f"s_{e}_{len(self.sems[e])}"))
        return self.sems[e][ep], n % EPOCH + 1

    def _collect(self, eng, reads, writes):
        deps = set()
        for k in reads:
            w = self.lastw.get(k)
            if w is not None:
                deps.add(w)
        for k in writes:
            w = self.lastw.get(k)
            if w is not None:
                deps.add(w)
            for r in self.reads.get(k, ()):
                deps.add(r)
        waits = []
        ck = self.clock[eng]
        import os as _os
        _rev = _os.environ.get("WREV", "0") == "1"
        for d in sorted(deps, key=lambda d: (str(d[0]), str(d[1])), reverse=_rev):
            if d[0] == "dma":
                if d[1] in self.dma_done[eng]:
                    continue
                self.dma_done[eng].add(d[1])
                waits.append(self.dma_info[d[1]])
            else:
                src, n = d
                if src == eng and (src == "pe" or not self.same_sync):
                    continue
                if ck.get(src, 0) >= n + 1:
                    continue
                waits.append(self._sem(src, n))
                for s2, c2 in self.iclock[src][n].items():
                    if ck.get(s2, 0) < c2:
                        ck[s2] = c2
                if ck.get(src, 0) < n + 1:
                    ck[src] = n + 1
        self.nwaits += len(waits)
        return waits

    def _record(self, tag, reads, writes):
        for k in writes:
            self.lastw[k] = tag
            self.reads[k] = []
        for k in reads:
            if k not in writes:
                self.reads.setdefault(k, []).append(tag)

    def op(self, eng, fn, reads=(), writes=()):
        reads = list(reads); writes = list(writes)
        ex = [k for k in reads if isinstance(k, tuple) and k[0] == "ps" and k not in writes]
        if ex:
            reads = [k for k in reads if k not in ex]
            writes = writes + ex
        waits = self._collect(eng, reads, writes)
        n = self.cnt[eng]
        self.cnt[eng] += 1
        sem, val = self._sem(eng, n)
        self.iclock[eng].append(dict(self.clock[eng]))
        self.ops[eng].append((waits, fn, (sem, 1)))
        self._record((eng, n), reads, writes)

    def dma(self, q, fn, reads=(), writes=()):
        reads = list(reads); writes = list(writes)
        waits = self._collect(q, reads, writes)
        j = self.dma_n[q]
        self.dma_n[q] += 1
        M = len(self.dma_sems[q])
        sem = self.dma_sems[q][j % M]
        target = 16 * (j // M + 1)
        if j >= M:
            prev = (q, j - M)
            if prev not in self.dma_done[q]:
                self.dma_done[q].add(prev)
                waits.append(self.dma_info[prev])
        did = (q, j)
        self.dma_info[did] = (sem, target)
        if q in COMPUTE:
            pass
        self.ops[q].append((waits, fn, (sem, 16)))
        self._record(("dma", did), reads, writes)
        return did

    def alias(self, new_keys, old_keys):
        tags = []
        for k in old_keys:
            w = self.lastw.get(k)
            if w is not None:
                tags.append(w)
            tags.extend(self.reads.get(k, ()))
        tags = list(dict.fromkeys(tags))
        for k in new_keys:
            self.lastw[k] = None
            self.reads[k] = list(tags)

    def finish(self, final_keys):
        nc = self.nc
        waits = self._collect("sp", list(final_keys), [])
        self.ops["sp"].append((waits, None, None))
        emap = {"pe": "tensor", "act": "scalar", "dve": "vector", "pool": "gpsimd", "sp": "sync"}
        with nc.Block() as block:
            for e, attr in emap.items():
                lst = self.ops[e]

                def body(engobj, lst=lst):
                    for waits, fn, inc in lst:
                        for s, v in waits:
                            engobj.wait_ge(s, v)
                        if fn is not None:
                            ins = fn(engobj)
                            ins.then_inc(inc[0], inc[1])
                getattr(block, attr)(body)


def t5_bucket_np(dist):
    n = np.maximum(dist, 0)
    nf = np.maximum(n, 1).astype(np.float32)
    large = 16 + (np.log(nf / np.float32(16)) / np.float32(math.log(8.0)) * np.float32(16)).astype(np.int32)
    large = np.minimum(large, 31)
    return np.where(n < 16, n, large)


NF = 4096
FOFF = 2048


def host_consts():
    c = {}
    c["ident_f"] = np.eye(128, dtype=np.float32)
    dist = NF - 1 - np.arange(NF) - FOFF
    bk = t5_bucket_np(dist)
    oh = np.zeros((33, NF), np.float32)
    oh[bk, np.arange(NF)] = 1.0
    oh[31, :] -= 1.0
    oh[:32, dist < 0] = 0.0
    oh[32, :] = np.where(dist < 0, -30000.0, 0.0)
    c["onehot"] = oh
    c["causalT"] = np.triu(np.ones((64, 64), np.float32))
    sm = np.ones((4, S), np.float32); sm[:, ::64] = 0.0
    c["scanmask"] = sm
    hm = np.zeros((4, 4, 32), np.float32)
    for h in range(4):
        hm[h, h, :] = 1.0
    c["hmask"] = hm.reshape(4, 128)
    tl = np.arange(128)
    c["d4mask"] = np.where(tl[None, :] < tl[:, None], 0.0, -30000.0).astype(np.float32)
    ex = np.zeros((32, S), np.float32)
    ex[np.arange(S) // 64, np.arange(S)] = 1.0
    c["expand"] = ex
    ac = np.zeros((128, 16, 32), np.float32)
    for qi in range(16):
        for t in range(128):
            qb = (qi * 128 + t) // 64
            for jb in range(32):
                if jb > qb:
                    ac[t, qi, jb] = -1e30
                elif jb == 0 or jb == qb or jb == qb - 1:
                    ac[t, qi, jb] = 1e4
    c["addc"] = ac.reshape(128, 512)
    start = np.arange(127) * 16
    sj = np.arange(32) * 64
    ov = np.minimum(start[:, None] + 32, sj[None, :] + 64) - np.maximum(start[:, None], sj[None, :])
    c["overlap"] = (np.clip(ov, 0, None) / 32).astype(np.float32)
    bo = np.zeros((128, 128), np.float32); bo[:64, :64] = 1; bo[64:, 64:] = 1
    c["blockones"] = bo
    return c


class B:
    def __init__(self, nlayers=DEPTH, mixers=True, debug=None):
        self.nlayers = nlayers
        self.mixers = mixers
        self.debug = debug
        nc = self.nc = bass.Bass("TRN2", target_bir_lowering=False)
        self.P = Prog(nc)
        self.din = {}
        self.sb_off = 16512
        self.sb_end = 229344
        self.scr_end = 229344
        self.uid = 0
        self.psn = 0

    def dram_in(self, name, shape, dt=F32):
        t = self.nc.dram_tensor(name, list(shape), dt, kind="ExternalInput")
        self.din[name] = t
        return t.ap()

    def sb(self, name, shape, dt, off=None):
        nbytes = int(np.prod(shape[1:])) * (4 if dt == F32 else 2)
        if off is None:
            off = self.sb_off
            self.sb_off += (nbytes + 63) // 64 * 64
            assert self.sb_off <= self.sb_end, (name, self.sb_off)
        self.uid += 1
        return self.nc.alloc_sbuf_tensor_at(f"{name}_{self.uid}", list(shape), dt, offset=off)

    def mm(self, out, lhsT, rhs, start, stop, reads, writes):
        self.P.op("pe", lambda e: e.matmul(out, lhsT, rhs, start=start, stop=stop), reads, writes)

    def tr(self, out, in_, ident, reads, writes):
        self.P.op("pe", lambda e: e.transpose(out, in_, ident), reads, writes)

    def act(self, out, in_, func, reads, writes, bias=None, scale=None, accum_out=None):
        kw = {}
        if bias is not None:
            kw["bias"] = bias
        if scale is not None:
            kw["scale"] = scale
        if accum_out is not None:
            kw["accum_out"] = accum_out
        self.P.op("act", lambda e: e.activation(out=out, in_=in_, func=func, **kw), reads, writes)

    def tt(self, out, in0, in1, op, reads, writes, eng="dve"):
        self.P.op(eng, lambda e: e.tensor_tensor(out=out, in0=in0, in1=in1, op=op), reads, writes)

    def ts(self, out, in0, s1, op0, reads, writes, s2=None, op1=None, eng="dve", accum_out=None):
        kw = {}
        if op1 is not None:
            kw["op1"] = op1
        if accum_out is not None:
            kw["accum_out"] = accum_out
        self.P.op(eng, lambda e: e.tensor_scalar(out=out, in0=in0, scalar1=s1, scalar2=s2, op0=op0, **kw), reads, writes)

    def stt(self, out, in0, scalar, in1, op0, op1, reads, writes):
        self.P.op("dve", lambda e: e.scalar_tensor_tensor(out=out, in0=in0, scalar=scalar, in1=in1, op0=op0, op1=op1), reads, writes)

    def copy(self, out, in_, reads, writes, eng="dve"):
        self.P.op(eng, lambda e: e.tensor_copy(out=out, in_=in_), reads, writes)

    def dma(self, out, in_, reads, writes, q="sp", **kw):
        self.P.dma(q, lambda e: e.dma_start(out=out, in_=in_, **kw), reads, writes)

    def build(self):
        nc, P = self.nc, self.P
        L = self.nlayers
        x_d = self.dram_in("x", [S, D])
        vec_d = {}
        w_ada = self.dram_in("w_ada", [DEPTH, D, 6 * D])
        b_ada = self.dram_in("b_ada", [DEPTH, 6 * D])
        g_mix = self.dram_in("g_norm_mix", [DEPTH, D])
        g_ffn = self.dram_in("g_norm_ffn", [DEPTH, D])
        w_fi = self.dram_in("w_ffn_in", [DEPTH, D, 2 * DFF])
        w_fo = self.dram_in("w_ffn_out", [DEPTH, DFF, D])
        c_d = self.dram_in("c", [1, D])
        ident_d = self.dram_in("ident_f", [128, 128])
        self.w_a_in = self.dram_in("w_a_in", [2, D, 3080])
        self.b_a_if = self.dram_in("b_a_if", [2, 8])
        g_a_out = self.dram_in("g_a_out", [2, D])
        self.w_a_out = self.dram_in("w_a_out", [2, D, D])
        self.causalT_d = self.dram_in("causalT", [64, 64])
        self.w_kv_ada = self.dram_in("w_kv_ada", [D, 2 * D])
        b_kv_ada = self.dram_in("b_kv_ada", [1, 2 * D])
        g_kv_norm = self.dram_in("g_kv_norm", [1, D])
        self.w_kv = self.dram_in("w_kv", [D, 1536])
        self.pos_cmp = [self.dram_in("pos_cmp_k", [32, 64]), self.dram_in("pos_cmp_v", [32, 64])]
        self.w_cmp1 = [self.dram_in("w_cmp_k1", [2048, 256]), self.dram_in("w_cmp_v1", [2048, 256])]
        self.w_cmp2 = [self.dram_in("w_cmp_k2", [256, 64]), self.dram_in("w_cmp_v2", [256, 64])]
        self.g_knorm = self.dram_in("g_knorm", [3, 64])
        self.w_b_q = self.dram_in("w_b_q", [2, D, 1072])
        self.b_b_gate = self.dram_in("b_b_gate", [2, 48])
        self.g_qnorm = self.dram_in("g_qnorm", [2, 64])
        self.w_b_out = self.dram_in("w_b_out", [2, D, D])
        self.rel_table = self.dram_in("rel_table", [32, 16])
        self.onehot_d = self.dram_in("onehot", [33, NF])
        self.d4mask_d = self.dram_in("d4mask", [128, 128])
        self.expand_d = self.dram_in("expand", [32, S])
        self.addc_d = self.dram_in("addc", [128, 512])
        self.overlap_d = self.dram_in("overlap", [127, 32])
        self.blockones_d = self.dram_in("blockones", [128, 128])
        self.fvd = nc.dram_tensor("fvd", [16, NF], F32).ap()
        self.scanmask_d = self.dram_in("scanmask", [4, S])
        self.hmask_d = self.dram_in("hmask", [4, 128])
        out_d = nc.dram_tensor("out", [S, D], F32, kind="ExternalOutput").ap()

        xT = self.xT = self.sb("xT", [128, KC, S], F32)
        hT = self.hT = self.sb("hT", [128, KC, S], BF16)
        ident_f = self.ident_f = self.sb("ident_f", [128, 128], F32)
        ident_b = self.ident_b = self.sb("ident_b", [128, 128], BF16)
        ones_b = self.ones_b = self.sb("ones_b", [128, 128], BF16)
        NV = 304
        vecT = self.vecT = self.sb("vecT", [128, NV], F32)
        cact = self.cact = self.sb("cact", [128, KC], F32)
        mod = self.mod = self.sb("mod", [128, DEPTH, 48], F32)
        modkv = self.modkv = self.sb("modkv", [128, 24], F32)
        gs = self.gs = self.sb("gs", [128, DEPTH, 2, KC], F32)
        epsc = self.epsc = self.sb("epsc", [128, 1], F32)
        self.persist_end = self.sb_off
        scr0 = self.sb_off
        ps = self.ps = [nc.alloc_psum_tensor(f"ps{i}", [128, 512], F32) for i in range(8)]

        self.dma(ident_f[:], ident_d, [], ["ident_f"])
        self.copy(ident_b[:], ident_f[:], ["ident_f"], ["ident_b"])
        P.op("dve", lambda e: e.memset(ones_b[:], 1.0), [], ["ones_b"])
        P.op("dve", lambda e: e.memset(epsc[:], 1e-6), [], ["epsc"])

        vrows = [self.sb(f"vrows{i}", [128, 128], F32, off=scr0 + i * 512) for i in range(3)]
        srcs = [(b_ada.rearrange("l (m p) -> (l m) p", p=128), 0, 192),
                (g_mix.rearrange("l (m p) -> (l m) p", p=128), 192, 32),
                (g_ffn.rearrange("l (m p) -> (l m) p", p=128), 224, 32),
                (g_kv_norm.rearrange("o (m p) -> (o m) p", p=128), 256, 8),
                (b_kv_ada.rearrange("o (m p) -> (o m) p", p=128), 264, 16),
                (g_a_out.rearrange("l (m p) -> (l m) p", p=128), 280, 16),
                (c_d.rearrange("o (m p) -> (o m) p", p=128), 296, 8)]
        for i in range(3):
            P.op("dve", lambda e, i=i: e.memset(vrows[i][:], 0.0), [], [("vrows", i)])
        for ap, r0, n in srcs:
            r = r0
            while r < r0 + n:
                ti = r // 128
                cnt = min(r0 + n - r, (ti + 1) * 128 - r)
                self.dma(vrows[ti][r - ti * 128: r - ti * 128 + cnt, :], ap[r - r0: r - r0 + cnt, :],
                         [], [("vrows", ti)])
                r += cnt
        for i in range(3):
            n = min(128, NV - i * 128)
            self.tr(ps[0][:, i * 128: i * 128 + n], vrows[i][0:n, :], ident_f[0:n, 0:n],
                    [("vrows", i), "ident_f"], [("ps", 0)])
        self.copy(vecT[:], ps[0][:, 0:NV], [("ps", 0)], ["vecT"])
        self.act(cact[:], vecT[:, 296:304], AF.Silu, ["vecT"], ["cact"])

        xin = [self.sb(f"xin{i}", [128, D], F32, off=scr0 + 2048 + i * 4096) for i in range(2)]
        for t in range(NT):
            b = t % 2
            self.dma(xin[b][:], x_d[t * 128:(t + 1) * 128, :], [], [("xin", b)])
            for half in range(2):
                pb = 1 + (2 * t + half) % 4
                for cc in range(4):
                    c = half * 4 + cc
                    self.tr(ps[pb][:, cc * 128:(cc + 1) * 128], xin[b][:, c * 128:(c + 1) * 128], ident_f[:],
                            [("xin", b), "ident_f"], [("ps", pb)])
                o = xT[:, half * 4:(half + 1) * 4, t * 128:(t + 1) * 128]
                i_ = ps[pb][:, :].rearrange("p (c t) -> p c t", c=4)
                wk = [("xT", c_, t // 4) for c_ in range(half * 4, half * 4 + 4)]
                if half == 0:
                    self.copy(o, i_, [("ps", pb)], wk)
                else:
                    self.act(o, i_, AF.Identity, [("ps", pb)], wk)

        wst = [self.sb(f"wada{i}", [128, KC, 512], F32, off=scr0 + 12288 + i * 16384) for i in range(2)]
        nch = 0
        for l in range(L):
            wv = w_ada[l].rearrange("(c p) n -> p c n", p=128)
            for g in range(12):
                b = nch % 2
                nch += 1
                self.dma(wst[b][:], wv[:, :, g * 512:(g + 1) * 512], [], [("wada", b)])
                for mm_ in range(4):
                    m = g * 4 + mm_
                    for kc in range(KC):
                        self.mm(ps[5][:, m:m + 1], wst[b][:, kc, mm_ * 128:(mm_ + 1) * 128], cact[:, kc:kc + 1],
                                kc == 0, kc == KC - 1, [("wada", b), "cact"], [("ps", 5)])
            self.tt(mod[:, l, :], ps[5][:, 0:48], vecT[:, l * 48:(l + 1) * 48], ALU.add,
                    [("ps", 5), "vecT"], [("mod", l)])
            for which, (gbase, scoff) in enumerate(((192, 8), (224, 32))):
                self.stt(gs[:, l, which, :], mod[:, l, scoff:scoff + 8], 1.0,
                         vecT[:, gbase + l * 8: gbase + l * 8 + 8], ALU.add, ALU.mult,
                         [("mod", l), "vecT"], [("gs", l)])

        if L > 2:
            wv = self.w_kv_ada.rearrange("(c p) n -> p c n", p=128)
            for g in range(4):
                b = nch % 2
                nch += 1
                self.dma(wst[b][:], wv[:, :, g * 512:(g + 1) * 512], [], [("wada", b)])
                for mm_ in range(4):
                    m = g * 4 + mm_
                    for kc in range(KC):
                        self.mm(ps[5][:, m:m + 1], wst[b][:, kc, mm_ * 128:(mm_ + 1) * 128], cact[:, kc:kc + 1],
                                kc == 0, kc == KC - 1, [("wada", b), "cact"], [("ps", 5)])
            self.tt(modkv[:, 0:16], ps[5][:, 0:16], vecT[:, 264:280], ALU.add, [("ps", 5), "vecT"], ["kvmod"])
            self.stt(modkv[:, 16:24], modkv[:, 8:16], 1.0, vecT[:, 256:264], ALU.add, ALU.mult,
                     ["kvmod", "vecT"], ["kvmod"])
        self.scr0 = scr0
        self._scratch_keys = set([("vrows", i) for i in range(3)] + [("xin", 0), ("xin", 1), ("wada", 0), ("wada", 1)])
        for l in range(L):
            if self.mixers:
                if l == 2:
                    self.nsa_kv()
                self.norm_mod(l, 0)
                if l < 2:
                    if not (self.debug or "").startswith("kv"):
                        self.mlstm(l)
                elif self.debug != "nomix" and not (self.debug or "").startswith("kv"):
                    self.nsa(l)
            if not (self.debug or "").startswith("kv"):
                self.norm_mod(l, 1)
                self.ffn(l, w_fi, w_fo)

        xo = [self.sb(f"xo{i}", [128, D], F32, off=scr0 + i * 4096) for i in range(2)]
        P.alias([("xo", 0), ("xo", 1)], self.scratch_keys())
        for t in range(NT):
            b = t % 2
            for half in range(2):
                pb = (2 * t + half) % 4
                for cc in range(4):
                    c = half * 4 + cc
                    self.tr(ps[pb][:, cc * 128:(cc + 1) * 128], xT[:, c, t * 128:(t + 1) * 128], ident_f[:],
                            [("xT", c, t // 4), "ident_f"], [("ps", pb)])
                o = xo[b][:, half * 512:(half + 1) * 512]
                if half == 0:
                    self.copy(o, ps[pb][:, :], [("ps", pb)], [("xo", b)])
                else:
                    self.act(o, ps[pb][:, :], AF.Identity, [("ps", pb)], [("xo", b)])
            self.dma(out_d[t * 128:(t + 1) * 128, :], xo[b][:], [("xo", b)], [("out", t)])
        P.finish([("out", t) for t in range(NT)])
        return nc

    def scratch_keys(self):
        return list(self._scratch_keys)


    def use_scratch(self, new_keys):
        self.P.alias(new_keys, list(self._scratch_keys))
        self._scratch_keys = set(new_keys)

    def norm_mod(self, l, which, gsc=None, shift=None):
        P, ps, xT, hT = self.P, self.ps, self.xT, self.hT
        if gsc is None:
            gsc = lambda c: self.gs[:, l, which, c:c + 1]
            sho = 0 if which == 0 else 24
            shift = lambda c: self.mod[:, l, sho + c: sho + c + 1]
            rk = [("gs", l), ("mod", l)]
        else:
            rk = ["kvmod"]
        scr0 = self.scr0
        sq = [self.sb(f"sq{i}", [128, 512], BF16, off=scr0 + i * 1024) for i in range(2)]
        rstd = [self.sb(f"rstd{i}", [128, 512], F32, off=scr0 + 2048 + i * 2048) for i in range(2)]
        tmp = [self.sb(f"ntmp{i}", [128, 512], F32, off=scr0 + 6144 + i * 2048) for i in range(2)]
        keys = [("sq", 0), ("sq", 1), ("rstd", 0), ("rstd", 1), ("ntmp", 0), ("ntmp", 1)]
        self.use_scratch(keys)
        k = 0
        for n in range(NB):
            tsl = slice(n * 512, (n + 1) * 512)
            pb = 6 + n % 2
            for c in range(KC):
                b = k % 2
                k += 1
                self.act(sq[b][:], xT[:, c, tsl], AF.Square, [("xT", c, n)], [("sq", b)])
                self.mm(ps[pb][:, :], self.ones_b[:], sq[b][:], c == 0, c == KC - 1,
                        [("sq", b), "ones_b"], [("ps", pb)])
            rb = n % 2
            self.act(rstd[rb][:], ps[pb][:, :], AF.Sqrt, [("ps", pb), "epsc"], [("rstd", rb)],
                     bias=self.epsc[:], scale=1.0 / D)
            P.op("dve", lambda e, rb=rb: e.reciprocal(out=rstd[rb][:], in_=rstd[rb][:]), [("rstd", rb)], [("rstd", rb)])
            for c in range(KC):
                b = c % 2
                self.tt(tmp[b][:], xT[:, c, tsl], rstd[rb][:], ALU.mult, [("xT", c, n), ("rstd", rb)], [("ntmp", b)])
                self.act(hT[:, c, tsl], tmp[b][:], AF.Identity, [("ntmp", b)] + rk, [("hT", c, n)],
                         bias=shift(c), scale=gsc(c))


    def bcast(self, t, nparts, pre, n, post):
        a = t[:]
        ps_ = a.ap[0][0]
        dims = [[ps_, nparts]]
        if pre > 1:
            dims.append([0, pre])
        dims.append([1, n])
        if post > 1:
            dims.append([0, post])
        return bass.AP(a.tensor, a.offset, dims)

    def mlstm(self, l):
        P, ps, xT, hT, nc = self.P, self.ps, self.xT, self.hT, self.nc
        o = self.scr0
        def take(name, shape, dt):
            nonlocal o
            t = self.sb(name, shape, dt, off=o)
            o += (int(np.prod(shape[1:])) * (4 if dt == F32 else 2) + 63) // 64 * 64
            assert o <= self.sb_end, (name, o)
            return t
        yT = take("yT", [128, KC, S], BF16)
        qT = take("qT", [128, S], BF16)
        kT = take("kT", [128, S], BF16)
        vtok = take("vtok", [64, 32, 257], BF16)
        soT = take("soT", [128, 2, S], BF16)
        wh = take("wh", [128, KC, 768], BF16)
        wst = [take(f"wst{i}", [128, KC, 128], F32) for i in range(2)]
        vo = self._off_of(vtok)
        GA = self.sb("GA", [4, S], F32, off=vo)
        GB = self.sb("GB", [4, S], F32, off=vo + 8192)
        GC = self.sb("GC", [4, S], F32, off=vo + 16384)
        small = take("msmall", [4, 8, 32], F32)
        Rm = take("Rm", [4, 128], F32)
        hmask = take("hmask", [4, 128], F32)
        ones4 = take("ones4", [4, 128], F32)
        scanmask = GC
        bif = take("bif", [4, 2], F32)
        wg_s = take("wg_s", [128, KC, 8], F32)
        wg = take("wg", [128, KC, 8], BF16)
        Etok = take("Etok", [64, 32, 4], F32)
        DNtok = take("DNtok", [64, 32, 4], F32)
        a_bc = take("a_bc", [128, 128], F32)
        causalT = take("causalT", [64, 64], F32)
        Cst = take("Cst", [128, 257], F32)
        Cab = [take(f"Cab{i}", [128, 257], BF16) for i in range(2)]
        STm = [take(f"STm{i}", [64, 64], BF16) for i in range(2)]
        ktil = [take(f"ktil{i}", [64, 128], BF16) for i in range(2)]
        hn = [take(f"hn{i}", [64, 256], BF16) for i in range(2)]
        sv = [take(f"sv{i}", [64, 8], F32) for i in range(2)]
        junk = take("junk", [64, 256], BF16)
        keys = ["yT", "qT", "kT", "wh", ("wst", 0), ("wst", 1), "GA", "GB", "GC", "msmall", "Rm",
                "hmask", "ones4", "bif", "wg_s", "wg", "Etok", "DNtok", "a_bc", "causalT", "Cst", ("Cab", 0),
                ("Cab", 1), ("STm", 0), ("STm", 1), ("ktil", 0), ("ktil", 1), ("hn", 0), ("hn", 1), ("sv", 0),
                ("sv", 1), "junk"] + [("yTk", c, n) for c in range(KC) for n in range(NB)]
        self.use_scratch(keys)
        w_in = self.w_a_in[l].rearrange("(c p) n -> p c n", p=128)
        hTk = lambda c, n: ("hT", c, n)
        self.dma(causalT[:], self.causalT_d, [], ["causalT"])
        self.dma(hmask[:], self.hmask_d, [], ["hmask"])
        self.dma(GC[:], self.scanmask_d, [], ["GC"])
        P.op("dve", lambda e: e.memset(ones4[:], 1.0), [], ["ones4"])
        self.dma(bif[:], self.b_a_if[l].rearrange("(two p) -> p two", p=4), [], ["bif"], allow_slow_non_contiguous=True)
        self.dma(wg_s[:], w_in[:, :, 3072:3080], [], ["wg_s"])
        self.copy(wg[:], wg_s[:], ["wg_s"], ["wg"])
        for n in range(NB):
            tsl = slice(n * 512, (n + 1) * 512)
            for part, dst, bcol in ((0, GA, 0), (1, GB, 1)):
                pb = (2 * n + part) % 4
                for c in range(KC):
                    self.mm(ps[pb][0:4, :], wg[:, c, part * 4:(part + 1) * 4], hT[:, c, tsl], c == 0, c == KC - 1,
                            ["wg", hTk(c, n)], [("ps", pb)])
                if part == 0:
                    self.act(dst[:, tsl], ps[pb][0:4, :], AF.Identity, [("ps", pb), "bif"], ["GA"], bias=bif[:, 0:1])
                else:
                    self.act(dst[:, tsl], ps[pb][0:4, :], AF.Identity, [("ps", pb), "bif"], ["GB"], bias=bif[:, 1:2])
        self.act(GB[:], GB[:], AF.Exp, ["GB"], ["GB"], scale=-1.0)
        self.act(GB[:], GB[:], AF.Ln, ["GB"], ["GB"], bias=1.0)
        umax, blast, mnext, mprev, mu, av = (small[:, i, :] for i in range(6))
        P.op("dve", lambda e: e.tensor_tensor_scan(out=GC[:], data0=GC[:], data1=GB[:], initial=0.0,
                                                    op0=ALU.mult, op1=ALU.add), ["GB", "GC"], ["GC"])
        self.tt(GA[:], GA[:], GC[:], ALU.add, ["GA", "GC"], ["GA"])
        P.op("dve", lambda e: e.tensor_reduce(out=umax, in_=GA[:].rearrange("p (c s) -> p c s", s=64), axis=AX.X,
                                               op=ALU.max), ["GA"], ["msmall"])
        self.ts(blast, GC[:].rearrange("p (c s) -> p c s", s=64)[:, :, 63], -1.0, ALU.mult, ["GC"], ["msmall"])
        P.op("dve", lambda e: e.tensor_tensor_scan(out=mnext, data0=umax, data1=blast, initial=0.0,
                                                    op0=ALU.max, op1=ALU.add), ["msmall"], ["msmall"])
        P.op("dve", lambda e: e.memset(mprev[:, 0:1], 0.0), ["msmall"], ["msmall"])
        self.copy(mprev[:, 1:32], mnext[:, 0:31], ["msmall"], ["msmall"])
        self.tt(mu, mprev, umax, ALU.max, ["msmall"], ["msmall"])
        self.tt(av, mprev, mu, ALU.subtract, ["msmall"], ["msmall"])
        self.act(av, av, AF.Exp, ["msmall"], ["msmall"])
        mu_b = self.bcast_small(small, 4, 32, 64)
        self.tt(GA[:].rearrange("p (c s) -> p c s", s=64), GA[:].rearrange("p (c s) -> p c s", s=64), mu_b,
                ALU.subtract, ["GA", "msmall"], ["GA"])
        self.act(GA[:], GA[:], AF.Exp, ["GA"], ["GA"])
        self.tt(GC[:].rearrange("p (c s) -> p c s", s=64), GC[:].rearrange("p (c s) -> p c s", s=64), mu_b,
                ALU.subtract, ["GC", "msmall"], ["GC"])
        self.act(GC[:], GC[:], AF.Exp, ["GC"], ["GC"])
        for src, dst, key, pb in ((GA, Etok, "Etok", 0), (GC, DNtok, "DNtok", 1)):
            for c in range(32):
                self.tr(ps[pb][0:64, c * 4:(c + 1) * 4], src[:, c * 64:(c + 1) * 64], self.ident_f[0:4, 0:4],
                        ["GA", "GC", "ident_f"], [("ps", pb)])
            self.copy(dst[:], ps[pb][0:64, 0:128].rearrange("p (c h) -> p c h", h=4), [("ps", pb)], [key])
        a_b = bass.AP(small[:].tensor, small[:, 5, :].offset, [[small[:].ap[0][0], 4], [0, 4], [1, 32]])
        self.tt(Rm[:].rearrange("p (h c) -> p h c", h=4), a_b, hmask[:].rearrange("p (h c) -> p h c", h=4), ALU.mult,
                ["msmall", "hmask"], ["Rm"])
        self.mm(ps[2][:, 0:128], ones4[:], Rm[:], True, True, ["ones4", "Rm"], [("ps", 2)])
        self.copy(a_bc[:], ps[2][:, 0:128], [("ps", 2)], ["a_bc"])

        vkeys = ["vtok"] + [("vtok", i) for i in range(4)] + [("soT", n) for n in range(NB)]
        P.alias(vkeys, ["GA", "GB", "GC"])
        self._scratch_keys |= set(vkeys)
        P.op("dve", lambda e: e.memset(vtok[:, :, 256:257], 1.0), [], ["vtok"])
        ps6b = ps[6][:].bitcast(BF16)
        ps7b = ps[7][:].bitcast(BF16)
        nst = 0
        for h in range(4):
            cols = [h * 128, 512 + h * 128, 1024 + h * 256, 1024 + h * 256 + 128, 2048 + h * 256, 2048 + h * 256 + 128]
            for i, c0 in enumerate(cols):
                b = nst % 2
                nst += 1
                self.dma(wst[b][:], w_in[:, :, c0:c0 + 128], [], [("wst", b)])
                self.copy(wh[:, :, i * 128:(i + 1) * 128], wst[b][:], [("wst", b)], ["wh"], eng="pool")
            for n in range(NB):
                tsl = slice(n * 512, (n + 1) * 512)
                for which, dst, key in ((0, qT, "qT"), (1, kT, "kT")):
                    pb = (2 * n + which) % 4
                    for c in range(KC):
                        self.mm(ps[pb][:, :], wh[:, c, which * 128:(which + 1) * 128], hT[:, c, tsl], c == 0, c == KC - 1,
                                ["wh", hTk(c, n)], [("ps", pb)])
                    if which == 0:
                        self.act(dst[:, tsl], ps[pb][:, :], AF.Copy, [("ps", pb)], [(key, n)], scale=128.0 ** -0.5)
                    else:
                        self.copy(dst[:, tsl], ps[pb][:, :], [("ps", pb)], [(key, n)])
                for half in range(2):
                    pb = 4 + half
                    for c in range(KC):
                        self.mm(ps[pb][:, :], wh[:, c, 512 + half * 128: 640 + half * 128], hT[:, c, tsl], c == 0,
                                c == KC - 1, ["wh", hTk(c, n)], [("ps", pb)])
                    self.act(soT[:, half, tsl], ps[pb][:, :], AF.Sigmoid, [("ps", pb)], [("soT", n)])
            for cp in range(16):
                pb = cp % 4
                for j in range(2):
                    ch = 2 * cp + j
                    for c in range(KC):
                        self.mm(ps[pb][0:64, j * 256:(j + 1) * 256], hT[:, c, ch * 64:(ch + 1) * 64], wh[:, c, 256:512],
                                c == 0, c == KC - 1, ["wh", hTk(c, ch // 8)], [("ps", pb)])
                self.copy(vtok[:, 2 * cp:2 * cp + 2, 0:256], ps[pb][0:64, :].rearrange("p (j v) -> p j v", j=2),
                          [("ps", pb)], [("vtok", cp // 4)])
            for ch in range(32):
                tsl = slice(ch * 64, (ch + 1) * 64)
                n = ch // 8
                b = ch % 2
                e_s = Etok[:, ch, h:h + 1]
                a_c = a_bc[:, h * 32 + ch: h * 32 + ch + 1]
                pst, pnum, pkv = b, 2 + b, 4 + b
                self.mm(ps[pst][0:64, 0:64], kT[:, tsl], qT[:, tsl], True, True, [("kT", n), ("qT", n)], [("ps", pst)])
                self.tr(ps6b[0:64, b * 128:(b + 1) * 128], kT[:, tsl], self.ident_b[:], [("kT", n), "ident_b"],
                        [("ps", 6)])
                self.stt(STm[b][:], ps[pst][0:64, 0:64], e_s, causalT[:], ALU.mult, ALU.mult,
                         [("ps", pst), "Etok", "causalT"], [("STm", b)])
                self.act(ktil[b][:], ps6b[0:64, b * 128:(b + 1) * 128], AF.Copy, [("ps", 6), "Etok"], [("ktil", b)],
                         scale=e_s)
                if ch > 0:
                    self.ts(Cab[b][:], Cst[:], a_c, ALU.mult, ["Cst", "a_bc"], [("Cab", b)])
                self.mm(ps[pnum][0:64, 0:257], STm[b][:], vtok[:, ch, :], True, ch == 0,
                        [("STm", b), ("vtok", ch // 8), "vtok"], [("ps", pnum)])
                if ch > 0:
                    self.mm(ps[pnum][0:64, 0:257], qT[:, tsl], Cab[b][:], False, True, [("qT", n), ("Cab", b)],
                            [("ps", pnum)])
                self.mm(ps[pkv][:, 0:257], ktil[b][:], vtok[:, ch, :], True, True,
                        [("ktil", b), ("vtok", ch // 8), "vtok"], [("ps", pkv)])
                if ch == 0:
                    self.copy(Cst[:], ps[pkv][:, 0:257], [("ps", pkv)], ["Cst"])
                else:
                    self.stt(Cst[:], Cst[:], a_c, ps[pkv][:, 0:257], ALU.mult, ALU.add,
                             ["Cst", "a_bc", ("ps", pkv)], ["Cst"])
                s_ = sv[b]
                self.ts(s_[:, 5:6], ps[pnum][0:64, 256:257], -1.0, ALU.mult, [("ps", pnum), "DNtok"], [("sv", b)],
                        s2=DNtok[:, ch, h:h + 1], op1=ALU.max)
                self.tt(s_[:, 0:1], s_[:, 5:6], ps[pnum][0:64, 256:257], ALU.max, [("ps", pnum), ("sv", b)], [("sv", b)])
                P.op("dve", lambda e, s_=s_: e.reciprocal(out=s_[:, 1:2], in_=s_[:, 0:1]), [("sv", b)], [("sv", b)])
                self.act(junk[:], ps[pnum][0:64, 0:256], AF.Square, [("ps", pnum), ("sv", b)], ["junk", ("sv", b)],
                         scale=s_[:, 1:2], accum_out=s_[:, 2:3])
                self.act(s_[:, 3:4], s_[:, 2:3], AF.Sqrt, [("sv", b), "epsc"], [("sv", b)], bias=self.epsc[0:64, :],
                         scale=1.0 / 256)
                P.op("dve", lambda e, s_=s_: e.reciprocal(out=s_[:, 3:4], in_=s_[:, 3:4]), [("sv", b)], [("sv", b)])
                self.tt(s_[:, 4:5], s_[:, 3:4], s_[:, 1:2], ALU.mult, [("sv", b)], [("sv", b)])
                self.act(hn[b][:], ps[pnum][0:64, 0:256], AF.Copy, [("ps", pnum), ("sv", b)], [("hn", b)],
                         scale=s_[:, 4:5])
                for half in range(2):
                    dc = 2 * h + half
                    self.tr(ps7b[:, (2 * b + half) * 64:(2 * b + half + 1) * 64], hn[b][:, half * 128:(half + 1) * 128],
                            self.ident_b[0:64, 0:64], [("hn", b), "ident_b"], [("ps", 7)])
                    self.stt(yT[:, dc, tsl], ps7b[:, (2 * b + half) * 64:(2 * b + half + 1) * 64],
                             self.vecT[:, 280 + l * 8 + dc: 281 + l * 8 + dc], soT[:, half, tsl], ALU.mult, ALU.mult,
                             [("ps", 7), "vecT", ("soT", n)], [("yTk", dc, n)])
        wao = self.sb("wao", [128, KC, D], BF16, off=vo)
        wos = [self.sb(f"waos{i}", [128, D], F32, off=vo + 16384 + i * 4096) for i in range(2)]
        P.alias(["wao", ("waos", 0), ("waos", 1)], vkeys)
        for c in range(KC):
            b = c % 2
            self.dma(wos[b][:], self.w_a_out[l, c * 128:(c + 1) * 128, :], [], [("waos", b)])
            self.copy(wao[:, c, :], wos[b][:], [("waos", b)], ["wao"], eng="pool")
        self._scratch_keys |= {"wao", ("waos", 0), ("waos", 1)}
        k = 0
        for dc in range(KC):
            for n in range(NB):
                tsl = slice(n * 512, (n + 1) * 512)
                pb = k % 2
                k += 1
                for c in range(KC):
                    self.mm(ps[pb][:, :], wao[:, c, dc * 128:(dc + 1) * 128], yT[:, c, tsl], c == 0, c == KC - 1,
                            ["wao", ("yTk", c, n)], [("ps", pb)])
                self.stt(xT[:, dc, tsl], ps[pb][:, :], self.mod[:, l, 16 + dc: 17 + dc], xT[:, dc, tsl], ALU.mult, ALU.add,
                         [("ps", pb), ("mod", l), ("xT", dc, n)], [("xT", dc, n)])

    def _off_of(self, t):
        return t.manual_sbuf_range[0]

    def bcast_small(self, small, idx, n, post):
        a = small[:, idx, :]
        return bass.AP(a.tensor, a.offset, [[a.ap[0][0], 4], [1, n], [0, post]])


    def ptake(self, name, shape, dt):
        nb = (int(np.prod(shape[1:])) * (4 if dt == F32 else 2) + 63) // 64 * 64
        self.scr_end -= nb
        return self.sb(name, shape, dt, off=self.scr_end)

    def rev_ap(self, t, nparts, mid, n):
        a = t[:]
        return bass.AP(a.tensor, a.offset + n - 1, [[a.ap[0][0], nparts], [n, mid], [-1, n]])

    def nsa_kv(self):
        P, ps, xT, hT, nc = self.P, self.ps, self.xT, self.hT, self.nc
        ksT = self.ksT = self.ptake("ksT", [128, 2, S], BF16)
        kwT = self.kwT = self.ptake("kwT", [128, 2, S], BF16)
        vs = self.vs = self.ptake("vs", [128, NT, 4, 65], BF16)
        vw = self.vw = self.ptake("vw", [128, NT, 4, 65], BF16)
        kcT = self.kcT = self.ptake("kcT", [128, 2, 128], BF16)
        vcx = self.vcx = self.ptake("vcx", [128, 4, 97], BF16)
        PatD = self.PatD = [self.ptake(f"PatD{i}", [128, 16, 128], BF16) for i in range(2)]
        PatD4 = self.PatD4 = self.ptake("PatD4", [128, 4, 128], BF16)
        Expand = self.Expand = self.ptake("Expand", [32, S], BF16)
        addc = self.addc = self.ptake("addc", [128, 16, 32], F32)
        zrow = self.zrow = self.ptake("zrow", [1, 512], BF16)
        blockones = self.blockones = self.ptake("blockones", [128, 128], BF16)
        gk = self.gk = self.ptake("gk", [128, 4], F32)
        gk0b = self.gk0b = self.ptake("gk0b", [128, 64], F32)
        tiny = self.tiny = self.ptake("tiny", [128, 1], F32)
        assert self.scr0 + 59392 <= self.scr_end, (self.scr0, self.scr_end)
        o = self.scr0
        def take(name, shape, dt):
            nonlocal o
            t = self.sb(name, shape, dt, off=o)
            o += (int(np.prod(shape[1:])) * (4 if dt == F32 else 2) + 63) // 64 * 64
            assert o <= self.scr_end, (name, o)
            return t
        stg = take("kvstg", [128, 2048], F32)
        wkb = take("wkb", [128, KC, 256], BF16)
        wpad = [take(f"wpad{i}", [128, KC, 128], BF16) for i in range(2)]
        c2T = take("c2T", [128, S], BF16)
        w1b = take("w1b", [128, 16, 256], BF16)
        posf = take("posf", [128, 16], F32)
        posb = take("posb", [128, 2], F32)
        w2b = take("w2b", [128, 2, 64], BF16)
        hid = take("hid", [128, 2, 128], BF16)
        gx = [take(f"gx{i}", [128, 128], F32) for i in range(3)]
        sqk = take("sqk", [128, 512], BF16)
        kraw = take("kraw", [128, 512], F32)
        rk_ = take("rk_", [128, 512], F32)
        ctok = take("ctok", [128, 4, 64], F32)
        ckn = take("ckn", [128, 4, 64], BF16)
        csm = take("csm", [128, 8], F32)
        tab33 = take("tab33", [33, 16], F32)
        ohs = take("ohs", [33, 512], F32)
        Fsb = self.sb("Fsb", [16, NF], F32, off=self._off_of(wpad[0]))
        keys = ["kvstg", "wkb", ("wpad", 0), ("wpad", 1), "c2T", "w1b", "posf", "posb", "w2b", "hid", "gx", ("sqk", "kv"), ("kraw", "kv"),
                ("rk_", "kv"), "ctok", "ckn", "csm", "tab33", "ohs", "Fsb"]
        self.use_scratch(keys)
        stg3 = stg[:].rearrange("p (c n) -> p c n", c=KC)
        self.dma(stg[:, 0:128], self.blockones_d, [], ["kvstg"])
        self.copy(blockones[:], stg[:, 0:128], ["kvstg"], ["blockones"])
        self.dma(stg[0:32, :], self.expand_d, [], ["kvstg"])
        self.copy(Expand[:], stg[0:32, :], ["kvstg"], ["Expand"])
        self.dma(addc[:].rearrange("p a b -> p (a b)"), self.addc_d, [], ["addc"])
        P.op("dve", lambda e: e.memset(zrow[:], 0.0), [], ["zrow"])
        P.op("dve", lambda e: e.memset(tiny[:], 1e-30), [], ["tiny"])
        self.dma(stg[:, 0:128], self.d4mask_d, ["kvstg"], ["kvstg"])
        for hh in range(4):
            self.copy(PatD4[:, hh, :], stg[:, 0:128], ["kvstg"], ["PatD4"])
        for j in (1, 2):
            for half in range(2):
                self.dma(gk[half * 64:(half + 1) * 64, j:j + 1], self.g_knorm[j:j + 1, :].rearrange("o d -> d o"), [], ["gk"],
                         allow_slow_non_contiguous=True)
        g0 = self.g_knorm[0:1, :]
        self.dma(gk0b[:], bass.AP(g0.tensor, g0.offset, [[0, 128], [1, 64]]), [], ["gk0b"])
        self.dma(stg[0:127, 0:32], self.overlap_d, ["kvstg"], ["kvstg"])
        for g in range(4):
            self.copy(vcx[0:127, g, 65:97], stg[0:127, 0:32], ["kvstg"], ["vcx"])
        P.op("dve", lambda e: e.memset(vcx[:, :, 64:65], 1.0), ["vcx"], ["vcx"])
        self.dma(tab33[0:32, :], self.rel_table, [], ["tab33"])
        P.op("dve", lambda e: e.memset(tab33[32:33, :], 1.0), [], ["tab33"])
        for ch in range(NF // 512):
            self.dma(ohs[:], self.onehot_d[:, ch * 512:(ch + 1) * 512], [], ["ohs"])
            self.mm(ps[0][0:16, :], tab33[:], ohs[:], True, True, ["tab33", "ohs"], [("ps", 0)])
            self.copy(Fsb[:, ch * 512:(ch + 1) * 512], ps[0][0:16, :], [("ps", 0)], ["Fsb"])
        self.dma(self.fvd, Fsb[:], ["Fsb"], ["fvd"])
        P.alias([("wpad", 0), ("wpad", 1), "c2T", "w1b"], ["Fsb"])
        for dl in range(2):
            base = 1920 - 128 * dl
            for h4 in range(4):
                src = bass.AP(self.fvd.tensor, base + h4 * 4 * NF, [[1, 128], [NF, 4], [1, 128]])
                self.dma(stg[:, h4 * 512:(h4 + 1) * 512].rearrange("p (h t) -> p h t", h=4), src, ["fvd", "kvstg"], ["kvstg"])
            self.copy(PatD[dl][:], self.rev_ap(stg, 128, 16, 128), ["kvstg"], [("PatD", dl)])
        stage = int(self.debug[2:3]) if (self.debug or "").startswith("kv") else 9
        if stage < 2:
            return
        self.norm_mod(None, None, gsc=lambda c: self.modkv[:, 16 + c:17 + c], shift=lambda c: self.modkv[:, c:c + 1])
        self.use_scratch(keys)
        if self.debug == "kv2a":
            return
        wkv = self.w_kv.rearrange("(c p) n -> p c n", p=128)
        hTk = lambda c, n: ("hT", c, n)
        for j, dst, key, gcol in ((2, ksT, "ksT", 1), (4, kwT, "kwT", 2)):
            for gp in range(2):
                self.dma(stg3[:, :, 0:128], wkv[:, :, j * 256 + gp * 128: j * 256 + (gp + 1) * 128], ["kvstg"], ["kvstg"])
                self.copy(wkb[:, :, 0:128], stg3[:, :, 0:128], ["kvstg"], ["wkb"], eng="pool")
                for n in range(NB):
                    tsl = slice(n * 512, (n + 1) * 512)
                    pb = n % 2
                    for c in range(KC):
                        self.mm(ps[pb][:, :], wkb[:, c, 0:128], hT[:, c, tsl], c == 0, c == KC - 1, ["wkb", hTk(c, n)],
                                [("ps", pb)])
                    self.qknorm(pb, dst[:, gp, tsl], gk[:, gcol:gcol + 1], sqk, kraw, rk_, 2 + pb, [(key, gp, n)], ["gk"], sfx="kv")
        if stage < 3:
            return
        for j, dst, key in ((3, vs, "vs"), (5, vw, "vw")):
            self.dma(stg3[:, :, 0:256], wkv[:, :, j * 256:(j + 1) * 256], ["kvstg"], ["kvstg"])
            self.copy(wkb[:], stg3[:, :, 0:256], ["kvstg"], ["wkb"], eng="pool")
            P.op("dve", lambda e, dst=dst: e.memset(dst[:, :, :, 64:65], 1.0), [], [key])
            for tp in range(NT // 2):
                pb = tp % 2
                for jj in range(2):
                    t = 2 * tp + jj
                    for c in range(KC):
                        self.mm(ps[pb][:, jj * 256:(jj + 1) * 256], hT[:, c, t * 128:(t + 1) * 128], wkb[:, c, :], c == 0,
                                c == KC - 1, ["wkb", hTk(c, t // 4)], [("ps", pb)])
                self.copy(dst[:, 2 * tp:2 * tp + 2, :, 0:64], ps[pb][:, :].rearrange("p (j g d) -> p j g d", j=2, g=4),
                          [("ps", pb)], [(key, tp)])
        if stage < 4:
            return
        for which in range(2):
            w1v = self.w_cmp1[which].rearrange("(a p) f -> p a f", p=128)
            for q4 in range(2):
                self.dma(stg[:].rearrange("p (a f) -> p a f", a=8), w1v[:, q4 * 8:(q4 + 1) * 8, :], ["kvstg"], ["kvstg"])
                self.copy(w1b[:, q4 * 8:(q4 + 1) * 8, :], stg[:].rearrange("p (a f) -> p a f", a=8), ["kvstg"], ["w1b"],
                          eng="pool")
            self.dma(stg[0:16, 0:128], self.pos_cmp[which].rearrange("(a two) d -> a (two d)", two=2), ["kvstg"], ["kvstg"])
            self.tr(ps[4][:, 16:32], stg[0:16, 0:128], self.ident_f[0:16, 0:16], ["kvstg", "ident_f"], [("ps", 4)])
            self.copy(hid[:, 0, 0:16], ps[4][:, 16:32], [("ps", 4)], ["hid"])
            for ft in range(2):
                for a in range(16):
                    self.mm(ps[4][:, ft:ft + 1], w1b[:, a, ft * 128:(ft + 1) * 128], hid[:, 0, a:a + 1], a == 0, a == 15,
                            ["w1b", "hid"], [("ps", 4)])
            self.copy(posb[:], ps[4][:, 0:2], [("ps", 4)], ["posb"])
            self.dma(stg[:, 0:128].rearrange("p (a d) -> p a d", a=2), self.w_cmp2[which].rearrange("(a p) d -> p a d", p=128),
                     ["kvstg"], ["kvstg"])
            self.copy(w2b[:], stg[:, 0:128].rearrange("p (a d) -> p a d", a=2), ["kvstg"], ["w2b"])
            for g in range(4):
                col = which * 256 + g * 64
                self.dma(stg3[:, :, 0:64], wkv[:, :, col:col + 64], ["kvstg"], ["kvstg"])
                P.op("pool", lambda e: e.memset(wpad[0][:], 0.0), [], [("wpad", 0)])
                P.op("pool", lambda e: e.memset(wpad[1][:], 0.0), [], [("wpad", 1)])
                self.copy(wpad[0][:, :, 0:64], stg3[:, :, 0:64], ["kvstg"], [("wpad", 0)], eng="pool")
                self.copy(wpad[1][:, :, 64:128], stg3[:, :, 0:64], ["kvstg"], [("wpad", 1)], eng="pool")
                for n in range(NB):
                    pb = n % 2
                    nn = 512 if n < NB - 1 else 511
                    for c in range(KC):
                        self.mm(ps[pb][:, :], wpad[0][:, c, :], hT[:, c, n * 512:(n + 1) * 512], c == 0, False,
                                [("wpad", 0), hTk(c, n)], [("ps", pb)])
                    for c in range(KC):
                        self.mm(ps[pb][:, 0:nn], wpad[1][:, c, :], hT[:, c, n * 512 + 1: n * 512 + 1 + nn], False,
                                c == KC - 1, [("wpad", 1), hTk(c, n), hTk(c, min(n + 1, NB - 1))], [("ps", pb)])
                    self.copy(c2T[:, n * 512:(n + 1) * 512], ps[pb][:, :], [("ps", pb)], ["c2T"])
                c2v = c2T[:].rearrange("p (n s) -> p s n", s=16)
                for ft in range(2):
                    pb = 2 + ft
                    for a in range(16):
                        rhs = bass.AP(c2T[:].tensor, c2T[:].offset + 2 * a, [[c2T[:].ap[0][0], 128], [16, 127]])
                        self.mm(ps[pb][:, 0:127], w1b[:, a, ft * 128:(ft + 1) * 128], rhs, a == 0, a == 15,
                                ["w1b", "c2T"], [("ps", pb)])
                    x_, x2, x3 = gx[0][:, 0:127], gx[1][:, 0:127], gx[2][:, 0:127]
                    self.act(x_, ps[pb][:, 0:127], AF.Identity, [("ps", pb), "posb"], ["gx"], bias=posb[:, ft:ft + 1])
                    self.tt(x2, x_, x_, ALU.mult, ["gx"], ["gx"])
                    self.ts(x2, x2, 0.044715, ALU.mult, ["gx"], ["gx"], s2=1.0, op1=ALU.add)
                    self.tt(x2, x2, x_, ALU.mult, ["gx"], ["gx"])
                    self.act(x3, x2, AF.Sigmoid, ["gx"], ["gx"], scale=2.0 * math.sqrt(2.0 / math.pi))
                    self.tt(hid[:, ft, 0:127], x_, x3, ALU.mult, ["gx"], ["hid"])
                for ft in range(2):
                    self.mm(ps[5][0:127, g * 64:(g + 1) * 64], hid[:, ft, 0:127], w2b[:, ft, :], ft == 0, ft == 1,
                            ["hid", "w2b"], [("ps", 5)])
                if which == 1:
                    self.copy(vcx[0:127, g, 0:64], ps[5][0:127, g * 64:(g + 1) * 64], [("ps", 5)], ["vcx"])
            if which == 0:
                self.copy(ctok[0:127, :, :], ps[5][0:127, 0:256].rearrange("p (g d) -> p g d", g=4), [("ps", 5)], ["ctok"])
                for g in range(4):
                    self.act(gx[0][0:127, 0:64], ctok[0:127, g, :], AF.Square, ["ctok"], ["gx", "csm"],
                             accum_out=csm[0:127, g:g + 1])
                self.act(csm[0:127, 4:8], csm[0:127, 0:4], AF.Sqrt, ["csm", "epsc"], ["csm"], bias=self.epsc[0:127, :],
                         scale=1.0 / 64)
                P.op("dve", lambda e: e.reciprocal(out=csm[0:127, 4:8], in_=csm[0:127, 4:8]), ["csm"], ["csm"])
                for g in range(4):
                    self.stt(ckn[0:127, g, :], ctok[0:127, g, :], csm[0:127, 4 + g:5 + g], gk0b[0:127, :], ALU.mult, ALU.mult,
                             ["ctok", "csm", "gk0b"], ["ckn"])
                ps6b = ps[6][:].bitcast(BF16)
                for gp in range(2):
                    self.tr(ps6b[:, gp * 128: gp * 128 + 127], ckn[0:127, 2 * gp:2 * gp + 2, :].rearrange("p g d -> p (g d)"),
                            self.ident_b[0:127, 0:127], ["ckn", "ident_b"], [("ps", 6)])
                    self.copy(kcT[:, gp, 0:127], ps6b[:, gp * 128: gp * 128 + 127], [("ps", 6)], ["kcT"])

    def qknorm(self, pin, dst, gvec, sqk, kraw, rk_, pbank, wkeys, rkeys, sfx=0):
        P, ps = self.P, self.ps
        psb = ps[pin]
        if self.debug == "kv2b":
            self.copy(dst, psb[:, :], [("ps", pin)], wkeys)
            return
        self.copy(kraw[:], psb[:, :], [("ps", pin)], [("kraw", sfx)])
        self.act(sqk[:], psb[:, :], AF.Square, [("ps", pin)], [("sqk", sfx)])
        self.mm(ps[pbank][:, :], self.blockones[:], sqk[:], True, True, ["blockones", ("sqk", sfx)], [("ps", pbank)])
        self.act(rk_[:], ps[pbank][:, :], AF.Sqrt, [("ps", pbank), "epsc"], [("rk_", sfx)], bias=self.epsc[:], scale=1.0 / 64)
        P.op("dve", lambda e: e.reciprocal(out=rk_[:], in_=rk_[:]), [("rk_", sfx)], [("rk_", sfx)])
        self.tt(kraw[:], kraw[:], rk_[:], ALU.mult, [("kraw", sfx), ("rk_", sfx)], [("kraw", sfx)])
        if self.debug == "kv2v1":
            self.ts(dst, kraw[:], gvec, ALU.mult, [("kraw", sfx)] + rkeys, wkeys)
        else:
            self.act(dst, kraw[:], AF.Copy, [("kraw", sfx)] + rkeys, wkeys, scale=gvec)


    def nsa(self, l):
        P, ps, xT, hT, nc = self.P, self.ps, self.xT, self.hT, self.nc
        j = l - 2
        o = self.scr0
        def take(name, shape, dt):
            nonlocal o
            t = self.sb(name, shape, dt, off=o)
            o += (int(np.prod(shape[1:])) * (4 if dt == F32 else 2) + 63) // 64 * 64
            assert o <= self.scr_end, (name, o)
            return t
        qT = take("nqT", [128, 8, S], BF16)
        gates = take("gates", [128, NT, 48], F32)
        bgb = take("bgb", [128, 48], F32)
        gq2 = take("gq2", [128, 2], F32)
        o_attn = o
        wst = take("nwst", [128, KC, 128], F32)
        wqb = [take(f"wqb{i}", [128, KC, 128], BF16) for i in range(2)]
        sqk = take("nsqk", [128, 512], BF16)
        kraw = take("nkraw", [128, 512], F32)
        rk_ = take("nrk", [128, 512], F32)
        wgs = take("wgs", [128, KC, 48], F32)
        wgb = take("wgb", [128, KC, 48], BF16)
        pkeys = ["nqT", "gates", "bgb", "gq2", "nwst", ("wqb", 0), ("wqb", 1), ("sqk", "q"), ("kraw", "q"), ("rk_", "q"), "wgs", "wgb"]
        self.use_scratch(pkeys)
        wq = self.w_b_q[j].rearrange("(c p) n -> p c n", p=128)
        hTk = lambda c, n: ("hT", c, n)
        for half in range(2):
            self.dma(gq2[half * 64:(half + 1) * 64, 0:1], self.g_qnorm[j:j + 1, :].rearrange("o d -> d o"), [], ["gq2"],
                     allow_slow_non_contiguous=True)
        self.ts(gq2[:, 1:2], gq2[:, 0:1], 0.125, ALU.mult, ["gq2"], ["gq2"])
        bg = self.b_b_gate[j:j + 1, :]
        self.dma(bgb[:], bass.AP(bg.tensor, bg.offset, [[0, 128], [1, 48]]), [], ["bgb"])
        for sl in range(8):
            gp, hh = sl // 4, sl % 4
            b = sl % 2
            for par in range(2):
                head = (2 * gp + par) * 4 + hh
                self.dma(wst[:, :, par * 64:(par + 1) * 64], wq[:, :, head * 64:(head + 1) * 64], [], ["nwst"])
            self.copy(wqb[b][:], wst[:], ["nwst"], [("wqb", b)], eng="pool")
            for n in range(NB):
                tsl = slice(n * 512, (n + 1) * 512)
                pb = n % 2
                for c in range(KC):
                    self.mm(ps[pb][:, :], wqb[b][:, c, :], hT[:, c, tsl], c == 0, c == KC - 1, [("wqb", b), hTk(c, n)],
                            [("ps", pb)])
                self.qknorm(pb, qT[:, sl, tsl], gq2[:, 1:2], sqk, kraw, rk_, 2 + pb, [("nqT", sl, n)], ["gq2"], sfx="q")
        self.dma(wgs[:], wq[:, :, 1024:1072], [], ["wgs"])
        self.copy(wgb[:], wgs[:], ["wgs"], ["wgb"])
        for t in range(NT):
            pb = 4 + t % 2
            for c in range(KC):
                self.mm(ps[pb][:, 0:48], hT[:, c, t * 128:(t + 1) * 128], wgb[:, c, :], c == 0, c == KC - 1,
                        ["wgb", hTk(c, t // 4)], [("ps", pb)])
            self.tt(gates[:, t, :], ps[pb][:, 0:48], bgb[:], ALU.add, [("ps", pb), "bgb"], [("gates", t)])
        self.act(gates[:], gates[:], AF.Sigmoid, [("gates", t) for t in range(NT)], [("gates", t) for t in range(NT)])
        o = o_attn
        cb = take("cb", [128, 4, 128], F32)
        ssum = take("ssum", [128, 512], F32)
        EcT = take("EcT", [128, 512], BF16)
        Et = [take(f"Et{i}", [128, 512], BF16) for i in range(3)]
        selT = [take(f"selT{i}", [32, 512], BF16) for i in range(2)]
        otok = [take(f"otok{i}", [128, D], BF16) for i in range(2)]
        acc = [take(f"acc{i}", [128, 4, 64], F32) for i in range(2)]
        zall = take("zall", [128, 16], F32)
        cf = take("cf", [128, 12], F32)
        impt = take("impt", [128, 4, 32], F32)
        score = take("score", [128, 64], F32)
        top8 = take("top8", [128, 8], F32)
        akeys = ["cb", "ssum", "EcT", ("Et", 0), ("Et", 1), ("Et", 2), ("selT", 0), ("selT", 1), ("otok", 0), ("otok", 1),
                 ("acc", 0), ("acc", 1), "zall", "cf", "impt", "score", "top8"]
        P.alias(akeys, ["nwst", ("wqb", 0), ("wqb", 1), ("sqk", "q"), ("kraw", "q"), ("rk_", "q"), "wgs", "wgb"])
        self._scratch_keys |= set(akeys)
        ksT, kwT, vs, vw, kcT, vcx = self.ksT, self.kwT, self.vs, self.vw, self.kcT, self.vcx
        ps6b = ps[6][:].bitcast(BF16)
        ne = 0
        nsc = 0
        for qi in range(NT):
            qsl = slice(qi * 128, (qi + 1) * 128)
            ob = qi % 2
            for g in range(4):
                gp, par = g // 2, g % 2
                hs = slice(par * 64, (par + 1) * 64)
                qv = qT[hs, gp * 4:(gp + 1) * 4, qsl]
                qk = [("nqT", gp * 4 + hh, qi // 4) for hh in range(4)]
                base = 1951 - 128 * qi
                src = bass.AP(self.fvd.tensor, (g * 4) * NF + base, [[16, 127], [NF, 4], [1, 128]])
                self.dma(cb[0:127, :, :], src, ["fvd"], ["cb"])
                pc = nsc % 3
                nsc += 1
                self.mm(ps[pc][0:127, :], kcT[hs, gp, 0:127], qv, True, True, ["kcT"] + qk, [("ps", pc)])
                self.tt(ssum[0:127, :].rearrange("p (h t) -> p h t", h=4), ps[pc][0:127, :].rearrange("p (h t) -> p h t", h=4),
                        self.rev_ap(cb, 127, 4, 128), ALU.add, [("ps", pc), "cb"], ["ssum"])
                self.act(EcT[0:127, :], ssum[0:127, :], AF.Exp, ["ssum"], ["EcT"])
                for hh in range(4):
                    self.mm(ps[3][:, hh * 97:(hh + 1) * 97], EcT[0:127, hh * 128:(hh + 1) * 128], vcx[0:127, g, :], True, True,
                            ["EcT", "vcx"], [("ps", 3)])
                Oc = ps[3][:, 0:388].rearrange("p (h c) -> p h c", h=4)
                self.ts(zall[:, 0:4], Oc[:, :, 64], self.tiny[:], ALU.max, [("ps", 3), "tiny"], ["zall"])
                P.op("dve", lambda e: e.reciprocal(out=zall[:, 12:16], in_=zall[:, 0:4]), ["zall"], ["zall"])
                rzb = bass.AP(zall[:].tensor, zall[:, 12:16].offset, [[zall[:].ap[0][0], 128], [1, 4], [0, 32]])
                self.tt(impt[:], Oc[:, :, 65:97], rzb, ALU.mult, [("ps", 3), "zall"], ["impt"])
                P.op("dve", lambda e: e.tensor_reduce(out=score[:, 0:32], in_=impt[:].rearrange("p h j -> p j h"), axis=AX.X,
                                                       op=ALU.add), ["impt"], ["score"])
                self.tt(score[:, 0:32], score[:, 0:32], self.addc[:, qi, :], ALU.add, ["score", "addc"], ["score"])
                P.op("dve", lambda e: e.max(out=top8[:], in_=score[:, 0:32]), ["score"], ["top8"])
                self.ts(score[:, 32:64], score[:, 0:32], top8[:, 7:8], ALU.is_ge, ["score", "top8"], ["score"])
                self.ts(score[:, 32:64], score[:, 32:64], -1.0, ALU.add, ["score"], ["score"], s2=30000.0, op1=ALU.mult)
                self.tr(ps[7][0:32, 0:128], score[:, 32:64], self.ident_f[:], ["score", "ident_f"], [("ps", 7)])
                sb_ = g % 2
                p7 = ps[7][0:32, 0:128]
                self.copy(selT[sb_][:].rearrange("p (h t) -> p h t", h=4),
                          bass.AP(p7.tensor, p7.offset, [[p7.ap[0][0], 32], [0, 4], [1, 128]]), [("ps", 7)], [("selT", sb_)])
                self.mm(ps[4][:, 0:260], self.zrow[0:1, 0:128], self.zrow[0:1, 0:260], True, False, ["zrow"], [("ps", 4)])
                for kt in range(qi + 1):
                    dl = qi - kt
                    pc = nsc % 3
                    nsc += 1
                    near = dl <= 1
                    self.mm(ps[pc][:, :], ksT[hs, gp, kt * 128:(kt + 1) * 128], qv, True, False,
                            [("ksT", gp, kt // 4)] + qk, [("ps", pc)])
                    self.mm(ps[pc][:, :], self.Expand[:, kt * 128:(kt + 1) * 128], selT[sb_][:], False, not near,
                            ["Expand", ("selT", sb_)], [("ps", pc)])
                    if near:
                        self.mm(ps[pc][:, :], self.ident_b[:], self.PatD[dl][:, g * 4:(g + 1) * 4, :], False, True,
                                ["ident_b", ("PatD", dl)], [("ps", pc)])
                    eb = ne % 3
                    ne += 1
                    self.act(Et[eb][:], ps[pc][:, :], AF.Exp, [("ps", pc)], [("Et", eb)])
                    for hh in range(4):
                        self.mm(ps[4][:, hh * 65:(hh + 1) * 65], Et[eb][:, hh * 128:(hh + 1) * 128], vs[:, kt, g, :], False,
                                kt == qi and hh == 3, [("Et", eb), "vs", ("vs", kt // 2)], [("ps", 4)])
                self.mm(ps[5][:, 0:260], self.zrow[0:1, 0:128], self.zrow[0:1, 0:260], True, False, ["zrow"], [("ps", 5)])
                k0 = max(0, qi - 4)
                for kt in range(k0, qi + 1):
                    dl = qi - kt
                    pc = nsc % 3
                    nsc += 1
                    pat = dl in (0, 1, 4)
                    self.mm(ps[pc][:, :], kwT[hs, gp, kt * 128:(kt + 1) * 128], qv, True, not pat,
                            [("kwT", gp, kt // 4)] + qk, [("ps", pc)])
                    if dl <= 1:
                        self.mm(ps[pc][:, :], self.ident_b[:], self.PatD[dl][:, g * 4:(g + 1) * 4, :], False, True,
                                ["ident_b", ("PatD", dl)], [("ps", pc)])
                    elif dl == 4:
                        self.mm(ps[pc][:, :], self.ident_b[:], self.PatD4[:], False, True, ["ident_b", "PatD4"], [("ps", pc)])
                    eb = ne % 3
                    ne += 1
                    self.act(Et[eb][:], ps[pc][:, :], AF.Exp, [("ps", pc)], [("Et", eb)])
                    for hh in range(4):
                        self.mm(ps[5][:, hh * 65:(hh + 1) * 65], Et[eb][:, hh * 128:(hh + 1) * 128], vw[:, kt, g, :], False,
                                kt == qi and hh == 3, [("Et", eb), "vw", ("vw", kt // 2)], [("ps", 5)])
                Os = ps[4][:, 0:260].rearrange("p (h c) -> p h c", h=4)
                Ow = ps[5][:, 0:260].rearrange("p (h c) -> p h c", h=4)
                self.ts(zall[:, 4:8], Os[:, :, 64], self.tiny[:], ALU.max, [("ps", 4), "tiny"], ["zall"])
                self.ts(zall[:, 8:12], Ow[:, :, 64], self.tiny[:], ALU.max, [("ps", 5), "tiny"], ["zall"])
                P.op("dve", lambda e: e.reciprocal(out=zall[:, 0:12], in_=zall[:, 0:12]), ["zall"], ["zall"])
                gv = gates[:, qi, g * 12:(g + 1) * 12].rearrange("p (h b) -> p b h", b=3)
                self.tt(cf[:].rearrange("p (b h) -> p b h", b=3), zall[:, 0:12].rearrange("p (b h) -> p b h", b=3), gv, ALU.mult,
                        ["zall", ("gates", qi)], ["cf"])
                ab = g % 2
                for hh in range(4):
                    self.ts(acc[ab][:, hh, :], Oc[:, hh, 0:64], cf[:, hh:hh + 1], ALU.mult, [("ps", 3), "cf"], [("acc", ab)])
                    self.stt(acc[ab][:, hh, :], Os[:, hh, 0:64], cf[:, 4 + hh:5 + hh], acc[ab][:, hh, :], ALU.mult, ALU.add,
                             [("ps", 4), "cf", ("acc", ab)], [("acc", ab)])
                    self.stt(otok[ob][:, (g * 4 + hh) * 64:(g * 4 + hh + 1) * 64], Ow[:, hh, 0:64], cf[:, 8 + hh:9 + hh],
                             acc[ab][:, hh, :], ALU.mult, ALU.add, [("ps", 5), "cf", ("acc", ab)], [("otok", ob)])
            for c in range(KC):
                self.tr(ps6b[:, c * 128:(c + 1) * 128], otok[ob][:, c * 128:(c + 1) * 128], self.ident_b[:],
                        [("otok", ob), "ident_b"], [("ps", 6)])
            self.copy(hT[:, :, qsl], ps6b[:, :].rearrange("p (c t) -> p c t", c=KC), [("ps", 6)],
                      [("hT", c, qi // 4) for c in range(KC)])
        wao = self.sb("nwao", [128, KC, D], BF16, off=self.scr0)
        wos = [self.sb(f"nwaos{i}", [128, D], F32, off=self.scr0 + 16384 + i * 4096) for i in range(2)]
        P.alias(["nwao", ("nwaos", 0), ("nwaos", 1)], ["nqT"] + [("nqT", sl, n) for sl in range(8) for n in range(NB)])
        self._scratch_keys |= {"nwao", ("nwaos", 0), ("nwaos", 1)} | set(("nqT", sl, n) for sl in range(8) for n in range(NB))
        for c in range(KC):
            b = c % 2
            self.dma(wos[b][:], self.w_b_out[j, c * 128:(c + 1) * 128, :], [], [("nwaos", b)])
            self.copy(wao[:, c, :], wos[b][:], [("nwaos", b)], ["nwao"], eng="pool")
        k = 0
        for dc in range(KC):
            for n in range(NB):
                tsl = slice(n * 512, (n + 1) * 512)
                pb = k % 2
                k += 1
                for c in range(KC):
                    self.mm(ps[pb][:, :], wao[:, c, dc * 128:(dc + 1) * 128], hT[:, c, tsl], c == 0, c == KC - 1,
                            ["nwao", ("hT", c, n)], [("ps", pb)])
                self.stt(xT[:, dc, tsl], ps[pb][:, :], self.mod[:, l, 16 + dc: 17 + dc], xT[:, dc, tsl], ALU.mult, ALU.add,
                         [("ps", pb), ("mod", l), ("xT", dc, n)], [("xT", dc, n)])

    def ffn(self, l, w_fi, w_fo):
        P, ps, xT, hT = self.P, self.ps, self.xT, self.hT
        scr0 = self.scr0
        groups = [(0, 6), (6, 6), (12, 5), (17, 5)]
        actT = self.sb("actT", [128, 6, S], BF16, off=scr0)
        wob = self.sb("wob", [128, 6, D], BF16, off=scr0 + 24576)
        wis = self.sb("wis", [128, KC, 256], F32, off=scr0 + 36864)
        wib = [self.sb(f"wib{i}", [128, KC, 256], BF16, off=scr0 + 45056 + i * 4096) for i in range(2)]
        wos = self.sb("wos", [128, D], F32, off=scr0 + 53248)
        sg = [self.sb(f"sg{i}", [128, 512], BF16, off=scr0 + 57344 + i * 1024) for i in range(2)]
        assert scr0 + 59392 <= self.scr_end
        keys = [("actT", j) for j in range(6)] + [("wob", j) for j in range(6)] + ["wis", ("wib", 0), ("wib", 1), "wos",
                                                                                 ("sg", 0), ("sg", 1)]
        self.use_scratch(keys)
        wiv = w_fi[l].rearrange("(c p) n -> p c n", p=128)
        ga = lambda c: self.mod[:, l, 40 + c: 41 + c]
        it = 0
        for (j0, nj) in groups:
            for jj in range(nj):
                j = j0 + jj
                wb = it % 2
                it += 1
                self.dma(wis[:, :, 0:128], wiv[:, :, j * 128:(j + 1) * 128], [], ["wis"])
                self.dma(wis[:, :, 128:256], wiv[:, :, DFF + j * 128: DFF + (j + 1) * 128], [], ["wis"])
                self.copy(wib[wb][:], wis[:], ["wis"], [("wib", wb)], eng="pool")
                for n in range(NB):
                    tsl = slice(n * 512, (n + 1) * 512)
                    pg, pu = (n % 2) * 2, (n % 2) * 2 + 1
                    for part, pb in ((0, pg), (1, pu)):
                        for c in range(KC):
                            self.mm(ps[pb][:, :], wib[wb][:, c, part * 128:(part + 1) * 128], hT[:, c, tsl],
                                    c == 0, c == KC - 1, [("wib", wb), ("hT", c, n)], [("ps", pb)])
                    sb_ = n % 2
                    self.act(sg[sb_][:], ps[pg][:, :], AF.Silu, [("ps", pg)], [("sg", sb_)])
                    self.tt(actT[:, jj, tsl], sg[sb_][:], ps[pu][:, :], ALU.mult,
                            [("sg", sb_), ("ps", pu)], [("actT", jj)])
                self.dma(wos[:], w_fo[l, j * 128:(j + 1) * 128, :], [], ["wos"])
                self.copy(wob[:, jj, :], wos[:], ["wos"], [("wob", jj)], eng="pool")
            k = 0
            for dc in range(KC):
                for n in range(NB):
                    tsl = slice(n * 512, (n + 1) * 512)
                    pb = 4 + k % 2
                    k += 1
                    for jj in range(nj):
                        self.mm(ps[pb][:, :], wob[:, jj, dc * 128:(dc + 1) * 128], actT[:, jj, tsl],
                                jj == 0, jj == nj - 1, [("wob", jj), ("actT", jj)], [("ps", pb)])
                    self.stt(xT[:, dc, tsl], ps[pb][:, :], ga(dc), xT[:, dc, tsl], ALU.mult, ALU.add,
                             [("ps", pb), ("mod", l), ("xT", dc, n)], [("xT", dc, n)])


_CACHE = {}


def _get_prog(key, **kw):
    if key not in _CACHE:
        b = B(**kw)
        b.build()
        _CACHE[key] = b
    return _CACHE[key]


def kernel(**inputs):
    b = _get_prog("full")
    hc = host_consts()
    in_maps = []
    for core in range(8):
        m = {}
        for name in b.din:
            shp = tuple(b.din[name].shape)
            if name == "x":
                m[name] = np.ascontiguousarray(inputs["x"][core])
            elif name == "c":
                m[name] = np.ascontiguousarray(inputs["c"][core:core + 1])
            elif name in hc:
                m[name] = hc[name]
            else:
                m[name] = np.ascontiguousarray(np.asarray(inputs[name], dtype=np.float32)).reshape(shp)
        in_maps.append(m)
    res = run_bass_kernel_spmd(b.nc, in_maps, core_ids=list(range(8)))
    return np.stack([r["out"] for r in res.results], axis=0)
```

```python
import math
import numpy as np
import ml_dtypes
import concourse.bass as bass
import concourse.mybir as mybir
from concourse.bass_utils import run_bass_kernel_spmd

F32 = mybir.dt.float32
BF16 = mybir.dt.bfloat16
AF = mybir.ActivationFunctionType
ALU = mybir.AluOpType
AX = mybir.AxisListType

D = 1024
S = 2048
DEPTH = 4
DFF = 2816
NT = S // 128
NB = S // 512
KC = D // 128

EPOCH = 4000
COMPUTE = ("pe", "act", "dve", "pool")
NDMA = {"sp": 20, "pool": 8, "act": 8}


class Prog:
    def __init__(self, nc, same_engine_sync=True):
        self.nc = nc
        self.same_sync = same_engine_sync
        self.ops = {e: [] for e in ("pe", "act", "dve", "pool", "sp")}
        self.cnt = {e: 0 for e in COMPUTE}
        self.clock = {e: {} for e in self.ops}
        self.iclock = {e: [] for e in COMPUTE}
        self.sems = {e: [] for e in COMPUTE}
        self.dma_sems = {q: [nc.alloc_semaphore(f"dq_{q}_{i}") for i in range(n)] for q, n in NDMA.items()}
        self.dma_n = {q: 0 for q in NDMA}
        self.dma_done = {e: set() for e in self.ops}
        self.dma_info = {}
        self.lastw = {}
        self.reads = {}
        self.nwaits = 0

    def _sem(self, e, n):
        ep = n // EPOCH
        while len(self.sems[e]) <= ep:
            self.sems[e].append(self.nc.alloc_semaphore(f"s_{e}_{len(self.sems[e])}"))
        return self.sems[e][ep], n % EPOCH + 1

    def _collect(self, eng, reads, writes):
        deps = set()
        for k in reads:
            w = self.lastw.get(k)
            if w is not None:
                deps.add(w)
        for k in writes:
            w = self.lastw.get(k)
            if w is not None:
                deps.add(w)
            for r in self.reads.get(k, ()):
                deps.add(r)
        waits = []
        ck = self.clock[eng]
        import os as _os
        _rev = _os.environ.get("WREV", "0") == "1"
        for d in sorted(deps, key=lambda d: (str(d[0]), str(d[1])), reverse=_rev):
            if d[0] == "dma":
                if d[1] in self.dma_done[eng]:
                    continue
                self.dma_done[eng].add(d[1])
                waits.append(self.dma_info[d[1]])
            else:
                src, n = d
                if src == eng and (src == "pe" or not self.same_sync):
                    continue
                if ck.get(src, 0) >= n + 1:
                    continue
                waits.append(self._sem(src, n))
                for s2, c2 in self.iclock[src][n].items():
                    if ck.get(s2, 0) < c2:
                        ck[s2] = c2
                if ck.get(src, 0) < n + 1:
                    ck[src] = n + 1
        self.nwaits += len(waits)
        return waits

    def _record(self, tag, reads, writes):
        for k in writes:
            self.lastw[k] = tag
            self.reads[k] = []
        for k in reads:
            if k not in writes:
                self.reads.setdefault(k, []).append(tag)

    def op(self, eng, fn, reads=(), writes=()):
        reads = list(reads); writes = list(writes)
        ex = [k for k in reads if isinstance(k, tuple) and k[0] == "ps" and k not in writes]
        if ex:
            reads = [k for k in reads if k not in ex]
            writes = writes + ex
        waits = self._collect(eng, reads, writes)
        n = self.cnt[eng]
        self.cnt[eng] += 1
        sem, val = self._sem(eng, n)
        self.iclock[eng].append(dict(self.clock[eng]))
        self.ops[eng].append((waits, fn, (sem, 1)))
        self._record((eng, n), reads, writes)

    def dma(self, q, fn, reads=(), writes=()):
        reads = list(reads); writes = list(writes)
        waits = self._collect(q, reads, writes)
        j = self.dma_n[q]
        self.dma_n[q] += 1
        M = len(self.dma_sems[q])
        sem = self.dma_sems[q][j % M]
        target = 16 * (j // M + 1)
        if j >= M:
            prev = (q, j - M)
            if prev not in self.dma_done[q]:
                self.dma_done[q].add(prev)
                waits.append(self.dma_info[prev])
        did = (q, j)
        self.dma_info[did] = (sem, target)
        if q in COMPUTE:
            pass
        self.ops[q].append((waits, fn, (sem, 16)))
        self._record(("dma", did), reads, writes)
        return did

    def alias(self, new_keys, old_keys):
        tags = []
        for k in old_keys:
            w = self.lastw.get(k)
            if w is not None:
                tags.append(w)
            tags.extend(self.reads.get(k, ()))
        tags = list(dict.fromkeys(tags))
        for k in new_keys:
            self.lastw[k] = None
            self.reads[k] = list(tags)

    def finish(self, final_keys):
        nc = self.nc
        waits = self._collect("sp", list(final_keys), [])
        self.ops["sp"].append((waits, None, None))
        emap = {"pe": "tensor", "act": "scalar", "dve": "vector", "pool": "gpsimd", "sp": "sync"}
        with nc.Block() as block:
            for e, attr in emap.items():
                lst = self.ops[e]

                def body(engobj, lst=lst):
                    for waits, fn, inc in lst:
                        for s, v in waits:
                            engobj.wait_ge(s, v)
                        if fn is not None:
                            ins = fn(engobj)
                            ins.then_inc(inc[0], inc[1])
                getattr(block, attr)(body)


def t5_bucket_np(dist):
    n = np.maximum(dist, 0)
    nf = np.maximum(n, 1).astype(np.float32)
    large = 16 + (np.log(nf / np.float32(16)) / np.float32(math.log(8.0)) * np.float32(16)).astype(np.int32)
    large = np.minimum(large, 31)
    return np.where(n < 16, n, large)


NF = 4096
FOFF = 2048


def host_consts():
    c = {}
    c["ident_f"] = np.eye(128, dtype=np.float32)
    dist = NF - 1 - np.arange(NF) - FOFF
    bk = t5_bucket_np(dist)
    oh = np.zeros((33, NF), np.float32)
    oh[bk, np.arange(NF)] = 1.0
    oh[31, :] -= 1.0
    oh[:32, dist < 0] = 0.0
    oh[32, :] = np.where(dist < 0, -30000.0, 0.0)
    c["onehot"] = oh
    c["causalT"] = np.triu(np.ones((64, 64), np.float32))
    sm = np.ones((4, S), np.float32); sm[:, ::64] = 0.0
    c["scanmask"] = sm
    hm = np.zeros((4, 4, 32), np.float32)
    for h in range(4):
        hm[h, h, :] = 1.0
    c["hmask"] = hm.reshape(4, 128)
    tl = np.arange(128)
    c["d4mask"] = np.where(tl[None, :] < tl[:, None], 0.0, -30000.0).astype(np.float32)
    ex = np.zeros((32, S), np.float32)
    ex[np.arange(S) // 64, np.arange(S)] = 1.0
    c["expand"] = ex
    ac = np.zeros((128, 16, 32), np.float32)
    for qi in range(16):
        for t in range(128):
            qb = (qi * 128 + t) // 64
            for jb in range(32):
                if jb > qb:
                    ac[t, qi, jb] = -1e30
                elif jb == 0 or jb == qb or jb == qb - 1:
                    ac[t, qi, jb] = 1e4
    c["addc"] = ac.reshape(128, 512)
    start = np.arange(127) * 16
    sj = np.arange(32) * 64
    ov = np.minimum(start[:, None] + 32, sj[None, :] + 64) - np.maximum(start[:, None], sj[None, :])
    c["overlap"] = (np.clip(ov, 0, None) / 32).astype(np.float32)
    bo = np.zeros((128, 128), np.float32); bo[:64, :64] = 1; bo[64:, 64:] = 1
    c["blockones"] = bo
    return c


class B:
    def __init__(self, nlayers=DEPTH, mixers=True, debug=None):
        self.nlayers = nlayers
        self.mixers = mixers
        self.debug = debug
        nc = self.nc = bass.Bass("TRN2", target_bir_lowering=False)
        self.P = Prog(nc)
        self.din = {}
        self.sb_off = 16512
        self.sb_end = 229344
        self.scr_end = 229344
        self.uid = 0
        self.psn = 0

    def dram_in(self, name, shape, dt=F32):
        t = self.nc.dram_tensor(name, list(shape), dt, kind="ExternalInput")
        self.din[name] = t
        return t.ap()

    def sb(self, name, shape, dt, off=None):
        nbytes = int(np.prod(shape[1:])) * (4 if dt == F32 else 2)
        if off is None:
            off = self.sb_off
            self.sb_off += (nbytes + 63) // 64 * 64
            assert self.sb_off <= self.sb_end, (name, self.sb_off)
        self.uid += 1
        return self.nc.alloc_sbuf_tensor_at(f"{name}_{self.uid}", list(shape), dt, offset=off)

    def mm(self, out, lhsT, rhs, start, stop, reads, writes):
        self.P.op("pe", lambda e: e.matmul(out, lhsT, rhs, start=start, stop=stop), reads, writes)

    def tr(self, out, in_, ident, reads, writes):
        self.P.op("pe", lambda e: e.transpose(out, in_, ident), reads, writes)

    def act(self, out, in_, func, reads, writes, bias=None, scale=None, accum_out=None):
        kw = {}
        if bias is not None:
            kw["bias"] = bias
        if scale is not None:
            kw["scale"] = scale
        if accum_out is not None:
            kw["accum_out"] = accum_out
        self.P.op("act", lambda e: e.activation(out=out, in_=in_, func=func, **kw), reads, writes)

    def tt(self, out, in0, in1, op, reads, writes, eng="dve"):
        self.P.op(eng, lambda e: e.tensor_tensor(out=out, in0=in0, in1=in1, op=op), reads, writes)

    def ts(self, out, in0, s1, op0, reads, writes, s2=None, op1=None, eng="dve", accum_out=None):
        kw = {}
        if op1 is not None:
            kw["op1"] = op1
        if accum_out is not None:
            kw["accum_out"] = accum_out
        self.P.op(eng, lambda e: e.tensor_scalar(out=out, in0=in0, scalar1=s1, scalar2=s2, op0=op0, **kw), reads, writes)

    def stt(self, out, in0, scalar, in1, op0, op1, reads, writes):
        self.P.op("dve", lambda e: e.scalar_tensor_tensor(out=out, in0=in0, scalar=scalar, in1=in1, op0=op0, op1=op1), reads, writes)

    def copy(self, out, in_, reads, writes, eng="dve"):
        self.P.op(eng, lambda e: e.tensor_copy(out=out, in_=in_), reads, writes)

    def dma(self, out, in_, reads, writes, q="sp", **kw):
        self.P.dma(q, lambda e: e.dma_start(out=out, in_=in_, **kw), reads, writes)

    def build(self):
        nc, P = self.nc, self.P
        L = self.nlayers
        x_d = self.dram_in("x", [S, D])
        vec_d = {}
        w_ada = self.dram_in("w_ada", [DEPTH, D, 6 * D])
        b_ada = self.dram_in("b_ada", [DEPTH, 6 * D])
        g_mix = self.dram_in("g_norm_mix", [DEPTH, D])
        g_ffn = self.dram_in("g_norm_ffn", [DEPTH, D])
        w_fi = self.dram_in("w_ffn_in", [DEPTH, D, 2 * DFF])
        w_fo = self.dram_in("w_ffn_out", [DEPTH, DFF, D])
        c_d = self.dram_in("c", [1, D])
        ident_d = self.dram_in("ident_f", [128, 128])
        self.w_a_in = self.dram_in("w_a_in", [2, D, 3080])
        self.b_a_if = self.dram_in("b_a_if", [2, 8])
        g_a_out = self.dram_in("g_a_out", [2, D])
        self.w_a_out = self.dram_in("w_a_out", [2, D, D])
        self.causalT_d = self.dram_in("causalT", [64, 64])
        self.w_kv_ada = self.dram_in("w_kv_ada", [D, 2 * D])
        b_kv_ada = self.dram_in("b_kv_ada", [1, 2 * D])
        g_kv_norm = self.dram_in("g_kv_norm", [1, D])
        self.w_kv = self.dram_in("w_kv", [D, 1536])
        self.pos_cmp = [self.dram_in("pos_cmp_k", [32, 64]), self.dram_in("pos_cmp_v", [32, 64])]
        self.w_cmp1 = [self.dram_in("w_cmp_k1", [2048, 256]), self.dram_in("w_cmp_v1", [2048, 256])]
        self.w_cmp2 = [self.dram_in("w_cmp_k2", [256, 64]), self.dram_in("w_cmp_v2", [256, 64])]
        self.g_knorm = self.dram_in("g_knorm", [3, 64])
        self.w_b_q = self.dram_in("w_b_q", [2, D, 1072])
        self.b_b_gate = self.dram_in("b_b_gate", [2, 48])
        self.g_qnorm = self.dram_in("g_qnorm", [2, 64])
        self.w_b_out = self.dram_in("w_b_out", [2, D, D])
        self.rel_table = self.dram_in("rel_table", [32, 16])
        self.onehot_d = self.dram_in("onehot", [33, NF])
        self.d4mask_d = self.dram_in("d4mask", [128, 128])
        self.expand_d = self.dram_in("expand", [32, S])
        self.addc_d = self.dram_in("addc", [128, 512])
        self.overlap_d = self.dram_in("overlap", [127, 32])
        self.blockones_d = self.dram_in("blockones", [128, 128])
        self.fvd = nc.dram_tensor("fvd", [16, NF], F32).ap()
        self.scanmask_d = self.dram_in("scanmask", [4, S])
        self.hmask_d = self.dram_in("hmask", [4, 128])
        out_d = nc.dram_tensor("out", [S, D], F32, kind="ExternalOutput").ap()

        xT = self.xT = self.sb("xT", [128, KC, S], F32)
        hT = self.hT = self.sb("hT", [128, KC, S], BF16)
        ident_f = self.ident_f = self.sb("ident_f", [128, 128], F32)
        ident_b = self.ident_b = self.sb("ident_b", [128, 128], BF16)
        ones_b = self.ones_b = self.sb("ones_b", [128, 128], BF16)
        NV = 304
        vecT = self.vecT = self.sb("vecT", [128, NV], F32)
        cact = self.cact = self.sb("cact", [128, KC], F32)
        mod = self.mod = self.sb("mod", [128, DEPTH, 48], F32)
        modkv = self.modkv = self.sb("modkv", [128, 24], F32)
        gs = self.gs = self.sb("gs", [128, DEPTH, 2, KC], F32)
        epsc = self.epsc = self.sb("epsc", [128, 1], F32)
        self.persist_end = self.sb_off
        scr0 = self.sb_off
        ps = self.ps = [nc.alloc_psum_tensor(f"ps{i}", [128, 512], F32) for i in range(8)]

        self.dma(ident_f[:], ident_d, [], ["ident_f"])
        self.copy(ident_b[:], ident_f[:], ["ident_f"], ["ident_b"])
        P.op("dve", lambda e: e.memset(ones_b[:], 1.0), [], ["ones_b"])
        P.op("dve", lambda e: e.memset(epsc[:], 1e-6), [], ["epsc"])

        vrows = [self.sb(f"vrows{i}", [128, 128], F32, off=scr0 + i * 512) for i in range(3)]
        srcs = [(b_ada.rearrange("l (m p) -> (l m) p", p=128), 0, 192),
                (g_mix.rearrange("l (m p) -> (l m) p", p=128), 192, 32),
                (g_ffn.rearrange("l (m p) -> (l m) p", p=128), 224, 32),
                (g_kv_norm.rearrange("o (m p) -> (o m) p", p=128), 256, 8),
                (b_kv_ada.rearrange("o (m p) -> (o m) p", p=128), 264, 16),
                (g_a_out.rearrange("l (m p) -> (l m) p", p=128), 280, 16),
                (c_d.rearrange("o (m p) -> (o m) p", p=128), 296, 8)]
        for i in range(3):
            P.op("dve", lambda e, i=i: e.memset(vrows[i][:], 0.0), [], [("vrows", i)])
        for ap, r0, n in srcs:
            r = r0
            while r < r0 + n:
                ti = r // 128
                cnt = min(r0 + n - r, (ti + 1) * 128 - r)
                self.dma(vrows[ti][r - ti * 128: r - ti * 128 + cnt, :], ap[r - r0: r - r0 + cnt, :],
                         [], [("vrows", ti)])
                r += cnt
        for i in range(3):
            n = min(128, NV - i * 128)
            self.tr(ps[0][:, i * 128: i * 128 + n], vrows[i][0:n, :], ident_f[0:n, 0:n],
                    [("vrows", i), "ident_f"], [("ps", 0)])
        self.copy(vecT[:], ps[0][:, 0:NV], [("ps", 0)], ["vecT"])
        self.act(cact[:], vecT[:, 296:304], AF.Silu, ["vecT"], ["cact"])

        xin = [self.sb(f"xin{i}", [128, D], F32, off=scr0 + 2048 + i * 4096) for i in range(2)]
        for t in range(NT):
            b = t % 2
            self.dma(xin[b][:], x_d[t * 128:(t + 1) * 128, :], [], [("xin", b)])
            for half in range(2):
                pb = 1 + (2 * t + half) % 4
                for cc in range(4):
                    c = half * 4 + cc
                    self.tr(ps[pb][:, cc * 128:(cc + 1) * 128], xin[b][:, c * 128:(c + 1) * 128], ident_f[:],
                            [("xin", b), "ident_f"], [("ps", pb)])
                o = xT[:, half * 4:(half + 1) * 4, t * 128:(t + 1) * 128]
                i_ = ps[pb][:, :].rearrange("p (c t) -> p c t", c=4)
                wk = [("xT", c_, t // 4) for c_ in range(half * 4, half * 4 + 4)]
                if half == 0:
                    self.copy(o, i_, [("ps", pb)], wk)
                else:
                    self.act(o, i_, AF.Identity, [("ps", pb)], wk)

        wst = [self.sb(f"wada{i}", [128, KC, 512], F32, off=scr0 + 12288 + i * 16384) for i in range(2)]
        nch = 0
        for l in range(L):
            wv = w_ada[l].rearrange("(c p) n -> p c n", p=128)
            for g in range(12):
                b = nch % 2
                nch += 1
                self.dma(wst[b][:], wv[:, :, g * 512:(g + 1) * 512], [], [("wada", b)])
                for mm_ in range(4):
                    m = g * 4 + mm_
                    for kc in range(KC):
                        self.mm(ps[5][:, m:m + 1], wst[b][:, kc, mm_ * 128:(mm_ + 1) * 128], cact[:, kc:kc + 1],
                                kc == 0, kc == KC - 1, [("wada", b), "cact"], [("ps", 5)])
            self.tt(mod[:, l, :], ps[5][:, 0:48], vecT[:, l * 48:(l + 1) * 48], ALU.add,
                    [("ps", 5), "vecT"], [("mod", l)])
            for which, (gbase, scoff) in enumerate(((192, 8), (224, 32))):
                self.stt(gs[:, l, which, :], mod[:, l, scoff:scoff + 8], 1.0,
                         vecT[:, gbase + l * 8: gbase + l * 8 + 8], ALU.add, ALU.mult,
                         [("mod", l), "vecT"], [("gs", l)])

        if L > 2:
            wv = self.w_kv_ada.rearrange("(c p) n -> p c n", p=128)
            for g in range(4):
                b = nch % 2
                nch += 1
                self.dma(wst[b][:], wv[:, :, g * 512:(g + 1) * 512], [], [("wada", b)])
                for mm_ in range(4):
                    m = g * 4 + mm_
                    for kc in range(KC):
                        self.mm(ps[5][:, m:m + 1], wst[b][:, kc, mm_ * 128:(mm_ + 1) * 128], cact[:, kc:kc + 1],
                                kc == 0, kc == KC - 1, [("wada", b), "cact"], [("ps", 5)])
            self.tt(modkv[:, 0:16], ps[5][:, 0:16], vecT[:, 264:280], ALU.add, [("ps", 5), "vecT"], ["kvmod"])
            self.stt(modkv[:, 16:24], modkv[:, 8:16], 1.0, vecT[:, 256:264], ALU.add, ALU.mult,
                     ["kvmod", "vecT"], ["kvmod"])
        self.scr0 = scr0
        self._scratch_keys = set([("vrows", i) for i in range(3)] + [("xin", 0), ("xin", 1), ("wada", 0), ("wada", 1)])
        for l in range(L):
            if self.mixers:
                if l == 2:
                    self.nsa_kv()
                self.norm_mod(l, 0)
                if l < 2:
                    if not (self.debug or "").startswith("kv"):
                        self.mlstm(l)
                elif self.debug != "nomix" and not (self.debug or "").startswith("kv"):
                    self.nsa(l)
            if not (self.debug or "").startswith("kv"):
                self.norm_mod(l, 1)
                self.ffn(l, w_fi, w_fo)

        xo = [self.sb(f"xo{i}", [128, D], F32, off=scr0 + i * 4096) for i in range(2)]
        P.alias([("xo", 0), ("xo", 1)], self.scratch_keys())
        for t in range(NT):
            b = t % 2
            for half in range(2):
                pb = (2 * t + half) % 4
                for cc in range(4):
                    c = half * 4 + cc
                    self.tr(ps[pb][:, cc * 128:(cc + 1) * 128], xT[:, c, t * 128:(t + 1) * 128], ident_f[:],
                            [("xT", c, t // 4), "ident_f"], [("ps", pb)])
                o = xo[b][:, half * 512:(half + 1) * 512]
                if half == 0:
                    self.copy(o, ps[pb][:, :], [("ps", pb)], [("xo", b)])
                else:
                    self.act(o, ps[pb][:, :], AF.Identity, [("ps", pb)], [("xo", b)])
            self.dma(out_d[t * 128:(t + 1) * 128, :], xo[b][:], [("xo", b)], [("out", t)])
        P.finish([("out", t) for t in range(NT)])
        return nc

    def scratch_keys(self):
        return list(self._scratch_keys)


    def use_scratch(self, new_keys):
        self.P.alias(new_keys, list(self._scratch_keys))
        self._scratch_keys = set(new_keys)

    def norm_mod(self, l, which, gsc=None, shift=None):
        P, ps, xT, hT = self.P, self.ps, self.xT, self.hT
        if gsc is None:
            gsc = lambda c: self.gs[:, l, which, c:c + 1]
            sho = 0 if which == 0 else 24
            shift = lambda c: self.mod[:, l, sho + c: sho + c + 1]
            rk = [("gs", l), ("mod", l)]
        else:
            rk = ["kvmod"]
        scr0 = self.scr0
        sq = [self.sb(f"sq{i}", [128, 512], BF16, off=scr0 + i * 1024) for i in range(2)]
        rstd = [self.sb(f"rstd{i}", [128, 512], F32, off=scr0 + 2048 + i * 2048) for i in range(2)]
        tmp = [self.sb(f"ntmp{i}", [128, 512], F32, off=scr0 + 6144 + i * 2048) for i in range(2)]
        keys = [("sq", 0), ("sq", 1), ("rstd", 0), ("rstd", 1), ("ntmp", 0), ("ntmp", 1)]
        self.use_scratch(keys)
        k = 0
        for n in range(NB):
            tsl = slice(n * 512, (n + 1) * 512)
            pb = 6 + n % 2
            for c in range(KC):
                b = k % 2
                k += 1
                self.act(sq[b][:], xT[:, c, tsl], AF.Square, [("xT", c, n)], [("sq", b)])
                self.mm(ps[pb][:, :], self.ones_b[:], sq[b][:], c == 0, c == KC - 1,
                        [("sq", b), "ones_b"], [("ps", pb)])
            rb = n % 2
            self.act(rstd[rb][:], ps[pb][:, :], AF.Sqrt, [("ps", pb), "epsc"], [("rstd", rb)],
                     bias=self.epsc[:], scale=1.0 / D)
            P.op("dve", lambda e, rb=rb: e.reciprocal(out=rstd[rb][:], in_=rstd[rb][:]), [("rstd", rb)], [("rstd", rb)])
            for c in range(KC):
                b = c % 2
                self.tt(tmp[b][:], xT[:, c, tsl], rstd[rb][:], ALU.mult, [("xT", c, n), ("rstd", rb)], [("ntmp", b)])
                self.act(hT[:, c, tsl], tmp[b][:], AF.Identity, [("ntmp", b)] + rk, [("hT", c, n)],
                         bias=shift(c), scale=gsc(c))


    def bcast(self, t, nparts, pre, n, post):
        a = t[:]
        ps_ = a.ap[0][0]
        dims = [[ps_, nparts]]
        if pre > 1:
            dims.append([0, pre])
        dims.append([1, n])
        if post > 1:
            dims.append([0, post])
        return bass.AP(a.tensor, a.offset, dims)

    def mlstm(self, l):
        P, ps, xT, hT, nc = self.P, self.ps, self.xT, self.hT, self.nc
        o = self.scr0
        def take(name, shape, dt):
            nonlocal o
            t = self.sb(name, shape, dt, off=o)
            o += (int(np.prod(shape[1:])) * (4 if dt == F32 else 2) + 63) // 64 * 64
            assert o <= self.sb_end, (name, o)
            return t
        yT = take("yT", [128, KC, S], BF16)
        qT = take("qT", [128, S], BF16)
        kT = take("kT", [128, S], BF16)
        vtok = take("vtok", [64, 32, 257], BF16)
        soT = take("soT", [128, 2, S], BF16)
        wh = take("wh", [128, KC, 768], BF16)
        wst = [take(f"wst{i}", [128, KC, 128], F32) for i in range(2)]
        vo = self._off_of(vtok)
        GA = self.sb("GA", [4, S], F32, off=vo)
        GB = self.sb("GB", [4, S], F32, off=vo + 8192)
        GC = self.sb("GC", [4, S], F32, off=vo + 16384)
        small = take("msmall", [4, 8, 32], F32)
        Rm = take("Rm", [4, 128], F32)
        hmask = take("hmask", [4, 128], F32)
        ones4 = take("ones4", [4, 128], F32)
        scanmask = GC
        bif = take("bif", [4, 2], F32)
        wg_s = take("wg_s", [128, KC, 8], F32)
        wg = take("wg", [128, KC, 8], BF16)
        Etok = take("Etok", [64, 32, 4], F32)
        DNtok = take("DNtok", [64, 32, 4], F32)
        a_bc = take("a_bc", [128, 128], F32)
        causalT = take("causalT", [64, 64], F32)
        Cst = take("Cst", [128, 257], F32)
        Cab = [take(f"Cab{i}", [128, 257], BF16) for i in range(2)]
        STm = [take(f"STm{i}", [64, 64], BF16) for i in range(2)]
        ktil = [take(f"ktil{i}", [64, 128], BF16) for i in range(2)]
        hn = [take(f"hn{i}", [64, 256], BF16) for i in range(2)]
        sv = [take(f"sv{i}", [64, 8], F32) for i in range(2)]
        junk = take("junk", [64, 256], BF16)
        keys = ["yT", "qT", "kT", "wh", ("wst", 0), ("wst", 1), "GA", "GB", "GC", "msmall", "Rm",
                "hmask", "ones4", "bif", "wg_s", "wg", "Etok", "DNtok", "a_bc", "causalT", "Cst", ("Cab", 0),
                ("Cab", 1), ("STm", 0), ("STm", 1), ("ktil", 0), ("ktil", 1), ("hn", 0), ("hn", 1), ("sv", 0),
                ("sv", 1), "junk"] + [("yTk", c, n) for c in range(KC) for n in range(NB)]
        self.use_scratch(keys)
        w_in = self.w_a_in[l].rearrange("(c p) n -> p c n", p=128)
        hTk = lambda c, n: ("hT", c, n)
        self.dma(causalT[:], self.causalT_d, [], ["causalT"])
        self.dma(hmask[:], self.hmask_d, [], ["hmask"])
        self.dma(GC[:], self.scanmask_d, [], ["GC"])
        P.op("dve", lambda e: e.memset(ones4[:], 1.0), [], ["ones4"])
        self.dma(bif[:], self.b_a_if[l].rearrange("(two p) -> p two", p=4), [], ["bif"], allow_slow_non_contiguous=True)
        self.dma(wg_s[:], w_in[:, :, 3072:3080], [], ["wg_s"])
        self.copy(wg[:], wg_s[:], ["wg_s"], ["wg"])
        for n in range(NB):
            tsl = slice(n * 512, (n + 1) * 512)
            for part, dst, bcol in ((0, GA, 0), (1, GB, 1)):
                pb = (2 * n + part) % 4
                for c in range(KC):
                    self.mm(ps[pb][0:4, :], wg[:, c, part * 4:(part + 1) * 4], hT[:, c, tsl], c == 0, c == KC - 1,
                            ["wg", hTk(c, n)], [("ps", pb)])
                if part == 0:
                    self.act(dst[:, tsl], ps[pb][0:4, :], AF.Identity, [("ps", pb), "bif"], ["GA"], bias=bif[:, 0:1])
                else:
                    self.act(dst[:, tsl], ps[pb][0:4, :], AF.Identity, [("ps", pb), "bif"], ["GB"], bias=bif[:, 1:2])
        self.act(GB[:], GB[:], AF.Exp, ["GB"], ["GB"], scale=-1.0)
        self.act(GB[:], GB[:], AF.Ln, ["GB"], ["GB"], bias=1.0)
        umax, blast, mnext, mprev, mu, av = (small[:, i, :] for i in range(6))
        P.op("dve", lambda e: e.tensor_tensor_scan(out=GC[:], data0=GC[:], data1=GB[:], initial=0.0,
                                                    op0=ALU.mult, op1=ALU.add), ["GB", "GC"], ["GC"])
        self.tt(GA[:], GA[:], GC[:], ALU.add, ["GA", "GC"], ["GA"])
        P.op("dve", lambda e: e.tensor_reduce(out=umax, in_=GA[:].rearrange("p (c s) -> p c s", s=64), axis=AX.X,
                                               op=ALU.max), ["GA"], ["msmall"])
        self.ts(blast, GC[:].rearrange("p (c s) -> p c s", s=64)[:, :, 63], -1.0, ALU.mult, ["GC"], ["msmall"])
        P.op("dve", lambda e: e.tensor_tensor_scan(out=mnext, data0=umax, data1=blast, initial=0.0,
                                                    op0=ALU.max, op1=ALU.add), ["msmall"], ["msmall"])
        P.op("dve", lambda e: e.memset(mprev[:, 0:1], 0.0), ["msmall"], ["msmall"])
        self.copy(mprev[:, 1:32], mnext[:, 0:31], ["msmall"], ["msmall"])
        self.tt(mu, mprev, umax, ALU.max, ["msmall"], ["msmall"])
        self.tt(av, mprev, mu, ALU.subtract, ["msmall"], ["msmall"])
        self.act(av, av, AF.Exp, ["msmall"], ["msmall"])
        mu_b = self.bcast_small(small, 4, 32, 64)
        self.tt(GA[:].rearrange("p (c s) -> p c s", s=64), GA[:].rearrange("p (c s) -> p c s", s=64), mu_b,
                ALU.subtract, ["GA", "msmall"], ["GA"])
        self.act(GA[:], GA[:], AF.Exp, ["GA"], ["GA"])
        self.tt(GC[:].rearrange("p (c s) -> p c s", s=64), GC[:].rearrange("p (c s) -> p c s", s=64), mu_b,
                ALU.subtract, ["GC", "msmall"], ["GC"])
        self.act(GC[:], GC[:], AF.Exp, ["GC"], ["GC"])
        for src, dst, key, pb in ((GA, Etok, "Etok", 0), (GC, DNtok, "DNtok", 1)):
            for c in range(32):
                self.tr(ps[pb][0:64, c * 4:(c + 1) * 4], src[:, c * 64:(c + 1) * 64], self.ident_f[0:4, 0:4],
                        ["GA", "GC", "ident_f"], [("ps", pb)])
            self.copy(dst[:], ps[pb][0:64, 0:128].rearrange("p (c h) -> p c h", h=4), [("ps", pb)], [key])
        a_b = bass.AP(small[:].tensor, small[:, 5, :].offset, [[small[:].ap[0][0], 4], [0, 4], [1, 32]])
        self.tt(Rm[:].rearrange("p (h c) -> p h c", h=4), a_b, hmask[:].rearrange("p (h c) -> p h c", h=4), ALU.mult,
                ["msmall", "hmask"], ["Rm"])
        self.mm(ps[2][:, 0:128], ones4[:], Rm[:], True, True, ["ones4", "Rm"], [("ps", 2)])
        self.copy(a_bc[:], ps[2][:, 0:128], [("ps", 2)], ["a_bc"])

        vkeys = ["vtok"] + [("vtok", i) for i in range(4)] + [("soT", n) for n in range(NB)]
        P.alias(vkeys, ["GA", "GB", "GC"])
        self._scratch_keys |= set(vkeys)
        P.op("dve", lambda e: e.memset(vtok[:, :, 256:257], 1.0), [], ["vtok"])
        ps6b = ps[6][:].bitcast(BF16)
        ps7b = ps[7][:].bitcast(BF16)
        nst = 0
        for h in range(4):
            cols = [h * 128, 512 + h * 128, 1024 + h * 256, 1024 + h * 256 + 128, 2048 + h * 256, 2048 + h * 256 + 128]
            for i, c0 in enumerate(cols):
                b = nst % 2
                nst += 1
                self.dma(wst[b][:], w_in[:, :, c0:c0 + 128], [], [("wst", b)])
                self.copy(wh[:, :, i * 128:(i + 1) * 128], wst[b][:], [("wst", b)], ["wh"], eng="pool")
            for n in range(NB):
                tsl = slice(n * 512, (n + 1) * 512)
                for which, dst, key in ((0, qT, "qT"), (1, kT, "kT")):
                    pb = (2 * n + which) % 4
                    for c in range(KC):
                        self.mm(ps[pb][:, :], wh[:, c, which * 128:(which + 1) * 128], hT[:, c, tsl], c == 0, c == KC - 1,
                                ["wh", hTk(c, n)], [("ps", pb)])
                    if which == 0:
                        self.act(dst[:, tsl], ps[pb][:, :], AF.Copy, [("ps", pb)], [(key, n)], scale=128.0 ** -0.5)
                    else:
                        self.copy(dst[:, tsl], ps[pb][:, :], [("ps", pb)], [(key, n)])
                for half in range(2):
                    pb = 4 + half
                    for c in range(KC):
                        self.mm(ps[pb][:, :], wh[:, c, 512 + half * 128: 640 + half * 128], hT[:, c, tsl], c == 0,
                                c == KC - 1, ["wh", hTk(c, n)], [("ps", pb)])
                    self.act(soT[:, half, tsl], ps[pb][:, :], AF.Sigmoid, [("ps", pb)], [("soT", n)])
            for cp in range(16):
                pb = cp % 4
                for j in range(2):
                    ch = 2 * cp + j
                    for c in range(KC):
                        self.mm(ps[pb][0:64, j * 256:(j + 1) * 256], hT[:, c, ch * 64:(ch + 1) * 64], wh[:, c, 256:512],
                                c == 0, c == KC - 1, ["wh", hTk(c, ch // 8)], [("ps", pb)])
                self.copy(vtok[:, 2 * cp:2 * cp + 2, 0:256], ps[pb][0:64, :].rearrange("p (j v) -> p j v", j=2),
                          [("ps", pb)], [("vtok", cp // 4)])
            for ch in range(32):
                tsl = slice(ch * 64, (ch + 1) * 64)
                n = ch // 8
                b = ch % 2
                e_s = Etok[:, ch, h:h + 1]
                a_c = a_bc[:, h * 32 + ch: h * 32 + ch + 1]
                pst, pnum, pkv = b, 2 + b, 4 + b
                self.mm(ps[pst][0:64, 0:64], kT[:, tsl], qT[:, tsl], True, True, [("kT", n), ("qT", n)], [("ps", pst)])
                self.tr(ps6b[0:64, b * 128:(b + 1) * 128], kT[:, tsl], self.ident_b[:], [("kT", n), "ident_b"],
                        [("ps", 6)])
                self.stt(STm[b][:], ps[pst][0:64, 0:64], e_s, causalT[:], ALU.mult, ALU.mult,
                         [("ps", pst), "Etok", "causalT"], [("STm", b)])
                self.act(ktil[b][:], ps6b[0:64, b * 128:(b + 1) * 128], AF.Copy, [("ps", 6), "Etok"], [("ktil", b)],
                         scale=e_s)
                if ch > 0:
                    self.ts(Cab[b][:], Cst[:], a_c, ALU.mult, ["Cst", "a_bc"], [("Cab", b)])
                self.mm(ps[pnum][0:64, 0:257], STm[b][:], vtok[:, ch, :], True, ch == 0,
                        [("STm", b), ("vtok", ch // 8), "vtok"], [("ps", pnum)])
                if ch > 0:
                    self.mm(ps[pnum][0:64, 0:257], qT[:, tsl], Cab[b][:], False, True, [("qT", n), ("Cab", b)],
                            [("ps", pnum)])
                self.mm(ps[pkv][:, 0:257], ktil[b][:], vtok[:, ch, :], True, True,
                        [("ktil", b), ("vtok", ch // 8), "vtok"], [("ps", pkv)])
                if ch == 0:
                    self.copy(Cst[:], ps[pkv][:, 0:257], [("ps", pkv)], ["Cst"])
                else:
                    self.stt(Cst[:], Cst[:], a_c, ps[pkv][:, 0:257], ALU.mult, ALU.add,
                             ["Cst", "a_bc", ("ps", pkv)], ["Cst"])
                s_ = sv[b]
                self.ts(s_[:, 5:6], ps[pnum][0:64, 256:257], -1.0, ALU.mult, [("ps", pnum), "DNtok"], [("sv", b)],
                        s2=DNtok[:, ch, h:h + 1], op1=ALU.max)
                self.tt(s_[:, 0:1], s_[:, 5:6], ps[pnum][0:64, 256:257], ALU.max, [("ps", pnum), ("sv", b)], [("sv", b)])
                P.op("dve", lambda e, s_=s_: e.reciprocal(out=s_[:, 1:2], in_=s_[:, 0:1]), [("sv", b)], [("sv", b)])
                self.act(junk[:], ps[pnum][0:64, 0:256], AF.Square, [("ps", pnum), ("sv", b)], ["junk", ("sv", b)],
                         scale=s_[:, 1:2], accum_out=s_[:, 2:3])
                self.act(s_[:, 3:4], s_[:, 2:3], AF.Sqrt, [("sv", b), "epsc"], [("sv", b)], bias=self.epsc[0:64, :],
                         scale=1.0 / 256)
                P.op("dve", lambda e, s_=s_: e.reciprocal(out=s_[:, 3:4], in_=s_[:, 3:4]), [("sv", b)], [("sv", b)])
                self.tt(s_[:, 4:5], s_[:, 3:4], s_[:, 1:2], ALU.mult, [("sv", b)], [("sv", b)])
                self.act(hn[b][:], ps[pnum][0:64, 0:256], AF.Copy, [("ps", pnum), ("sv", b)], [("hn", b)],
                         scale=s_[:, 4:5])
                for half in range(2):
                    dc = 2 * h + half
                    self.tr(ps7b[:, (2 * b + half) * 64:(2 * b + half + 1) * 64], hn[b][:, half * 128:(half + 1) * 128],
                            self.ident_b[0:64, 0:64], [("hn", b), "ident_b"], [("ps", 7)])
                    self.stt(yT[:, dc, tsl], ps7b[:, (2 * b + half) * 64:(2 * b + half + 1) * 64],
                             self.vecT[:, 280 + l * 8 + dc: 281 + l * 8 + dc], soT[:, half, tsl], ALU.mult, ALU.mult,
                             [("ps", 7), "vecT", ("soT", n)], [("yTk", dc, n)])
        wao = self.sb("wao", [128, KC, D], BF16, off=vo)
        wos = [self.sb(f"waos{i}", [128, D], F32, off=vo + 16384 + i * 4096) for i in range(2)]
        P.alias(["wao", ("waos", 0), ("waos", 1)], vkeys)
        for c in range(KC):
            b = c % 2
            self.dma(wos[b][:], self.w_a_out[l, c * 128:(c + 1) * 128, :], [], [("waos", b)])
            self.copy(wao[:, c, :], wos[b][:], [("waos", b)], ["wao"], eng="pool")
        self._scratch_keys |= {"wao", ("waos", 0), ("waos", 1)}
        k = 0
        for dc in range(KC):
            for n in range(NB):
                tsl = slice(n * 512, (n + 1) * 512)
                pb = k % 2
                k += 1
                for c in range(KC):
                    self.mm(ps[pb][:, :], wao[:, c, dc * 128:(dc + 1) * 128], yT[:, c, tsl], c == 0, c == KC - 1,
                            ["wao", ("yTk", c, n)], [("ps", pb)])
                self.stt(xT[:, dc, tsl], ps[pb][:, :], self.mod[:, l, 16 + dc: 17 + dc], xT[:, dc, tsl], ALU.mult, ALU.add,
                         [("ps", pb), ("mod", l), ("xT", dc, n)], [("xT", dc, n)])

    def _off_of(self, t):
        return t.manual_sbuf_range[0]

    def bcast_small(self, small, idx, n, post):
        a = small[:, idx, :]
        return bass.AP(a.tensor, a.offset, [[a.ap[0][0], 4], [1, n], [0, post]])


    def ptake(self, name, shape, dt):
        nb = (int(np.prod(shape[1:])) * (4 if dt == F32 else 2) + 63) // 64 * 64
        self.scr_end -= nb
        return self.sb(name, shape, dt, off=self.scr_end)

    def rev_ap(self, t, nparts, mid, n):
        a = t[:]
        return bass.AP(a.tensor, a.offset + n - 1, [[a.ap[0][0], nparts], [n, mid], [-1, n]])

    def nsa_kv(self):
        P, ps, xT, hT, nc = self.P, self.ps, self.xT, self.hT, self.nc
        ksT = self.ksT = self.ptake("ksT", [128, 2, S], BF16)
        kwT = self.kwT = self.ptake("kwT", [128, 2, S], BF16)
        vs = self.vs = self.ptake("vs", [128, NT, 4, 65], BF16)
        vw = self.vw = self.ptake("vw", [128, NT, 4, 65], BF16)
        kcT = self.kcT = self.ptake("kcT", [128, 2, 128], BF16)
        vcx = self.vcx = self.ptake("vcx", [128, 4, 97], BF16)
        PatD = self.PatD = [self.ptake(f"PatD{i}", [128, 16, 128], BF16) for i in range(2)]
        PatD4 = self.PatD4 = self.ptake("PatD4", [128, 4, 128], BF16)
        Expand = self.Expand = self.ptake("Expand", [32, S], BF16)
        addc = self.addc = self.ptake("addc", [128, 16, 32], F32)
        zrow = self.zrow = self.ptake("zrow", [1, 512], BF16)
        blockones = self.blockones = self.ptake("blockones", [128, 128], BF16)
        gk = self.gk = self.ptake("gk", [128, 4], F32)
        gk0b = self.gk0b = self.ptake("gk0b", [128, 64], F32)
        tiny = self.tiny = self.ptake("tiny", [128, 1], F32)
        assert self.scr0 + 59392 <= self.scr_end, (self.scr0, self.scr_end)
        o = self.scr0
        def take(name, shape, dt):
            nonlocal o
            t = self.sb(name, shape, dt, off=o)
            o += (int(np.prod(shape[1:])) * (4 if dt == F32 else 2) + 63) // 64 * 64
            assert o <= self.scr_end, (name, o)
            return t
        stg = take("kvstg", [128, 2048], F32)
        wkb = take("wkb", [128, KC, 256], BF16)
        wpad = [take(f"wpad{i}", [128, KC, 128], BF16) for i in range(2)]
        c2T = take("c2T", [128, S], BF16)
        w1b = take("w1b", [128, 16, 256], BF16)
        posf = take("posf", [128, 16], F32)
        posb = take("posb", [128, 2], F32)
        w2b = take("w2b", [128, 2, 64], BF16)
        hid = take("hid", [128, 2, 128], BF16)
        gx = [take(f"gx{i}", [128, 128], F32) for i in range(3)]
        sqk = take("sqk", [128, 512], BF16)
        kraw = take("kraw", [128, 512], F32)
        rk_ = take("rk_", [128, 512], F32)
        ctok = take("ctok", [128, 4, 64], F32)
        ckn = take("ckn", [128, 4, 64], BF16)
        csm = take("csm", [128, 8], F32)
        tab33 = take("tab33", [33, 16], F32)
        ohs = take("ohs", [33, 512], F32)
        Fsb = self.sb("Fsb", [16, NF], F32, off=self._off_of(wpad[0]))
        keys = ["kvstg", "wkb", ("wpad", 0), ("wpad", 1), "c2T", "w1b", "posf", "posb", "w2b", "hid", "gx", ("sqk", "kv"), ("kraw", "kv"),
                ("rk_", "kv"), "ctok", "ckn", "csm", "tab33", "ohs", "Fsb"]
        self.use_scratch(keys)
        stg3 = stg[:].rearrange("p (c n) -> p c n", c=KC)
        self.dma(stg[:, 0:128], self.blockones_d, [], ["kvstg"])
        self.copy(blockones[:], stg[:, 0:128], ["kvstg"], ["blockones"])
        self.dma(stg[0:32, :], self.expand_d, [], ["kvstg"])
        self.copy(Expand[:], stg[0:32, :], ["kvstg"], ["Expand"])
        self.dma(addc[:].rearrange("p a b -> p (a b)"), self.addc_d, [], ["addc"])
        P.op("dve", lambda e: e.memset(zrow[:], 0.0), [], ["zrow"])
        P.op("dve", lambda e: e.memset(tiny[:], 1e-30), [], ["tiny"])
        self.dma(stg[:, 0:128], self.d4mask_d, ["kvstg"], ["kvstg"])
        for hh in range(4):
            self.copy(PatD4[:, hh, :], stg[:, 0:128], ["kvstg"], ["PatD4"])
        for j in (1, 2):
            for half in range(2):
                self.dma(gk[half * 64:(half + 1) * 64, j:j + 1], self.g_knorm[j:j + 1, :].rearrange("o d -> d o"), [], ["gk"],
                         allow_slow_non_contiguous=True)
        g0 = self.g_knorm[0:1, :]
        self.dma(gk0b[:], bass.AP(g0.tensor, g0.offset, [[0, 128], [1, 64]]), [], ["gk0b"])
        self.dma(stg[0:127, 0:32], self.overlap_d, ["kvstg"], ["kvstg"])
        for g in range(4):
            self.copy(vcx[0:127, g, 65:97], stg[0:127, 0:32], ["kvstg"], ["vcx"])
        P.op("dve", lambda e: e.memset(vcx[:, :, 64:65], 1.0), ["vcx"], ["vcx"])
        self.dma(tab33[0:32, :], self.rel_table, [], ["tab33"])
        P.op("dve", lambda e: e.memset(tab33[32:33, :], 1.0), [], ["tab33"])
        for ch in range(NF // 512):
            self.dma(ohs[:], self.onehot_d[:, ch * 512:(ch + 1) * 512], [], ["ohs"])
            self.mm(ps[0][0:16, :], tab33[:], ohs[:], True, True, ["tab33", "ohs"], [("ps", 0)])
            self.copy(Fsb[:, ch * 512:(ch + 1) * 512], ps[0][0:16, :], [("ps", 0)], ["Fsb"])
        self.dma(self.fvd, Fsb[:], ["Fsb"], ["fvd"])
        P.alias([("wpad", 0), ("wpad", 1), "c2T", "w1b"], ["Fsb"])
        for dl in range(2):
            base = 1920 - 128 * dl
            for h4 in range(4):
                src = bass.AP(self.fvd.tensor, base + h4 * 4 * NF, [[1, 128], [NF, 4], [1, 128]])
                self.dma(stg[:, h4 * 512:(h4 + 1) * 512].rearrange("p (h t) -> p h t", h=4), src, ["fvd", "kvstg"], ["kvstg"])
            self.copy(PatD[dl][:], self.rev_ap(stg, 128, 16, 128), ["kvstg"], [("PatD", dl)])
        stage = int(self.debug[2:3]) if (self.debug or "").startswith("kv") else 9
        if stage < 2:
            return
        self.norm_mod(None, None, gsc=lambda c: self.modkv[:, 16 + c:17 + c], shift=lambda c: self.modkv[:, c:c + 1])
        self.use_scratch(keys)
        if self.debug == "kv2a":
            return
        wkv = self.w_kv.rearrange("(c p) n -> p c n", p=128)
        hTk = lambda c, n: ("hT", c, n)
        for j, dst, key, gcol in ((2, ksT, "ksT", 1), (4, kwT, "kwT", 2)):
            for gp in range(2):
                self.dma(stg3[:, :, 0:128], wkv[:, :, j * 256 + gp * 128: j * 256 + (gp + 1) * 128], ["kvstg"], ["kvstg"])
                self.copy(wkb[:, :, 0:128], stg3[:, :, 0:128], ["kvstg"], ["wkb"], eng="pool")
                for n in range(NB):
                    tsl = slice(n * 512, (n + 1) * 512)
                    pb = n % 2
                    for c in range(KC):
                        self.mm(ps[pb][:, :], wkb[:, c, 0:128], hT[:, c, tsl], c == 0, c == KC - 1, ["wkb", hTk(c, n)],
                                [("ps", pb)])
                    self.qknorm(pb, dst[:, gp, tsl], gk[:, gcol:gcol + 1], sqk, kraw, rk_, 2 + pb, [(key, gp, n)], ["gk"], sfx="kv")
        if stage < 3:
            return
        for j, dst, key in ((3, vs, "vs"), (5, vw, "vw")):
            self.dma(stg3[:, :, 0:256], wkv[:, :, j * 256:(j + 1) * 256], ["kvstg"], ["kvstg"])
            self.copy(wkb[:], stg3[:, :, 0:256], ["kvstg"], ["wkb"], eng="pool")
            P.op("dve", lambda e, dst=dst: e.memset(dst[:, :, :, 64:65], 1.0), [], [key])
            for tp in range(NT // 2):
                pb = tp % 2
                for jj in range(2):
                    t = 2 * tp + jj
                    for c in range(KC):
                        self.mm(ps[pb][:, jj * 256:(jj + 1) * 256], hT[:, c, t * 128:(t + 1) * 128], wkb[:, c, :], c == 0,
                                c == KC - 1, ["wkb", hTk(c, t // 4)], [("ps", pb)])
                self.copy(dst[:, 2 * tp:2 * tp + 2, :, 0:64], ps[pb][:, :].rearrange("p (j g d) -> p j g d", j=2, g=4),
                          [("ps", pb)], [(key, tp)])
        if stage < 4:
            return
        for which in range(2):
            w1v = self.w_cmp1[which].rearrange("(a p) f -> p a f", p=128)
            for q4 in range(2):
                self.dma(stg[:].rearrange("p (a f) -> p a f", a=8), w1v[:, q4 * 8:(q4 + 1) * 8, :], ["kvstg"], ["kvstg"])
                self.copy(w1b[:, q4 * 8:(q4 + 1) * 8, :], stg[:].rearrange("p (a f) -> p a f", a=8), ["kvstg"], ["w1b"],
                          eng="pool")
            self.dma(stg[0:16, 0:128], self.pos_cmp[which].rearrange("(a two) d -> a (two d)", two=2), ["kvstg"], ["kvstg"])
            self.tr(ps[4][:, 16:32], stg[0:16, 0:128], self.ident_f[0:16, 0:16], ["kvstg", "ident_f"], [("ps", 4)])
            self.copy(hid[:, 0, 0:16], ps[4][:, 16:32], [("ps", 4)], ["hid"])
            for ft in range(2):
                for a in range(16):
                    self.mm(ps[4][:, ft:ft + 1], w1b[:, a, ft * 128:(ft + 1) * 128], hid[:, 0, a:a + 1], a == 0, a == 15,
                            ["w1b", "hid"], [("ps", 4)])
            self.copy(posb[:], ps[4][:, 0:2], [("ps", 4)], ["posb"])
            self.dma(stg[:, 0:128].rearrange("p (a d) -> p a d", a=2), self.w_cmp2[which].rearrange("(a p) d -> p a d", p=128),
                     ["kvstg"], ["kvstg"])
            self.copy(w2b[:], stg[:, 0:128].rearrange("p (a d) -> p a d", a=2), ["kvstg"], ["w2b"])
            for g in range(4):
                col = which * 256 + g * 64
                self.dma(stg3[:, :, 0:64], wkv[:, :, col:col + 64], ["kvstg"], ["kvstg"])
                P.op("pool", lambda e: e.memset(wpad[0][:], 0.0), [], [("wpad", 0)])
                P.op("pool", lambda e: e.memset(wpad[1][:], 0.0), [], [("wpad", 1)])
                self.copy(wpad[0][:, :, 0:64], stg3[:, :, 0:64], ["kvstg"], [("wpad", 0)], eng="pool")
                self.copy(wpad[1][:, :, 64:128], stg3[:, :, 0:64], ["kvstg"], [("wpad", 1)], eng="pool")
                for n in range(NB):
                    pb = n % 2
                    nn = 512 if n < NB - 1 else 511
                    for c in range(KC):
                        self.mm(ps[pb][:, :], wpad[0][:, c, :], hT[:, c, n * 512:(n + 1) * 512], c == 0, False,
                                [("wpad", 0), hTk(c, n)], [("ps", pb)])
                    for c in range(KC):
                        self.mm(ps[pb][:, 0:nn], wpad[1][:, c, :], hT[:, c, n * 512 + 1: n * 512 + 1 + nn], False,
                                c == KC - 1, [("wpad", 1), hTk(c, n), hTk(c, min(n + 1, NB - 1))], [("ps", pb)])
                    self.copy(c2T[:, n * 512:(n + 1) * 512], ps[pb][:, :], [("ps", pb)], ["c2T"])
                c2v = c2T[:].rearrange("p (n s) -> p s n", s=16)
                for ft in range(2):
                    pb = 2 + ft
                    for a in range(16):
                        rhs = bass.AP(c2T[:].tensor, c2T[:].offset + 2 * a, [[c2T[:].ap[0][0], 128], [16, 127]])
                        self.mm(ps[pb][:, 0:127], w1b[:, a, ft * 128:(ft + 1) * 128], rhs, a == 0, a == 15,
                                ["w1b", "c2T"], [("ps", pb)])
                    x_, x2, x3 = gx[0][:, 0:127], gx[1][:, 0:127], gx[2][:, 0:127]
                    self.act(x_, ps[pb][:, 0:127], AF.Identity, [("ps", pb), "posb"], ["gx"], bias=posb[:, ft:ft + 1])
                    self.tt(x2, x_, x_, ALU.mult, ["gx"], ["gx"])
                    self.ts(x2, x2, 0.044715, ALU.mult, ["gx"], ["gx"], s2=1.0, op1=ALU.add)
                    self.tt(x2, x2, x_, ALU.mult, ["gx"], ["gx"])
                    self.act(x3, x2, AF.Sigmoid, ["gx"], ["gx"], scale=2.0 * math.sqrt(2.0 / math.pi))
                    self.tt(hid[:, ft, 0:127], x_, x3, ALU.mult, ["gx"], ["hid"])
                for ft in range(2):
                    self.mm(ps[5][0:127, g * 64:(g + 1) * 64], hid[:, ft, 0:127], w2b[:, ft, :], ft == 0, ft == 1,
                            ["hid", "w2b"], [("ps", 5)])
                if which == 1:
                    self.copy(vcx[0:127, g, 0:64], ps[5][0:127, g * 64:(g + 1) * 64], [("ps", 5)], ["vcx"])
            if which == 0:
                self.copy(ctok[0:127, :, :], ps[5][0:127, 0:256].rearrange("p (g d) -> p g d", g=4), [("ps", 5)], ["ctok"])
                for g in range(4):
                    self.act(gx[0][0:127, 0:64], ctok[0:127, g, :], AF.Square, ["ctok"], ["gx", "csm"],
                             accum_out=csm[0:127, g:g + 1])
                self.act(csm[0:127, 4:8], csm[0:127, 0:4], AF.Sqrt, ["csm", "epsc"], ["csm"], bias=self.epsc[0:127, :],
                         scale=1.0 / 64)
                P.op("dve", lambda e: e.reciprocal(out=csm[0:127, 4:8], in_=csm[0:127, 4:8]), ["csm"], ["csm"])
                for g in range(4):
                    self.stt(ckn[0:127, g, :], ctok[0:127, g, :], csm[0:127, 4 + g:5 + g], gk0b[0:127, :], ALU.mult, ALU.mult,
                             ["ctok", "csm", "gk0b"], ["ckn"])
                ps6b = ps[6][:].bitcast(BF16)
                for gp in range(2):
                    self.tr(ps6b[:, gp * 128: gp * 128 + 127], ckn[0:127, 2 * gp:2 * gp + 2, :].rearrange("p g d -> p (g d)"),
                            self.ident_b[0:127, 0:127], ["ckn", "ident_b"], [("ps", 6)])
                    self.copy(kcT[:, gp, 0:127], ps6b[:, gp * 128: gp * 128 + 127], [("ps", 6)], ["kcT"])

    def qknorm(self, pin, dst, gvec, sqk, kraw, rk_, pbank, wkeys, rkeys, sfx=0):
        P, ps = self.P, self.ps
        psb = ps[pin]
        if self.debug == "kv2b":
            self.copy(dst, psb[:, :], [("ps", pin)], wkeys)
            return
        self.copy(kraw[:], psb[:, :], [("ps", pin)], [("kraw", sfx)])
        self.act(sqk[:], psb[:, :], AF.Square, [("ps", pin)], [("sqk", sfx)])
        self.mm(ps[pbank][:, :], self.blockones[:], sqk[:], True, True, ["blockones", ("sqk", sfx)], [("ps", pbank)])
        self.act(rk_[:], ps[pbank][:, :], AF.Sqrt, [("ps", pbank), "epsc"], [("rk_", sfx)], bias=self.epsc[:], scale=1.0 / 64)
        P.op("dve", lambda e: e.reciprocal(out=rk_[:], in_=rk_[:]), [("rk_", sfx)], [("rk_", sfx)])
        self.tt(kraw[:], kraw[:], rk_[:], ALU.mult, [("kraw", sfx), ("rk_", sfx)], [("kraw", sfx)])
        if self.debug == "kv2v1":
            self.ts(dst, kraw[:], gvec, ALU.mult, [("kraw", sfx)] + rkeys, wkeys)
        else:
            self.act(dst, kraw[:], AF.Copy, [("kraw", sfx)] + rkeys, wkeys, scale=gvec)


    def nsa(self, l):
        P, ps, xT, hT, nc = self.P, self.ps, self.xT, self.hT, self.nc
        j = l - 2
        o = self.scr0
        def take(name, shape, dt):
            nonlocal o
            t = self.sb(name, shape, dt, off=o)
            o += (int(np.prod(shape[1:])) * (4 if dt == F32 else 2) + 63) // 64 * 64
            assert o <= self.scr_end, (name, o)
            return t
        qT = take("nqT", [128, 8, S], BF16)
        gates = take("gates", [128, NT, 48], F32)
        bgb = take("bgb", [128, 48], F32)
        gq2 = take("gq2", [128, 2], F32)
        o_attn = o
        wst = take("nwst", [128, KC, 128], F32)
        wqb = [take(f"wqb{i}", [128, KC, 128], BF16) for i in range(2)]
        sqk = take("nsqk", [128, 512], BF16)
        kraw = take("nkraw", [128, 512], F32)
        rk_ = take("nrk", [128, 512], F32)
        wgs = take("wgs", [128, KC, 48], F32)
        wgb = take("wgb", [128, KC, 48], BF16)
        pkeys = ["nqT", "gates", "bgb", "gq2", "nwst", ("wqb", 0), ("wqb", 1), ("sqk", "q"), ("kraw", "q"), ("rk_", "q"), "wgs", "wgb"]
        self.use_scratch(pkeys)
        wq = self.w_b_q[j].rearrange("(c p) n -> p c n", p=128)
        hTk = lambda c, n: ("hT", c, n)
        for half in range(2):
            self.dma(gq2[half * 64:(half + 1) * 64, 0:1], self.g_qnorm[j:j + 1, :].rearrange("o d -> d o"), [], ["gq2"],
                     allow_slow_non_contiguous=True)
        self.ts(gq2[:, 1:2], gq2[:, 0:1], 0.125, ALU.mult, ["gq2"], ["gq2"])
        bg = self.b_b_gate[j:j + 1, :]
        self.dma(bgb[:], bass.AP(bg.tensor, bg.offset, [[0, 128], [1, 48]]), [], ["bgb"])
        for sl in range(8):
            gp, hh = sl // 4, sl % 4
            b = sl % 2
            for par in range(2):
                head = (2 * gp + par) * 4 + hh
                self.dma(wst[:, :, par * 64:(par + 1) * 64], wq[:, :, head * 64:(head + 1) * 64], [], ["nwst"])
            self.copy(wqb[b][:], wst[:], ["nwst"], [("wqb", b)], eng="pool")
            for n in range(NB):
                tsl = slice(n * 512, (n + 1) * 512)
                pb = n % 2
                for c in range(KC):
                    self.mm(ps[pb][:, :], wqb[b][:, c, :], hT[:, c, tsl], c == 0, c == KC - 1, [("wqb", b), hTk(c, n)],
                            [("ps", pb)])
                self.qknorm(pb, qT[:, sl, tsl], gq2[:, 1:2], sqk, kraw, rk_, 2 + pb, [("nqT", sl, n)], ["gq2"], sfx="q")
        self.dma(wgs[:], wq[:, :, 1024:1072], [], ["wgs"])
        self.copy(wgb[:], wgs[:], ["wgs"], ["wgb"])
        for t in range(NT):
            pb = 4 + t % 2
            for c in range(KC):
                self.mm(ps[pb][:, 0:48], hT[:, c, t * 128:(t + 1) * 128], wgb[:, c, :], c == 0, c == KC - 1,
                        ["wgb", hTk(c, t // 4)], [("ps", pb)])
            self.tt(gates[:, t, :], ps[pb][:, 0:48], bgb[:], ALU.add, [("ps", pb), "bgb"], [("gates", t)])
        self.act(gates[:], gates[:], AF.Sigmoid, [("gates", t) for t in range(NT)], [("gates", t) for t in range(NT)])
        o = o_attn
        cb = [take(f"cb{i}", [128, 4, 128], F32) for i in range(2)]
        ssum = take("ssum", [128, 512], F32)
        EcT = take("EcT", [128, 512], BF16)
        Et = [take(f"Et{i}", [128, 512], BF16) for i in range(3)]
        selT = [take(f"selT{i}", [32, 512], BF16) for i in range(2)]
        otok = [take(f"otok{i}", [128, D], BF16) for i in range(2)]
        acc = [take(f"acc{i}", [128, 4, 64], F32) for i in range(2)]
        ocs = [take(f"ocs{i}", [128, 4, 65], F32) for i in range(2)]
        zA = take("zA", [128, 8], F32)
        zB = take("zB", [128, 12], F32)
        cf = take("cf", [128, 12], F32)
        impt = take("impt", [128, 4, 32], F32)
        score = take("score", [128, 64], F32)
        top8 = take("top8", [128, 8], F32)
        akeys = [("cb", 0), ("cb", 1), "ssum", "EcT", ("Et", 0), ("Et", 1), ("Et", 2), ("selT", 0), ("selT", 1), ("otok", 0),
                 ("otok", 1), ("acc", 0), ("acc", 1), ("ocs", 0), ("ocs", 1), "zA", "zB", "cf", "impt", "score", "top8"]
        P.alias(akeys, ["nwst", ("wqb", 0), ("wqb", 1), ("sqk", "q"), ("kraw", "q"), ("rk_", "q"), "wgs", "wgb"])
        self._scratch_keys |= set(akeys)
        ksT, kwT, vs, vw, kcT, vcx = self.ksT, self.kwT, self.vs, self.vw, self.kcT, self.vcx
        ps6b = ps[6][:].bitcast(BF16)
        cnt = {"e": 0, "s": 0}

        def qview(qi, g):
            gp, par = g // 2, g % 2
            hs = slice(par * 64, (par + 1) * 64)
            qv = qT[hs, gp * 4:(gp + 1) * 4, qi * 128:(qi + 1) * 128]
            qk = [("nqT", gp * 4 + hh, qi // 4) for hh in range(4)]
            return gp, hs, qv, qk

        def A_ops(qi, g):
            gp, hs, qv, qk = qview(qi, g)
            b = (qi * 4 + g) % 2
            base = 1951 - 128 * qi
            src = bass.AP(self.fvd.tensor, (g * 4) * NF + base, [[16, 127], [NF, 4], [1, 128]])
            Oc = ps[3][:, 0:388].rearrange("p (h c) -> p h c", h=4)
            rzb = bass.AP(zA[:].tensor, zA[:, 4:8].offset, [[zA[:].ap[0][0], 128], [1, 4], [0, 32]])
            p7 = ps[6][0:32, 0:128]
            ops = []
            ops.append(lambda: self.dma(cb[b][0:127, :, :], src, ["fvd"], [("cb", b)]))
            ops.append(lambda: self.mm(ps[6][0:127, :], kcT[hs, gp, 0:127], qv, True, True, ["kcT"] + qk, [("ps", 6)]))
            ops.append(lambda: self.tt(ssum[0:127, :].rearrange("p (h t) -> p h t", h=4),
                                       ps[6][0:127, :].rearrange("p (h t) -> p h t", h=4), self.rev_ap(cb[b], 127, 4, 128),
                                       ALU.add, [("ps", 6), ("cb", b)], ["ssum"]))
            ops.append(lambda: self.act(EcT[0:127, :], ssum[0:127, :], AF.Exp, ["ssum"], ["EcT"]))

            def pvc():
                for hh in range(4):
                    self.mm(ps[3][:, hh * 97:(hh + 1) * 97], EcT[0:127, hh * 128:(hh + 1) * 128], vcx[0:127, g, :], True, True,
                            ["EcT", "vcx"], [("ps", 3)])
            ops.append(pvc)
            ops.append(lambda: self.copy(ocs[b][:], Oc[:, :, 0:65], [("ps", 3)], [("ocs", b)]))
            ops.append(lambda: self.ts(zA[:, 0:4], ocs[b][:, :, 64], self.tiny[:], ALU.max, [("ocs", b), "tiny"], ["zA"]))
            ops.append(lambda: P.op("dve", lambda e: e.reciprocal(out=zA[:, 4:8], in_=zA[:, 0:4]), ["zA"], ["zA"]))
            ops.append(lambda: self.tt(impt[:], Oc[:, :, 65:97], rzb, ALU.mult, [("ps", 3), "zA"], ["impt"]))
            ops.append(lambda: P.op("dve", lambda e: e.tensor_reduce(out=score[:, 0:32], in_=impt[:].rearrange("p h j -> p j h"),
                                                                     axis=AX.X, op=ALU.add), ["impt"], ["score"]))
            ops.append(lambda: self.tt(score[:, 0:32], score[:, 0:32], self.addc[:, qi, :], ALU.add, ["score", "addc"], ["score"]))
            ops.append(lambda: P.op("dve", lambda e: e.max(out=top8[:], in_=score[:, 0:32]), ["score"], ["top8"]))
            ops.append(lambda: self.ts(score[:, 32:64], score[:, 0:32], top8[:, 7:8], ALU.is_ge, ["score", "top8"], ["score"]))
            ops.append(lambda: self.ts(score[:, 32:64], score[:, 32:64], -1.0, ALU.add, ["score"], ["score"], s2=30000.0,
                                       op1=ALU.mult))
            ops.append(lambda: self.tr(ps[6][0:32, 0:128], score[:, 32:64], self.ident_f[:], ["score", "ident_f"], [("ps", 6)]))
            ops.append(lambda: self.copy(selT[b][:].rearrange("p (h t) -> p h t", h=4),
                                         bass.AP(p7.tensor, p7.offset, [[p7.ap[0][0], 32], [0, 4], [1, 128]]), [("ps", 6)],
                                         [("selT", b)]))
            return ops

        def B_emit(qi, g, a_next):
            gp, hs, qv, qk = qview(qi, g)
            b = (qi * 4 + g) % 2
            ob = qi % 2
            pO, pW = (4, 5) if g % 2 == 0 else (2, 7)
            a_next = list(a_next)
            self.mm(ps[pO][:, 0:260], self.zrow[0:1, 0:128], self.zrow[0:1, 0:260], True, False, ["zrow"], [("ps", pO)])
            self.mm(ps[pW][:, 0:260], self.zrow[0:1, 0:128], self.zrow[0:1, 0:260], True, False, ["zrow"], [("ps", pW)])
            units = [("s", kt) for kt in range(qi + 1)] + [("w", kt) for kt in range(max(0, qi - 4), qi + 1)]
            last = {"s": qi, "w": qi}

            def scores(u):
                br, kt = u
                dl = qi - kt
                pc = cnt["s"] % 2
                cnt["s"] += 1
                eb = cnt["e"] % 3
                cnt["e"] += 1
                if br == "s":
                    near = dl <= 1
                    self.mm(ps[pc][:, :], ksT[hs, gp, kt * 128:(kt + 1) * 128], qv, True, False,
                            [("ksT", gp, kt // 4)] + qk, [("ps", pc)])
                    self.mm(ps[pc][:, :], self.Expand[:, kt * 128:(kt + 1) * 128], selT[b][:], False, not near,
                            ["Expand", ("selT", b)], [("ps", pc)])
                    if near:
                        self.mm(ps[pc][:, :], self.ident_b[:], self.PatD[dl][:, g * 4:(g + 1) * 4, :], False, True,
                                ["ident_b", ("PatD", dl)], [("ps", pc)])
                else:
                    pat = dl in (0, 1, 4)
                    self.mm(ps[pc][:, :], kwT[hs, gp, kt * 128:(kt + 1) * 128], qv, True, not pat,
                            [("kwT", gp, kt // 4)] + qk, [("ps", pc)])
                    if dl <= 1:
                        self.mm(ps[pc][:, :], self.ident_b[:], self.PatD[dl][:, g * 4:(g + 1) * 4, :], False, True,
                                ["ident_b", ("PatD", dl)], [("ps", pc)])
                    elif dl == 4:
                        self.mm(ps[pc][:, :], self.ident_b[:], self.PatD4[:], False, True, ["ident_b", "PatD4"], [("ps", pc)])
                self.act(Et[eb][:], ps[pc][:, :], AF.Exp, [("ps", pc)], [("Et", eb)])
                return eb

            def pv(u, eb):
                br, kt = u
                po = pO if br == "s" else pW
                vv, vk = (vs, "vs") if br == "s" else (vw, "vw")
                for hh in range(4):
                    self.mm(ps[po][:, hh * 65:(hh + 1) * 65], Et[eb][:, hh * 128:(hh + 1) * 128], vv[:, kt, g, :], False,
                            kt == last[br] and hh == 3, [("Et", eb), vk, (vk, kt // 2)], [("ps", po)])

            nA = max(1, -(-len(a_next) // max(1, len(units))))
            prev = None
            for u in units:
                eb = scores(u)
                for _ in range(nA):
                    if a_next:
                        a_next.pop(0)()
                if prev is not None:
                    pv(*prev)
                prev = (u, eb)
            pv(*prev)
            while a_next:
                a_next.pop(0)()
            Os = ps[pO][:, 0:260].rearrange("p (h c) -> p h c", h=4)
            Ow = ps[pW][:, 0:260].rearrange("p (h c) -> p h c", h=4)
            self.ts(zB[:, 0:4], ocs[b][:, :, 64], self.tiny[:], ALU.max, [("ocs", b), "tiny"], ["zB"])
            self.ts(zB[:, 4:8], Os[:, :, 64], self.tiny[:], ALU.max, [("ps", pO), "tiny"], ["zB"])
            self.ts(zB[:, 8:12], Ow[:, :, 64], self.tiny[:], ALU.max, [("ps", pW), "tiny"], ["zB"])
            P.op("dve", lambda e: e.reciprocal(out=zB[:], in_=zB[:]), ["zB"], ["zB"])
            gv = gates[:, qi, g * 12:(g + 1) * 12].rearrange("p (h b) -> p b h", b=3)
            self.tt(cf[:].rearrange("p (b h) -> p b h", b=3), zB[:].rearrange("p (b h) -> p b h", b=3), gv, ALU.mult,
                    ["zB", ("gates", qi)], ["cf"])
            ab = g % 2
            for hh in range(4):
                self.ts(acc[ab][:, hh, :], ocs[b][:, hh, 0:64], cf[:, hh:hh + 1], ALU.mult, [("ocs", b), "cf"], [("acc", ab)])
                self.stt(acc[ab][:, hh, :], Os[:, hh, 0:64], cf[:, 4 + hh:5 + hh], acc[ab][:, hh, :], ALU.mult, ALU.add,
                         [("ps", pO), "cf", ("acc", ab)], [("acc", ab)])
                self.stt(otok[ob][:, (g * 4 + hh) * 64:(g * 4 + hh + 1) * 64], Ow[:, hh, 0:64], cf[:, 8 + hh:9 + hh],
                         acc[ab][:, hh, :], ALU.mult, ALU.add, [("ps", pW), "cf", ("acc", ab)], [("otok", ob)])

        order = [(qi, g) for qi in range(NT) for g in range(4)]
        for op_ in A_ops(*order[0]):
            op_()
        for i, (qi, g) in enumerate(order):
            nxt = A_ops(*order[i + 1]) if i + 1 < len(order) else []
            B_emit(qi, g, nxt)
            if g == 3:
                ob = qi % 2
                qsl = slice(qi * 128, (qi + 1) * 128)
                for c in range(KC):
                    self.tr(ps6b[:, c * 128:(c + 1) * 128], otok[ob][:, c * 128:(c + 1) * 128], self.ident_b[:],
                            [("otok", ob), "ident_b"], [("ps", 6)])
                self.copy(hT[:, :, qsl], ps6b[:, :].rearrange("p (c t) -> p c t", c=KC), [("ps", 6)],
                          [("hT", c, qi // 4) for c in range(KC)])
        wao = self.sb("nwao", [128, KC, D], BF16, off=self.scr0)
        wos = [self.sb(f"nwaos{i}", [128, D], F32, off=self.scr0 + 16384 + i * 4096) for i in range(2)]
        P.alias(["nwao", ("nwaos", 0), ("nwaos", 1)], ["nqT"] + [("nqT", sl, n) for sl in range(8) for n in range(NB)])
        self._scratch_keys |= {"nwao", ("nwaos", 0), ("nwaos", 1)} | set(("nqT", sl, n) for sl in range(8) for n in range(NB))
        for c in range(KC):
            b = c % 2
            self.dma(wos[b][:], self.w_b_out[j, c * 128:(c + 1) * 128, :], [], [("nwaos", b)])
            self.copy(wao[:, c, :], wos[b][:], [("nwaos", b)], ["nwao"], eng="pool")
        k = 0
        for dc in range(KC):
            for n in range(NB):
                tsl = slice(n * 512, (n + 1) * 512)
                pb = k % 2
                k += 1
                for c in range(KC):
                    self.mm(ps[pb][:, :], wao[:, c, dc * 128:(dc + 1) * 128], hT[:, c, tsl], c == 0, c == KC - 1,
                            ["nwao", ("hT", c, n)], [("ps", pb)])
                self.stt(xT[:, dc, tsl], ps[pb][:, :], self.mod[:, l, 16 + dc: 17 + dc], xT[:, dc, tsl], ALU.mult, ALU.add,
                         [("ps", pb), ("mod", l), ("xT", dc, n)], [("xT", dc, n)])

    def ffn(self, l, w_fi, w_fo):
        P, ps, xT, hT = self.P, self.ps, self.xT, self.hT
        scr0 = self.scr0
        groups = [(0, 6), (6, 6), (12, 5), (17, 5)]
        actT = self.sb("actT", [128, 6, S], BF16, off=scr0)
        wob = self.sb("wob", [128, 6, D], BF16, off=scr0 + 24576)
        wis = self.sb("wis", [128, KC, 256], F32, off=scr0 + 36864)
        wib = [self.sb(f"wib{i}", [128, KC, 256], BF16, off=scr0 + 45056 + i * 4096) for i in range(2)]
        wos = self.sb("wos", [128, D], F32, off=scr0 + 53248)
        sg = [self.sb(f"sg{i}", [128, 512], BF16, off=scr0 + 57344 + i * 1024) for i in range(2)]
        assert scr0 + 59392 <= self.scr_end
        keys = [("actT", j) for j in range(6)] + [("wob", j) for j in range(6)] + ["wis", ("wib", 0), ("wib", 1), "wos",
                                                                                 ("sg", 0), ("sg", 1)]
        self.use_scratch(keys)
        wiv = w_fi[l].rearrange("(c p) n -> p c n", p=128)
        ga = lambda c: self.mod[:, l, 40 + c: 41 + c]
        it = 0
        for (j0, nj) in groups:
            for jj in range(nj):
                j = j0 + jj
                wb = it % 2
                it += 1
                self.dma(wis[:, :, 0:128], wiv[:, :, j * 128:(j + 1) * 128], [], ["wis"])
                self.dma(wis[:, :, 128:256], wiv[:, :, DFF + j * 128: DFF + (j + 1) * 128], [], ["wis"])
                self.copy(wib[wb][:], wis[:], ["wis"], [("wib", wb)], eng="pool")
                for n in range(NB):
                    tsl = slice(n * 512, (n + 1) * 512)
                    pg, pu = (n % 2) * 2, (n % 2) * 2 + 1
                    for part, pb in ((0, pg), (1, pu)):
                        for c in range(KC):
                            self.mm(ps[pb][:, :], wib[wb][:, c, part * 128:(part + 1) * 128], hT[:, c, tsl],
                                    c == 0, c == KC - 1, [("wib", wb), ("hT", c, n)], [("ps", pb)])
                    sb_ = n % 2
                    self.act(sg[sb_][:], ps[pg][:, :], AF.Silu, [("ps", pg)], [("sg", sb_)])
                    self.tt(actT[:, jj, tsl], sg[sb_][:], ps[pu][:, :], ALU.mult,
                            [("sg", sb_), ("ps", pu)], [("actT", jj)])
                self.dma(wos[:], w_fo[l, j * 128:(j + 1) * 128, :], [], ["wos"])
                self.copy(wob[:, jj, :], wos[:], ["wos"], [("wob", jj)], eng="pool")
            k = 0
            for dc in range(KC):
                for n in range(NB):
                    tsl = slice(n * 512, (n + 1) * 512)
                    pb = 4 + k % 2
                    k += 1
                    for jj in range(nj):
                        self.mm(ps[pb][:, :], wob[:, jj, dc * 128:(dc + 1) * 128], actT[:, jj, tsl],
                                jj == 0, jj == nj - 1, [("wob", jj), ("actT", jj)], [("ps", pb)])
                    self.stt(xT[:, dc, tsl], ps[pb][:, :], ga(dc), xT[:, dc, tsl], ALU.mult, ALU.add,
                             [("ps", pb), ("mod", l), ("xT", dc, n)], [("xT", dc, n)])


_CACHE = {}


def _get_prog(key, **kw):
    if key not in _CACHE:
        b = B(**kw)
        b.build()
        _CACHE[key] = b
    return _CACHE[key]


def kernel(**inputs):
    b = _get_prog("full")
    hc = host_consts()
    in_maps = []
    for core in range(8):
        m = {}
        for name in b.din:
            shp = tuple(b.din[name].shape)
            if name == "x":
                m[name] = np.ascontiguousarray(inputs["x"][core])
            elif name == "c":
                m[name] = np.ascontiguousarray(inputs["c"][core:core + 1])
            elif name in hc:
                m[name] = hc[name]
            else:
                m[name] = np.ascontiguousarray(np.asarray(inputs[name], dtype=np.float32)).reshape(shp)
        in_maps.append(m)
    res = run_bass_kernel_spmd(b.nc, in_maps, core_ids=list(range(8)))
    return np.stack([r["out"] for r in res.results], axis=0)
```

```python
import math
import numpy as np
import ml_dtypes
import concourse.bass as bass
import concourse.mybir as mybir
from concourse.bass_utils import run_bass_kernel_spmd

F32 = mybir.dt.float32
BF16 = mybir.dt.bfloat16
AF = mybir.ActivationFunctionType
ALU = mybir.AluOpType
AX = mybir.AxisListType

D = 1024
S = 2048
DEPTH = 4
DFF = 2816
NT = S // 128
NB = S // 512
KC = D // 128

EPOCH = 4000
COMPUTE = ("pe", "act", "dve", "pool")
NDMA = {"sp": 20, "pool": 8, "act": 8}


class Prog:
    def __init__(self, nc, same_engine_sync=True):
        self.nc = nc
        self.same_sync = same_engine_sync
        self.ops = {e: [] for e in ("pe", "act", "dve", "pool", "sp")}
        self.cnt = {e: 0 for e in COMPUTE}
        self.clock = {e: {} for e in self.ops}
        self.iclock = {e: [] for e in COMPUTE}
        self.sems = {e: [] for e in COMPUTE}
        self.dma_sems = {q: [nc.alloc_semaphore(f"dq_{q}_{i}") for i in range(n)] for q, n in NDMA.items()}
        self.dma_n = {q: 0 for q in NDMA}
        self.dma_done = {e: set() for e in self.ops}
        self.dma_info = {}
        self.lastw = {}
        self.reads = {}
        self.nwaits = 0

    def _sem(self, e, n):
        ep = n // EPOCH
        while len(self.sems[e]) <= ep:
            self.sems[e].append(self.nc.alloc_semaphore(f"s_{e}_{len(self.sems[e])}"))
        return self.sems[e][ep], n % EPOCH + 1

    def _collect(self, eng, reads, writes):
        deps = set()
        for k in reads:
            w = self.lastw.get(k)
            if w is not None:
                deps.add(w)
        for k in writes:
            w = self.lastw.get(k)
            if w is not None:
                deps.add(w)
            for r in self.reads.get(k, ()):
                deps.add(r)
        waits = []
        ck = self.clock[eng]
        import os as _os
        _rev = _os.environ.get("WREV", "0") == "1"
        for d in sorted(deps, key=lambda d: (str(d[0]), str(d[1])), reverse=_rev):
            if d[0] == "dma":
                if d[1] in self.dma_done[eng]:
                    continue
                self.dma_done[eng].add(d[1])
                waits.append(self.dma_info[d[1]])
            else:
                src, n = d
                if src == eng and (src == "pe" or not self.same_sync):
                    continue
                if ck.get(src, 0) >= n + 1:
                    continue
                waits.append(self._sem(src, n))
                for s2, c2 in self.iclock[src][n].items():
                    if ck.get(s2, 0) < c2:
                        ck[s2] = c2
                if ck.get(src, 0) < n + 1:
                    ck[src] = n + 1
        self.nwaits += len(waits)
        return waits

    def _record(self, tag, reads, writes):
        for k in writes:
            self.lastw[k] = tag
            self.reads[k] = []
        for k in reads:
            if k not in writes:
                self.reads.setdefault(k, []).append(tag)

    def op(self, eng, fn, reads=(), writes=()):
        reads = list(reads); writes = list(writes)
        ex = [k for k in reads if isinstance(k, tuple) and k[0] == "ps" and k not in writes]
        if ex:
            reads = [k for k in reads if k not in ex]
            writes = writes + ex
        waits = self._collect(eng, reads, writes)
        n = self.cnt[eng]
        self.cnt[eng] += 1
        sem, val = self._sem(eng, n)
        self.iclock[eng].append(dict(self.clock[eng]))
        self.ops[eng].append((waits, fn, (sem, 1)))
        self._record((eng, n), reads, writes)

    def dma(self, q, fn, reads=(), writes=()):
        reads = list(reads); writes = list(writes)
        waits = self._collect(q, reads, writes)
        j = self.dma_n[q]
        self.dma_n[q] += 1
        M = len(self.dma_sems[q])
        sem = self.dma_sems[q][j % M]
        target = 16 * (j // M + 1)
        if j >= M:
            prev = (q, j - M)
            if prev not in self.dma_done[q]:
                self.dma_done[q].add(prev)
                waits.append(self.dma_info[prev])
        did = (q, j)
        self.dma_info[did] = (sem, target)
        if q in COMPUTE:
            pass
        self.ops[q].append((waits, fn, (sem, 16)))
        self._record(("dma", did), reads, writes)
        return did

    def alias(self, new_keys, old_keys):
        tags = []
        for k in old_keys:
            w = self.lastw.get(k)
            if w is not None:
                tags.append(w)
            tags.extend(self.reads.get(k, ()))
        tags = list(dict.fromkeys(tags))
        for k in new_keys:
            self.lastw[k] = None
            self.reads[k] = list(tags)

    def finish(self, final_keys):
        nc = self.nc
        waits = self._collect("sp", list(final_keys), [])
        self.ops["sp"].append((waits, None, None))
        emap = {"pe": "tensor", "act": "scalar", "dve": "vector", "pool": "gpsimd", "sp": "sync"}
        with nc.Block() as block:
            for e, attr in emap.items():
                lst = self.ops[e]

                def body(engobj, lst=lst):
                    for waits, fn, inc in lst:
                        for s, v in waits:
                            engobj.wait_ge(s, v)
                        if fn is not None:
                            ins = fn(engobj)
                            ins.then_inc(inc[0], inc[1])
                getattr(block, attr)(body)


def t5_bucket_np(dist):
    n = np.maximum(dist, 0)
    nf = np.maximum(n, 1).astype(np.float32)
    large = 16 + (np.log(nf / np.float32(16)) / np.float32(math.log(8.0)) * np.float32(16)).astype(np.int32)
    large = np.minimum(large, 31)
    return np.where(n < 16, n, large)


NF = 4096
FOFF = 2048


def host_consts():
    c = {}
    c["ident_f"] = np.eye(128, dtype=np.float32)
    dist = NF - 1 - np.arange(NF) - FOFF
    bk = t5_bucket_np(dist)
    oh = np.zeros((33, NF), np.float32)
    oh[bk, np.arange(NF)] = 1.0
    oh[31, :] -= 1.0
    oh[:32, dist < 0] = 0.0
    oh[32, :] = np.where(dist < 0, -30000.0, 0.0)
    c["onehot"] = oh
    c["causalT"] = np.triu(np.ones((64, 64), np.float32))
    sm = np.ones((4, S), np.float32); sm[:, ::64] = 0.0
    c["scanmask"] = sm
    hm = np.zeros((4, 4, 32), np.float32)
    for h in range(4):
        hm[h, h, :] = 1.0
    c["hmask"] = hm.reshape(4, 128)
    tl = np.arange(128)
    c["d4mask"] = np.where(tl[None, :] < tl[:, None], 0.0, -30000.0).astype(np.float32)
    ex = np.zeros((32, S), np.float32)
    ex[np.arange(S) // 64, np.arange(S)] = 1.0
    c["expand"] = ex
    ac = np.zeros((128, 16, 32), np.float32)
    for qi in range(16):
        for t in range(128):
            qb = (qi * 128 + t) // 64
            for jb in range(32):
                if jb > qb:
                    ac[t, qi, jb] = -1e30
                elif jb == 0 or jb == qb or jb == qb - 1:
                    ac[t, qi, jb] = 1e4
    c["addc"] = ac.reshape(128, 512)
    start = np.arange(127) * 16
    sj = np.arange(32) * 64
    ov = np.minimum(start[:, None] + 32, sj[None, :] + 64) - np.maximum(start[:, None], sj[None, :])
    c["overlap"] = (np.clip(ov, 0, None) / 32).astype(np.float32)
    bo = np.zeros((128, 128), np.float32); bo[:64, :64] = 1; bo[64:, 64:] = 1
    c["blockones"] = bo
    return c


class B:
    def __init__(self, nlayers=DEPTH, mixers=True, debug=None):
        self.nlayers = nlayers
        self.mixers = mixers
        self.debug = debug
        nc = self.nc = bass.Bass("TRN2", target_bir_lowering=False)
        self.P = Prog(nc)
        self.din = {}
        self.sb_off = 16512
        self.sb_end = 229344
        self.scr_end = 229344
        self.uid = 0
        self.psn = 0

    def dram_in(self, name, shape, dt=F32):
        t = self.nc.dram_tensor(name, list(shape), dt, kind="ExternalInput")
        self.din[name] = t
        return t.ap()

    def sb(self, name, shape, dt, off=None):
        nbytes = int(np.prod(shape[1:])) * (4 if dt == F32 else 2)
        if off is None:
            off = self.sb_off
            self.sb_off += (nbytes + 63) // 64 * 64
            assert self.sb_off <= self.sb_end, (name, self.sb_off)
        self.uid += 1
        return self.nc.alloc_sbuf_tensor_at(f"{name}_{self.uid}", list(shape), dt, offset=off)

    def mm(self, out, lhsT, rhs, start, stop, reads, writes):
        self.P.op("pe", lambda e: e.matmul(out, lhsT, rhs, start=start, stop=stop), reads, writes)

    def tr(self, out, in_, ident, reads, writes):
        self.P.op("pe", lambda e: e.transpose(out, in_, ident), reads, writes)

    def act(self, out, in_, func, reads, writes, bias=None, scale=None, accum_out=None):
        kw = {}
        if bias is not None:
            kw["bias"] = bias
        if scale is not None:
            kw["scale"] = scale
        if accum_out is not None:
            kw["accum_out"] = accum_out
        self.P.op("act", lambda e: e.activation(out=out, in_=in_, func=func, **kw), reads, writes)

    def tt(self, out, in0, in1, op, reads, writes, eng="dve"):
        self.P.op(eng, lambda e: e.tensor_tensor(out=out, in0=in0, in1=in1, op=op), reads, writes)

    def ts(self, out, in0, s1, op0, reads, writes, s2=None, op1=None, eng="dve", accum_out=None):
        kw = {}
        if op1 is not None:
            kw["op1"] = op1
        if accum_out is not None:
            kw["accum_out"] = accum_out
        self.P.op(eng, lambda e: e.tensor_scalar(out=out, in0=in0, scalar1=s1, scalar2=s2, op0=op0, **kw), reads, writes)

    def stt(self, out, in0, scalar, in1, op0, op1, reads, writes):
        self.P.op("dve", lambda e: e.scalar_tensor_tensor(out=out, in0=in0, scalar=scalar, in1=in1, op0=op0, op1=op1), reads, writes)

    def copy(self, out, in_, reads, writes, eng="dve"):
        self.P.op(eng, lambda e: e.tensor_copy(out=out, in_=in_), reads, writes)

    def dma(self, out, in_, reads, writes, q="sp", **kw):
        self.P.dma(q, lambda e: e.dma_start(out=out, in_=in_, **kw), reads, writes)

    def build(self):
        nc, P = self.nc, self.P
        L = self.nlayers
        x_d = self.dram_in("x", [S, D])
        vec_d = {}
        w_ada = self.dram_in("w_ada", [DEPTH, D, 6 * D])
        b_ada = self.dram_in("b_ada", [DEPTH, 6 * D])
        g_mix = self.dram_in("g_norm_mix", [DEPTH, D])
        g_ffn = self.dram_in("g_norm_ffn", [DEPTH, D])
        w_fi = self.dram_in("w_ffn_in", [DEPTH, D, 2 * DFF])
        w_fo = self.dram_in("w_ffn_out", [DEPTH, DFF, D])
        c_d = self.dram_in("c", [1, D])
        ident_d = self.dram_in("ident_f", [128, 128])
        self.w_a_in = self.dram_in("w_a_in", [2, D, 3080])
        self.b_a_if = self.dram_in("b_a_if", [2, 8])
        g_a_out = self.dram_in("g_a_out", [2, D])
        self.w_a_out = self.dram_in("w_a_out", [2, D, D])
        self.causalT_d = self.dram_in("causalT", [64, 64])
        self.w_kv_ada = self.dram_in("w_kv_ada", [D, 2 * D])
        b_kv_ada = self.dram_in("b_kv_ada", [1, 2 * D])
        g_kv_norm = self.dram_in("g_kv_norm", [1, D])
        self.w_kv = self.dram_in("w_kv", [D, 1536])
        self.pos_cmp = [self.dram_in("pos_cmp_k", [32, 64]), self.dram_in("pos_cmp_v", [32, 64])]
        self.w_cmp1 = [self.dram_in("w_cmp_k1", [2048, 256]), self.dram_in("w_cmp_v1", [2048, 256])]
        self.w_cmp2 = [self.dram_in("w_cmp_k2", [256, 64]), self.dram_in("w_cmp_v2", [256, 64])]
        self.g_knorm = self.dram_in("g_knorm", [3, 64])
        self.w_b_q = self.dram_in("w_b_q", [2, D, 1072])
        self.b_b_gate = self.dram_in("b_b_gate", [2, 48])
        self.g_qnorm = self.dram_in("g_qnorm", [2, 64])
        self.w_b_out = self.dram_in("w_b_out", [2, D, D])
        self.rel_table = self.dram_in("rel_table", [32, 16])
        self.onehot_d = self.dram_in("onehot", [33, NF])
        self.d4mask_d = self.dram_in("d4mask", [128, 128])
        self.expand_d = self.dram_in("expand", [32, S])
        self.addc_d = self.dram_in("addc", [128, 512])
        self.overlap_d = self.dram_in("overlap", [127, 32])
        self.blockones_d = self.dram_in("blockones", [128, 128])
        self.fvd = nc.dram_tensor("fvd", [16, NF], F32).ap()
        self.scanmask_d = self.dram_in("scanmask", [4, S])
        self.hmask_d = self.dram_in("hmask", [4, 128])
        out_d = nc.dram_tensor("out", [S, D], F32, kind="ExternalOutput").ap()

        xT = self.xT = self.sb("xT", [128, KC, S], F32)
        hT = self.hT = self.sb("hT", [128, KC, S], BF16)
        ident_f = self.ident_f = self.sb("ident_f", [128, 128], F32)
        ident_b = self.ident_b = self.sb("ident_b", [128, 128], BF16)
        ones_b = self.ones_b = self.sb("ones_b", [128, 128], BF16)
        NV = 304
        vecT = self.vecT = self.sb("vecT", [128, NV], F32)
        cact = self.cact = self.sb("cact", [128, KC], F32)
        mod = self.mod = self.sb("mod", [128, DEPTH, 48], F32)
        modkv = self.modkv = self.sb("modkv", [128, 24], F32)
        gs = self.gs = self.sb("gs", [128, DEPTH, 2, KC], F32)
        epsc = self.epsc = self.sb("epsc", [128, 1], F32)
        self.persist_end = self.sb_off
        scr0 = self.sb_off
        ps = self.ps = [nc.alloc_psum_tensor(f"ps{i}", [128, 512], F32) for i in range(8)]

        self.dma(ident_f[:], ident_d, [], ["ident_f"])
        self.copy(ident_b[:], ident_f[:], ["ident_f"], ["ident_b"])
        P.op("dve", lambda e: e.memset(ones_b[:], 1.0), [], ["ones_b"])
        P.op("dve", lambda e: e.memset(epsc[:], 1e-6), [], ["epsc"])

        vrows = [self.sb(f"vrows{i}", [128, 128], F32, off=scr0 + i * 512) for i in range(3)]
        srcs = [(b_ada.rearrange("l (m p) -> (l m) p", p=128), 0, 192),
                (g_mix.rearrange("l (m p) -> (l m) p", p=128), 192, 32),
                (g_ffn.rearrange("l (m p) -> (l m) p", p=128), 224, 32),
                (g_kv_norm.rearrange("o (m p) -> (o m) p", p=128), 256, 8),
                (b_kv_ada.rearrange("o (m p) -> (o m) p", p=128), 264, 16),
                (g_a_out.rearrange("l (m p) -> (l m) p", p=128), 280, 16),
                (c_d.rearrange("o (m p) -> (o m) p", p=128), 296, 8)]
        for i in range(3):
            P.op("dve", lambda e, i=i: e.memset(vrows[i][:], 0.0), [], [("vrows", i)])
        for ap, r0, n in srcs:
            r = r0
            while r < r0 + n:
                ti = r // 128
                cnt = min(r0 + n - r, (ti + 1) * 128 - r)
                self.dma(vrows[ti][r - ti * 128: r - ti * 128 + cnt, :], ap[r - r0: r - r0 + cnt, :],
                         [], [("vrows", ti)])
                r += cnt
        for i in range(3):
            n = min(128, NV - i * 128)
            self.tr(ps[0][:, i * 128: i * 128 + n], vrows[i][0:n, :], ident_f[0:n, 0:n],
                    [("vrows", i), "ident_f"], [("ps", 0)])
        self.copy(vecT[:], ps[0][:, 0:NV], [("ps", 0)], ["vecT"])
        self.act(cact[:], vecT[:, 296:304], AF.Silu, ["vecT"], ["cact"])

        xin = [self.sb(f"xin{i}", [128, D], F32, off=scr0 + 2048 + i * 4096) for i in range(2)]
        for t in range(NT):
            b = t % 2
            self.dma(xin[b][:], x_d[t * 128:(t + 1) * 128, :], [], [("xin", b)])
            for half in range(2):
                pb = 1 + (2 * t + half) % 4
                for cc in range(4):
                    c = half * 4 + cc
                    self.tr(ps[pb][:, cc * 128:(cc + 1) * 128], xin[b][:, c * 128:(c + 1) * 128], ident_f[:],
                            [("xin", b), "ident_f"], [("ps", pb)])
                o = xT[:, half * 4:(half + 1) * 4, t * 128:(t + 1) * 128]
                i_ = ps[pb][:, :].rearrange("p (c t) -> p c t", c=4)
                wk = [("xT", c_, t // 4) for c_ in range(half * 4, half * 4 + 4)]
                if half == 0:
                    self.copy(o, i_, [("ps", pb)], wk)
                else:
                    self.act(o, i_, AF.Identity, [("ps", pb)], wk)

        wst = [self.sb(f"wada{i}", [128, KC, 512], F32, off=scr0 + 12288 + i * 16384) for i in range(2)]
        nch = 0
        for l in range(L):
            wv = w_ada[l].rearrange("(c p) n -> p c n", p=128)
            for g in range(12):
                b = nch % 2
                nch += 1
                self.dma(wst[b][:], wv[:, :, g * 512:(g + 1) * 512], [], [("wada", b)])
                for mm_ in range(4):
                    m = g * 4 + mm_
                    for kc in range(KC):
                        self.mm(ps[5][:, m:m + 1], wst[b][:, kc, mm_ * 128:(mm_ + 1) * 128], cact[:, kc:kc + 1],
                                kc == 0, kc == KC - 1, [("wada", b), "cact"], [("ps", 5)])
            self.tt(mod[:, l, :], ps[5][:, 0:48], vecT[:, l * 48:(l + 1) * 48], ALU.add,
                    [("ps", 5), "vecT"], [("mod", l)])
            for which, (gbase, scoff) in enumerate(((192, 8), (224, 32))):
                self.stt(gs[:, l, which, :], mod[:, l, scoff:scoff + 8], 1.0,
                         vecT[:, gbase + l * 8: gbase + l * 8 + 8], ALU.add, ALU.mult,
                         [("mod", l), "vecT"], [("gs", l)])

        if L > 2:
            wv = self.w_kv_ada.rearrange("(c p) n -> p c n", p=128)
            for g in range(4):
                b = nch % 2
                nch += 1
                self.dma(wst[b][:], wv[:, :, g * 512:(g + 1) * 512], [], [("wada", b)])
                for mm_ in range(4):
                    m = g * 4 + mm_
                    for kc in range(KC):
                        self.mm(ps[5][:, m:m + 1], wst[b][:, kc, mm_ * 128:(mm_ + 1) * 128], cact[:, kc:kc + 1],
                                kc == 0, kc == KC - 1, [("wada", b), "cact"], [("ps", 5)])
            self.tt(modkv[:, 0:16], ps[5][:, 0:16], vecT[:, 264:280], ALU.add, [("ps", 5), "vecT"], ["kvmod"])
            self.stt(modkv[:, 16:24], modkv[:, 8:16], 1.0, vecT[:, 256:264], ALU.add, ALU.mult,
                     ["kvmod", "vecT"], ["kvmod"])
        self.scr0 = scr0
        self._scratch_keys = set([("vrows", i) for i in range(3)] + [("xin", 0), ("xin", 1), ("wada", 0), ("wada", 1)])
        for l in range(L):
            if self.mixers:
                if l == 2:
                    self.nsa_kv()
                self.norm_mod(l, 0)
                if l < 2:
                    if not (self.debug or "").startswith("kv"):
                        self.mlstm(l)
                elif self.debug != "nomix" and not (self.debug or "").startswith("kv"):
                    self.nsa(l)
            if not (self.debug or "").startswith("kv"):
                self.norm_mod(l, 1)
                self.ffn(l, w_fi, w_fo)

        xo = [self.sb(f"xo{i}", [128, D], F32, off=scr0 + i * 4096) for i in range(2)]
        P.alias([("xo", 0), ("xo", 1)], self.scratch_keys())
        for t in range(NT):
            b = t % 2
            for half in range(2):
                pb = (2 * t + half) % 4
                for cc in range(4):
                    c = half * 4 + cc
                    self.tr(ps[pb][:, cc * 128:(cc + 1) * 128], xT[:, c, t * 128:(t + 1) * 128], ident_f[:],
                            [("xT", c, t // 4), "ident_f"], [("ps", pb)])
                o = xo[b][:, half * 512:(half + 1) * 512]
                if half == 0:
                    self.copy(o, ps[pb][:, :], [("ps", pb)], [("xo", b)])
                else:
                    self.act(o, ps[pb][:, :], AF.Identity, [("ps", pb)], [("xo", b)])
            self.dma(out_d[t * 128:(t + 1) * 128, :], xo[b][:], [("xo", b)], [("out", t)])
        P.finish([("out", t) for t in range(NT)])
        return nc

    def scratch_keys(self):
        return list(self._scratch_keys)


    def use_scratch(self, new_keys):
        self.P.alias(new_keys, list(self._scratch_keys))
        self._scratch_keys = set(new_keys)

    def norm_mod(self, l, which, gsc=None, shift=None):
        P, ps, xT, hT = self.P, self.ps, self.xT, self.hT
        if gsc is None:
            gsc = lambda c: self.gs[:, l, which, c:c + 1]
            sho = 0 if which == 0 else 24
            shift = lambda c: self.mod[:, l, sho + c: sho + c + 1]
            rk = [("gs", l), ("mod", l)]
        else:
            rk = ["kvmod"]
        scr0 = self.scr0
        sq = [self.sb(f"sq{i}", [128, 512], BF16, off=scr0 + i * 1024) for i in range(2)]
        rstd = [self.sb(f"rstd{i}", [128, 512], F32, off=scr0 + 2048 + i * 2048) for i in range(2)]
        tmp = [self.sb(f"ntmp{i}", [128, 512], F32, off=scr0 + 6144 + i * 2048) for i in range(2)]
        keys = [("sq", 0), ("sq", 1), ("rstd", 0), ("rstd", 1), ("ntmp", 0), ("ntmp", 1)]
        self.use_scratch(keys)
        k = 0
        for n in range(NB):
            tsl = slice(n * 512, (n + 1) * 512)
            pb = 6 + n % 2
            for c in range(KC):
                b = k % 2
                k += 1
                self.act(sq[b][:], xT[:, c, tsl], AF.Square, [("xT", c, n)], [("sq", b)])
                self.mm(ps[pb][:, :], self.ones_b[:], sq[b][:], c == 0, c == KC - 1,
                        [("sq", b), "ones_b"], [("ps", pb)])
            rb = n % 2
            self.act(rstd[rb][:], ps[pb][:, :], AF.Sqrt, [("ps", pb), "epsc"], [("rstd", rb)],
                     bias=self.epsc[:], scale=1.0 / D)
            P.op("dve", lambda e, rb=rb: e.reciprocal(out=rstd[rb][:], in_=rstd[rb][:]), [("rstd", rb)], [("rstd", rb)])
            for c in range(KC):
                b = c % 2
                self.tt(tmp[b][:], xT[:, c, tsl], rstd[rb][:], ALU.mult, [("xT", c, n), ("rstd", rb)], [("ntmp", b)])
                self.act(hT[:, c, tsl], tmp[b][:], AF.Identity, [("ntmp", b)] + rk, [("hT", c, n)],
                         bias=shift(c), scale=gsc(c))


    def bcast(self, t, nparts, pre, n, post):
        a = t[:]
        ps_ = a.ap[0][0]
        dims = [[ps_, nparts]]
        if pre > 1:
            dims.append([0, pre])
        dims.append([1, n])
        if post > 1:
            dims.append([0, post])
        return bass.AP(a.tensor, a.offset, dims)

    def mlstm(self, l):
        P, ps, xT, hT, nc = self.P, self.ps, self.xT, self.hT, self.nc
        o = self.scr0
        def take(name, shape, dt):
            nonlocal o
            t = self.sb(name, shape, dt, off=o)
            o += (int(np.prod(shape[1:])) * (4 if dt == F32 else 2) + 63) // 64 * 64
            assert o <= self.sb_end, (name, o)
            return t
        yT = take("yT", [128, KC, S], BF16)
        qT = take("qT", [128, S], BF16)
        kT = take("kT", [128, S], BF16)
        vtok = take("vtok", [64, 32, 257], BF16)
        soT = take("soT", [128, 2, S], BF16)
        wh = take("wh", [128, KC, 768], BF16)
        wst = [take(f"wst{i}", [128, KC, 128], F32) for i in range(2)]
        vo = self._off_of(vtok)
        GA = self.sb("GA", [4, S], F32, off=vo)
        GB = self.sb("GB", [4, S], F32, off=vo + 8192)
        GC = self.sb("GC", [4, S], F32, off=vo + 16384)
        small = take("msmall", [4, 8, 32], F32)
        Rm = take("Rm", [4, 128], F32)
        hmask = take("hmask", [4, 128], F32)
        ones4 = take("ones4", [4, 128], F32)
        scanmask = GC
        bif = take("bif", [4, 2], F32)
        wg_s = take("wg_s", [128, KC, 8], F32)
        wg = take("wg", [128, KC, 8], BF16)
        Etok = take("Etok", [64, 32, 4], F32)
        DNtok = take("DNtok", [64, 32, 4], F32)
        a_bc = take("a_bc", [128, 128], F32)
        causalT = take("causalT", [64, 64], F32)
        Cst = take("Cst", [128, 257], F32)
        Cab = [take(f"Cab{i}", [128, 257], BF16) for i in range(2)]
        STm = [take(f"STm{i}", [64, 64], BF16) for i in range(2)]
        ktil = [take(f"ktil{i}", [64, 128], BF16) for i in range(2)]
        hn = [take(f"hn{i}", [64, 256], BF16) for i in range(2)]
        sv = [take(f"sv{i}", [64, 8], F32) for i in range(2)]
        junk = take("junk", [64, 256], BF16)
        keys = ["yT", "qT", "kT", "wh", ("wst", 0), ("wst", 1), "GA", "GB", "GC", "msmall", "Rm",
                "hmask", "ones4", "bif", "wg_s", "wg", "Etok", "DNtok", "a_bc", "causalT", "Cst", ("Cab", 0),
                ("Cab", 1), ("STm", 0), ("STm", 1), ("ktil", 0), ("ktil", 1), ("hn", 0), ("hn", 1), ("sv", 0),
                ("sv", 1), "junk"] + [("yTk", c, n) for c in range(KC) for n in range(NB)]
        self.use_scratch(keys)
        w_in = self.w_a_in[l].rearrange("(c p) n -> p c n", p=128)
        hTk = lambda c, n: ("hT", c, n)
        self.dma(causalT[:], self.causalT_d, [], ["causalT"])
        self.dma(hmask[:], self.hmask_d, [], ["hmask"])
        self.dma(GC[:], self.scanmask_d, [], ["GC"])
        P.op("dve", lambda e: e.memset(ones4[:], 1.0), [], ["ones4"])
        self.dma(bif[:], self.b_a_if[l].rearrange("(two p) -> p two", p=4), [], ["bif"], allow_slow_non_contiguous=True)
        self.dma(wg_s[:], w_in[:, :, 3072:3080], [], ["wg_s"])
        self.copy(wg[:], wg_s[:], ["wg_s"], ["wg"])
        for n in range(NB):
            tsl = slice(n * 512, (n + 1) * 512)
            for part, dst, bcol in ((0, GA, 0), (1, GB, 1)):
                pb = (2 * n + part) % 4
                for c in range(KC):
                    self.mm(ps[pb][0:4, :], wg[:, c, part * 4:(part + 1) * 4], hT[:, c, tsl], c == 0, c == KC - 1,
                            ["wg", hTk(c, n)], [("ps", pb)])
                if part == 0:
                    self.act(dst[:, tsl], ps[pb][0:4, :], AF.Identity, [("ps", pb), "bif"], ["GA"], bias=bif[:, 0:1])
                else:
                    self.act(dst[:, tsl], ps[pb][0:4, :], AF.Identity, [("ps", pb), "bif"], ["GB"], bias=bif[:, 1:2])
        self.act(GB[:], GB[:], AF.Exp, ["GB"], ["GB"], scale=-1.0)
        self.act(GB[:], GB[:], AF.Ln, ["GB"], ["GB"], bias=1.0)
        umax, blast, mnext, mprev, mu, av = (small[:, i, :] for i in range(6))
        P.op("dve", lambda e: e.tensor_tensor_scan(out=GC[:], data0=GC[:], data1=GB[:], initial=0.0,
                                                    op0=ALU.mult, op1=ALU.add), ["GB", "GC"], ["GC"])
        self.tt(GA[:], GA[:], GC[:], ALU.add, ["GA", "GC"], ["GA"])
        P.op("dve", lambda e: e.tensor_reduce(out=umax, in_=GA[:].rearrange("p (c s) -> p c s", s=64), axis=AX.X,
                                               op=ALU.max), ["GA"], ["msmall"])
        self.ts(blast, GC[:].rearrange("p (c s) -> p c s", s=64)[:, :, 63], -1.0, ALU.mult, ["GC"], ["msmall"])
        P.op("dve", lambda e: e.tensor_tensor_scan(out=mnext, data0=umax, data1=blast, initial=0.0,
                                                    op0=ALU.max, op1=ALU.add), ["msmall"], ["msmall"])
        P.op("dve", lambda e: e.memset(mprev[:, 0:1], 0.0), ["msmall"], ["msmall"])
        self.copy(mprev[:, 1:32], mnext[:, 0:31], ["msmall"], ["msmall"])
        self.tt(mu, mprev, umax, ALU.max, ["msmall"], ["msmall"])
        self.tt(av, mprev, mu, ALU.subtract, ["msmall"], ["msmall"])
        self.act(av, av, AF.Exp, ["msmall"], ["msmall"])
        mu_b = self.bcast_small(small, 4, 32, 64)
        self.tt(GA[:].rearrange("p (c s) -> p c s", s=64), GA[:].rearrange("p (c s) -> p c s", s=64), mu_b,
                ALU.subtract, ["GA", "msmall"], ["GA"])
        self.act(GA[:], GA[:], AF.Exp, ["GA"], ["GA"])
        self.tt(GC[:].rearrange("p (c s) -> p c s", s=64), GC[:].rearrange("p (c s) -> p c s", s=64), mu_b,
                ALU.subtract, ["GC", "msmall"], ["GC"])
        self.act(GC[:], GC[:], AF.Exp, ["GC"], ["GC"])
        for src, dst, key, pb in ((GA, Etok, "Etok", 0), (GC, DNtok, "DNtok", 1)):
            for c in range(32):
                self.tr(ps[pb][0:64, c * 4:(c + 1) * 4], src[:, c * 64:(c + 1) * 64], self.ident_f[0:4, 0:4],
                        ["GA", "GC", "ident_f"], [("ps", pb)])
            self.copy(dst[:], ps[pb][0:64, 0:128].rearrange("p (c h) -> p c h", h=4), [("ps", pb)], [key])
        a_b = bass.AP(small[:].tensor, small[:, 5, :].offset, [[small[:].ap[0][0], 4], [0, 4], [1, 32]])
        self.tt(Rm[:].rearrange("p (h c) -> p h c", h=4), a_b, hmask[:].rearrange("p (h c) -> p h c", h=4), ALU.mult,
                ["msmall", "hmask"], ["Rm"])
        self.mm(ps[2][:, 0:128], ones4[:], Rm[:], True, True, ["ones4", "Rm"], [("ps", 2)])
        self.copy(a_bc[:], ps[2][:, 0:128], [("ps", 2)], ["a_bc"])

        vkeys = ["vtok"] + [("vtok", i) for i in range(4)] + [("soT", n) for n in range(NB)]
        P.alias(vkeys, ["GA", "GB", "GC"])
        self._scratch_keys |= set(vkeys)
        P.op("dve", lambda e: e.memset(vtok[:, :, 256:257], 1.0), [], ["vtok"])
        ps6b = ps[6][:].bitcast(BF16)
        ps7b = ps[7][:].bitcast(BF16)
        nst = 0
        for h in range(4):
            cols = [h * 128, 512 + h * 128, 1024 + h * 256, 1024 + h * 256 + 128, 2048 + h * 256, 2048 + h * 256 + 128]
            for i, c0 in enumerate(cols):
                b = nst % 2
                nst += 1
                self.dma(wst[b][:], w_in[:, :, c0:c0 + 128], [], [("wst", b)])
                self.copy(wh[:, :, i * 128:(i + 1) * 128], wst[b][:], [("wst", b)], ["wh"], eng="pool")
            for n in range(NB):
                tsl = slice(n * 512, (n + 1) * 512)
                for which, dst, key in ((0, qT, "qT"), (1, kT, "kT")):
                    pb = (2 * n + which) % 4
                    for c in range(KC):
                        self.mm(ps[pb][:, :], wh[:, c, which * 128:(which + 1) * 128], hT[:, c, tsl], c == 0, c == KC - 1,
                                ["wh", hTk(c, n)], [("ps", pb)])
                    if which == 0:
                        self.act(dst[:, tsl], ps[pb][:, :], AF.Copy, [("ps", pb)], [(key, n)], scale=128.0 ** -0.5)
                    else:
                        self.copy(dst[:, tsl], ps[pb][:, :], [("ps", pb)], [(key, n)])
                for half in range(2):
                    pb = 4 + half
                    for c in range(KC):
                        self.mm(ps[pb][:, :], wh[:, c, 512 + half * 128: 640 + half * 128], hT[:, c, tsl], c == 0,
                                c == KC - 1, ["wh", hTk(c, n)], [("ps", pb)])
                    self.act(soT[:, half, tsl], ps[pb][:, :], AF.Sigmoid, [("ps", pb)], [("soT", n)])
            for cp in range(16):
                pb = cp % 4
                for j in range(2):
                    ch = 2 * cp + j
                    for c in range(KC):
                        self.mm(ps[pb][0:64, j * 256:(j + 1) * 256], hT[:, c, ch * 64:(ch + 1) * 64], wh[:, c, 256:512],
                                c == 0, c == KC - 1, ["wh", hTk(c, ch // 8)], [("ps", pb)])
                self.copy(vtok[:, 2 * cp:2 * cp + 2, 0:256], ps[pb][0:64, :].rearrange("p (j v) -> p j v", j=2),
                          [("ps", pb)], [("vtok", cp // 4)])
            def F(ch):
                tsl = slice(ch * 64, (ch + 1) * 64)
                n, b = ch // 8, ch % 2
                e_s = Etok[:, ch, h:h + 1]
                pst, pkv = b, 4 + b
                self.mm(ps[pst][0:64, 0:64], kT[:, tsl], qT[:, tsl], True, True, [("kT", n), ("qT", n)], [("ps", pst)])
                self.tr(ps6b[0:64, b * 128:(b + 1) * 128], kT[:, tsl], self.ident_b[:], [("kT", n), "ident_b"],
                        [("ps", 6)])
                self.stt(STm[b][:], ps[pst][0:64, 0:64], e_s, causalT[:], ALU.mult, ALU.mult,
                         [("ps", pst), "Etok", "causalT"], [("STm", b)])
                self.act(ktil[b][:], ps6b[0:64, b * 128:(b + 1) * 128], AF.Copy, [("ps", 6), "Etok"], [("ktil", b)],
                         scale=e_s)
                self.mm(ps[pkv][:, 0:257], ktil[b][:], vtok[:, ch, :], True, True,
                        [("ktil", b), ("vtok", ch // 8), "vtok"], [("ps", pkv)])

            def R(ch):
                tsl = slice(ch * 64, (ch + 1) * 64)
                n, b = ch // 8, ch % 2
                a_c = a_bc[:, h * 32 + ch: h * 32 + ch + 1]
                pnum, pkv = 2 + b, 4 + b
                if ch > 0:
                    self.ts(Cab[b][:], Cst[:], a_c, ALU.mult, ["Cst", "a_bc"], [("Cab", b)])
                self.mm(ps[pnum][0:64, 0:257], STm[b][:], vtok[:, ch, :], True, ch == 0,
                        [("STm", b), ("vtok", ch // 8), "vtok"], [("ps", pnum)])
                if ch > 0:
                    self.mm(ps[pnum][0:64, 0:257], qT[:, tsl], Cab[b][:], False, True, [("qT", n), ("Cab", b)],
                            [("ps", pnum)])
                if ch == 0:
                    self.copy(Cst[:], ps[pkv][:, 0:257], [("ps", pkv)], ["Cst"])
                else:
                    self.stt(Cst[:], Cst[:], a_c, ps[pkv][:, 0:257], ALU.mult, ALU.add,
                             ["Cst", "a_bc", ("ps", pkv)], ["Cst"])

            def N1(ch):
                b = ch % 2
                pnum = 2 + b
                s_ = sv[b]
                self.ts(s_[:, 5:6], ps[pnum][0:64, 256:257], -1.0, ALU.mult, [("ps", pnum), "DNtok"], [("sv", b)],
                        s2=DNtok[:, ch, h:h + 1], op1=ALU.max)
                self.tt(s_[:, 0:1], s_[:, 5:6], ps[pnum][0:64, 256:257], ALU.max, [("ps", pnum), ("sv", b)], [("sv", b)])
                P.op("dve", lambda e, s_=s_: e.reciprocal(out=s_[:, 1:2], in_=s_[:, 0:1]), [("sv", b)], [("sv", b)])
                self.act(junk[:], ps[pnum][0:64, 0:256], AF.Square, [("ps", pnum), ("sv", b)], ["junk", ("sv", b)],
                         scale=s_[:, 1:2], accum_out=s_[:, 2:3])
                self.act(s_[:, 3:4], s_[:, 2:3], AF.Sqrt, [("sv", b), "epsc"], [("sv", b)], bias=self.epsc[0:64, :],
                         scale=1.0 / 256)

            def N2(ch):
                tsl = slice(ch * 64, (ch + 1) * 64)
                n, b = ch // 8, ch % 2
                pnum = 2 + b
                s_ = sv[b]
                P.op("dve", lambda e, s_=s_: e.reciprocal(out=s_[:, 3:4], in_=s_[:, 3:4]), [("sv", b)], [("sv", b)])
                self.tt(s_[:, 4:5], s_[:, 3:4], s_[:, 1:2], ALU.mult, [("sv", b)], [("sv", b)])
                self.act(hn[b][:], ps[pnum][0:64, 0:256], AF.Copy, [("ps", pnum), ("sv", b)], [("hn", b)],
                         scale=s_[:, 4:5])
                for half in range(2):
                    dc = 2 * h + half
                    self.tr(ps7b[:, (2 * b + half) * 64:(2 * b + half + 1) * 64], hn[b][:, half * 128:(half + 1) * 128],
                            self.ident_b[0:64, 0:64], [("hn", b), "ident_b"], [("ps", 7)])
                    self.stt(yT[:, dc, tsl], ps7b[:, (2 * b + half) * 64:(2 * b + half + 1) * 64],
                             self.vecT[:, 280 + l * 8 + dc: 281 + l * 8 + dc], soT[:, half, tsl], ALU.mult, ALU.mult,
                             [("ps", 7), "vecT", ("soT", n)], [("yTk", dc, n)])

            F(0)
            for it in range(33):
                if it >= 1:
                    N1(it - 1)
                if it + 1 < 32:
                    F(it + 1)
                if it < 32:
                    R(it)
                if it >= 1:
                    N2(it - 1)
        wao = self.sb("wao", [128, KC, D], BF16, off=vo)
        wos = [self.sb(f"waos{i}", [128, D], F32, off=vo + 16384 + i * 4096) for i in range(2)]
        P.alias(["wao", ("waos", 0), ("waos", 1)], vkeys)
        for c in range(KC):
            b = c % 2
            self.dma(wos[b][:], self.w_a_out[l, c * 128:(c + 1) * 128, :], [], [("waos", b)])
            self.copy(wao[:, c, :], wos[b][:], [("waos", b)], ["wao"], eng="pool")
        self._scratch_keys |= {"wao", ("waos", 0), ("waos", 1)}
        k = 0
        for dc in range(KC):
            for n in range(NB):
                tsl = slice(n * 512, (n + 1) * 512)
                pb = k % 2
                k += 1
                for c in range(KC):
                    self.mm(ps[pb][:, :], wao[:, c, dc * 128:(dc + 1) * 128], yT[:, c, tsl], c == 0, c == KC - 1,
                            ["wao", ("yTk", c, n)], [("ps", pb)])
                self.stt(xT[:, dc, tsl], ps[pb][:, :], self.mod[:, l, 16 + dc: 17 + dc], xT[:, dc, tsl], ALU.mult, ALU.add,
                         [("ps", pb), ("mod", l), ("xT", dc, n)], [("xT", dc, n)])

    def _off_of(self, t):
        return t.manual_sbuf_range[0]

    def bcast_small(self, small, idx, n, post):
        a = small[:, idx, :]
        return bass.AP(a.tensor, a.offset, [[a.ap[0][0], 4], [1, n], [0, post]])


    def ptake(self, name, shape, dt):
        nb = (int(np.prod(shape[1:])) * (4 if dt == F32 else 2) + 63) // 64 * 64
        self.scr_end -= nb
        return self.sb(name, shape, dt, off=self.scr_end)

    def rev_ap(self, t, nparts, mid, n):
        a = t[:]
        return bass.AP(a.tensor, a.offset + n - 1, [[a.ap[0][0], nparts], [n, mid], [-1, n]])

    def nsa_kv(self):
        P, ps, xT, hT, nc = self.P, self.ps, self.xT, self.hT, self.nc
        ksT = self.ksT = self.ptake("ksT", [128, 2, S], BF16)
        kwT = self.kwT = self.ptake("kwT", [128, 2, S], BF16)
        vs = self.vs = self.ptake("vs", [128, NT, 4, 65], BF16)
        vw = self.vw = self.ptake("vw", [128, NT, 4, 65], BF16)
        kcT = self.kcT = self.ptake("kcT", [128, 2, 128], BF16)
        vcx = self.vcx = self.ptake("vcx", [128, 4, 97], BF16)
        PatD = self.PatD = [self.ptake(f"PatD{i}", [128, 16, 128], BF16) for i in range(2)]
        PatD4 = self.PatD4 = self.ptake("PatD4", [128, 4, 128], BF16)
        Expand = self.Expand = self.ptake("Expand", [32, S], BF16)
        addc = self.addc = self.ptake("addc", [128, 16, 32], F32)
        zrow = self.zrow = self.ptake("zrow", [1, 512], BF16)
        blockones = self.blockones = self.ptake("blockones", [128, 128], BF16)
        gk = self.gk = self.ptake("gk", [128, 4], F32)
        gk0b = self.gk0b = self.ptake("gk0b", [128, 64], F32)
        tiny = self.tiny = self.ptake("tiny", [128, 1], F32)
        assert self.scr0 + 59392 <= self.scr_end, (self.scr0, self.scr_end)
        o = self.scr0
        def take(name, shape, dt):
            nonlocal o
            t = self.sb(name, shape, dt, off=o)
            o += (int(np.prod(shape[1:])) * (4 if dt == F32 else 2) + 63) // 64 * 64
            assert o <= self.scr_end, (name, o)
            return t
        stg = take("kvstg", [128, 2048], F32)
        wkb = take("wkb", [128, KC, 256], BF16)
        wpad = [take(f"wpad{i}", [128, KC, 128], BF16) for i in range(2)]
        c2T = take("c2T", [128, S], BF16)
        w1b = take("w1b", [128, 16, 256], BF16)
        posf = take("posf", [128, 16], F32)
        posb = take("posb", [128, 2], F32)
        w2b = take("w2b", [128, 2, 64], BF16)
        hid = take("hid", [128, 2, 128], BF16)
        gx = [take(f"gx{i}", [128, 128], F32) for i in range(3)]
        sqk = take("sqk", [128, 512], BF16)
        kraw = take("kraw", [128, 512], F32)
        rk_ = take("rk_", [128, 512], F32)
        ctok = take("ctok", [128, 4, 64], F32)
        ckn = take("ckn", [128, 4, 64], BF16)
        csm = take("csm", [128, 8], F32)
        tab33 = take("tab33", [33, 16], F32)
        ohs = take("ohs", [33, 512], F32)
        Fsb = self.sb("Fsb", [16, NF], F32, off=self._off_of(wpad[0]))
        keys = ["kvstg", "wkb", ("wpad", 0), ("wpad", 1), "c2T", "w1b", "posf", "posb", "w2b", "hid", "gx", ("sqk", "kv"), ("kraw", "kv"),
                ("rk_", "kv"), "ctok", "ckn", "csm", "tab33", "ohs", "Fsb"]
        self.use_scratch(keys)
        stg3 = stg[:].rearrange("p (c n) -> p c n", c=KC)
        self.dma(stg[:, 0:128], self.blockones_d, [], ["kvstg"])
        self.copy(blockones[:], stg[:, 0:128], ["kvstg"], ["blockones"])
        self.dma(stg[0:32, :], self.expand_d, [], ["kvstg"])
        self.copy(Expand[:], stg[0:32, :], ["kvstg"], ["Expand"])
        self.dma(addc[:].rearrange("p a b -> p (a b)"), self.addc_d, [], ["addc"])
        P.op("dve", lambda e: e.memset(zrow[:], 0.0), [], ["zrow"])
        P.op("dve", lambda e: e.memset(tiny[:], 1e-30), [], ["tiny"])
        self.dma(stg[:, 0:128], self.d4mask_d, ["kvstg"], ["kvstg"])
        for hh in range(4):
            self.copy(PatD4[:, hh, :], stg[:, 0:128], ["kvstg"], ["PatD4"])
        for j in (1, 2):
            for half in range(2):
                self.dma(gk[half * 64:(half + 1) * 64, j:j + 1], self.g_knorm[j:j + 1, :].rearrange("o d -> d o"), [], ["gk"],
                         allow_slow_non_contiguous=True)
        g0 = self.g_knorm[0:1, :]
        self.dma(gk0b[:], bass.AP(g0.tensor, g0.offset, [[0, 128], [1, 64]]), [], ["gk0b"])
        self.dma(stg[0:127, 0:32], self.overlap_d, ["kvstg"], ["kvstg"])
        for g in range(4):
            self.copy(vcx[0:127, g, 65:97], stg[0:127, 0:32], ["kvstg"], ["vcx"])
        P.op("dve", lambda e: e.memset(vcx[:, :, 64:65], 1.0), ["vcx"], ["vcx"])
        self.dma(tab33[0:32, :], self.rel_table, [], ["tab33"])
        P.op("dve", lambda e: e.memset(tab33[32:33, :], 1.0), [], ["tab33"])
        for ch in range(NF // 512):
            self.dma(ohs[:], self.onehot_d[:, ch * 512:(ch + 1) * 512], [], ["ohs"])
            self.mm(ps[0][0:16, :], tab33[:], ohs[:], True, True, ["tab33", "ohs"], [("ps", 0)])
            self.copy(Fsb[:, ch * 512:(ch + 1) * 512], ps[0][0:16, :], [("ps", 0)], ["Fsb"])
        self.dma(self.fvd, Fsb[:], ["Fsb"], ["fvd"])
        P.alias([("wpad", 0), ("wpad", 1), "c2T", "w1b"], ["Fsb"])
        for dl in range(2):
            base = 1920 - 128 * dl
            for h4 in range(4):
                src = bass.AP(self.fvd.tensor, base + h4 * 4 * NF, [[1, 128], [NF, 4], [1, 128]])
                self.dma(stg[:, h4 * 512:(h4 + 1) * 512].rearrange("p (h t) -> p h t", h=4), src, ["fvd", "kvstg"], ["kvstg"])
            self.copy(PatD[dl][:], self.rev_ap(stg, 128, 16, 128), ["kvstg"], [("PatD", dl)])
        stage = int(self.debug[2:3]) if (self.debug or "").startswith("kv") else 9
        if stage < 2:
            return
        self.norm_mod(None, None, gsc=lambda c: self.modkv[:, 16 + c:17 + c], shift=lambda c: self.modkv[:, c:c + 1])
        self.use_scratch(keys)
        if self.debug == "kv2a":
            return
        wkv = self.w_kv.rearrange("(c p) n -> p c n", p=128)
        hTk = lambda c, n: ("hT", c, n)
        for j, dst, key, gcol in ((2, ksT, "ksT", 1), (4, kwT, "kwT", 2)):
            for gp in range(2):
                self.dma(stg3[:, :, 0:128], wkv[:, :, j * 256 + gp * 128: j * 256 + (gp + 1) * 128], ["kvstg"], ["kvstg"])
                self.copy(wkb[:, :, 0:128], stg3[:, :, 0:128], ["kvstg"], ["wkb"], eng="pool")
                for n in range(NB):
                    tsl = slice(n * 512, (n + 1) * 512)
                    pb = n % 2
                    for c in range(KC):
                        self.mm(ps[pb][:, :], wkb[:, c, 0:128], hT[:, c, tsl], c == 0, c == KC - 1, ["wkb", hTk(c, n)],
                                [("ps", pb)])
                    self.qknorm(pb, dst[:, gp, tsl], gk[:, gcol:gcol + 1], sqk, kraw, rk_, 2 + pb, [(key, gp, n)], ["gk"], sfx="kv")
        if stage < 3:
            return
        for j, dst, key in ((3, vs, "vs"), (5, vw, "vw")):
            self.dma(stg3[:, :, 0:256], wkv[:, :, j * 256:(j + 1) * 256], ["kvstg"], ["kvstg"])
            self.copy(wkb[:], stg3[:, :, 0:256], ["kvstg"], ["wkb"], eng="pool")
            P.op("dve", lambda e, dst=dst: e.memset(dst[:, :, :, 64:65], 1.0), [], [key])
            for tp in range(NT // 2):
                pb = tp % 2
                for jj in range(2):
                    t = 2 * tp + jj
                    for c in range(KC):
                        self.mm(ps[pb][:, jj * 256:(jj + 1) * 256], hT[:, c, t * 128:(t + 1) * 128], wkb[:, c, :], c == 0,
                                c == KC - 1, ["wkb", hTk(c, t // 4)], [("ps", pb)])
                self.copy(dst[:, 2 * tp:2 * tp + 2, :, 0:64], ps[pb][:, :].rearrange("p (j g d) -> p j g d", j=2, g=4),
                          [("ps", pb)], [(key, tp)])
        if stage < 4:
            return
        for which in range(2):
            w1v = self.w_cmp1[which].rearrange("(a p) f -> p a f", p=128)
            for q4 in range(2):
                self.dma(stg[:].rearrange("p (a f) -> p a f", a=8), w1v[:, q4 * 8:(q4 + 1) * 8, :], ["kvstg"], ["kvstg"])
                self.copy(w1b[:, q4 * 8:(q4 + 1) * 8, :], stg[:].rearrange("p (a f) -> p a f", a=8), ["kvstg"], ["w1b"],
                          eng="pool")
            self.dma(stg[0:16, 0:128], self.pos_cmp[which].rearrange("(a two) d -> a (two d)", two=2), ["kvstg"], ["kvstg"])
            self.tr(ps[4][:, 16:32], stg[0:16, 0:128], self.ident_f[0:16, 0:16], ["kvstg", "ident_f"], [("ps", 4)])
            self.copy(hid[:, 0, 0:16], ps[4][:, 16:32], [("ps", 4)], ["hid"])
            for ft in range(2):
                for a in range(16):
                    self.mm(ps[4][:, ft:ft + 1], w1b[:, a, ft * 128:(ft + 1) * 128], hid[:, 0, a:a + 1], a == 0, a == 15,
                            ["w1b", "hid"], [("ps", 4)])
            self.copy(posb[:], ps[4][:, 0:2], [("ps", 4)], ["posb"])
            self.dma(stg[:, 0:128].rearrange("p (a d) -> p a d", a=2), self.w_cmp2[which].rearrange("(a p) d -> p a d", p=128),
                     ["kvstg"], ["kvstg"])
            self.copy(w2b[:], stg[:, 0:128].rearrange("p (a d) -> p a d", a=2), ["kvstg"], ["w2b"])
            for g in range(4):
                col = which * 256 + g * 64
                self.dma(stg3[:, :, 0:64], wkv[:, :, col:col + 64], ["kvstg"], ["kvstg"])
                P.op("pool", lambda e: e.memset(wpad[0][:], 0.0), [], [("wpad", 0)])
                P.op("pool", lambda e: e.memset(wpad[1][:], 0.0), [], [("wpad", 1)])
                self.copy(wpad[0][:, :, 0:64], stg3[:, :, 0:64], ["kvstg"], [("wpad", 0)], eng="pool")
                self.copy(wpad[1][:, :, 64:128], stg3[:, :, 0:64], ["kvstg"], [("wpad", 1)], eng="pool")
                for n in range(NB):
                    pb = n % 2
                    nn = 512 if n < NB - 1 else 511
                    for c in range(KC):
                        self.mm(ps[pb][:, :], wpad[0][:, c, :], hT[:, c, n * 512:(n + 1) * 512], c == 0, False,
                                [("wpad", 0), hTk(c, n)], [("ps", pb)])
                    for c in range(KC):
                        self.mm(ps[pb][:, 0:nn], wpad[1][:, c, :], hT[:, c, n * 512 + 1: n * 512 + 1 + nn], False,
                                c == KC - 1, [("wpad", 1), hTk(c, n), hTk(c, min(n + 1, NB - 1))], [("ps", pb)])
                    self.copy(c2T[:, n * 512:(n + 1) * 512], ps[pb][:, :], [("ps", pb)], ["c2T"])
                c2v = c2T[:].rearrange("p (n s) -> p s n", s=16)
                for ft in range(2):
                    pb = 2 + ft
                    for a in range(16):
                        rhs = bass.AP(c2T[:].tensor, c2T[:].offset + 2 * a, [[c2T[:].ap[0][0], 128], [16, 127]])
                        self.mm(ps[pb][:, 0:127], w1b[:, a, ft * 128:(ft + 1) * 128], rhs, a == 0, a == 15,
                                ["w1b", "c2T"], [("ps", pb)])
                    x_, x2, x3 = gx[0][:, 0:127], gx[1][:, 0:127], gx[2][:, 0:127]
                    self.act(x_, ps[pb][:, 0:127], AF.Identity, [("ps", pb), "posb"], ["gx"], bias=posb[:, ft:ft + 1])
                    self.tt(x2, x_, x_, ALU.mult, ["gx"], ["gx"])
                    self.ts(x2, x2, 0.044715, ALU.mult, ["gx"], ["gx"], s2=1.0, op1=ALU.add)
                    self.tt(x2, x2, x_, ALU.mult, ["gx"], ["gx"])
                    self.act(x3, x2, AF.Sigmoid, ["gx"], ["gx"], scale=2.0 * math.sqrt(2.0 / math.pi))
                    self.tt(hid[:, ft, 0:127], x_, x3, ALU.mult, ["gx"], ["hid"])
                for ft in range(2):
                    self.mm(ps[5][0:127, g * 64:(g + 1) * 64], hid[:, ft, 0:127], w2b[:, ft, :], ft == 0, ft == 1,
                            ["hid", "w2b"], [("ps", 5)])
                if which == 1:
                    self.copy(vcx[0:127, g, 0:64], ps[5][0:127, g * 64:(g + 1) * 64], [("ps", 5)], ["vcx"])
            if which == 0:
                self.copy(ctok[0:127, :, :], ps[5][0:127, 0:256].rearrange("p (g d) -> p g d", g=4), [("ps", 5)], ["ctok"])
                for g in range(4):
                    self.act(gx[0][0:127, 0:64], ctok[0:127, g, :], AF.Square, ["ctok"], ["gx", "csm"],
                             accum_out=csm[0:127, g:g + 1])
                self.act(csm[0:127, 4:8], csm[0:127, 0:4], AF.Sqrt, ["csm", "epsc"], ["csm"], bias=self.epsc[0:127, :],
                         scale=1.0 / 64)
                P.op("dve", lambda e: e.reciprocal(out=csm[0:127, 4:8], in_=csm[0:127, 4:8]), ["csm"], ["csm"])
                for g in range(4):
                    self.stt(ckn[0:127, g, :], ctok[0:127, g, :], csm[0:127, 4 + g:5 + g], gk0b[0:127, :], ALU.mult, ALU.mult,
                             ["ctok", "csm", "gk0b"], ["ckn"])
                ps6b = ps[6][:].bitcast(BF16)
                for gp in range(2):
                    self.tr(ps6b[:, gp * 128: gp * 128 + 127], ckn[0:127, 2 * gp:2 * gp + 2, :].rearrange("p g d -> p (g d)"),
                            self.ident_b[0:127, 0:127], ["ckn", "ident_b"], [("ps", 6)])
                    self.copy(kcT[:, gp, 0:127], ps6b[:, gp * 128: gp * 128 + 127], [("ps", 6)], ["kcT"])

    def qknorm(self, pin, dst, gvec, sqk, kraw, rk_, pbank, wkeys, rkeys, sfx=0):
        P, ps = self.P, self.ps
        psb = ps[pin]
        if self.debug == "kv2b":
            self.copy(dst, psb[:, :], [("ps", pin)], wkeys)
            return
        self.copy(kraw[:], psb[:, :], [("ps", pin)], [("kraw", sfx)])
        self.act(sqk[:], psb[:, :], AF.Square, [("ps", pin)], [("sqk", sfx)])
        self.mm(ps[pbank][:, :], self.blockones[:], sqk[:], True, True, ["blockones", ("sqk", sfx)], [("ps", pbank)])
        self.act(rk_[:], ps[pbank][:, :], AF.Sqrt, [("ps", pbank), "epsc"], [("rk_", sfx)], bias=self.epsc[:], scale=1.0 / 64)
        P.op("dve", lambda e: e.reciprocal(out=rk_[:], in_=rk_[:]), [("rk_", sfx)], [("rk_", sfx)])
        self.tt(kraw[:], kraw[:], rk_[:], ALU.mult, [("kraw", sfx), ("rk_", sfx)], [("kraw", sfx)])
        if self.debug == "kv2v1":
            self.ts(dst, kraw[:], gvec, ALU.mult, [("kraw", sfx)] + rkeys, wkeys)
        else:
            self.act(dst, kraw[:], AF.Copy, [("kraw", sfx)] + rkeys, wkeys, scale=gvec)


    def nsa(self, l):
        P, ps, xT, hT, nc = self.P, self.ps, self.xT, self.hT, self.nc
        j = l - 2
        o = self.scr0
        def take(name, shape, dt):
            nonlocal o
            t = self.sb(name, shape, dt, off=o)
            o += (int(np.prod(shape[1:])) * (4 if dt == F32 else 2) + 63) // 64 * 64
            assert o <= self.scr_end, (name, o)
            return t
        qT = take("nqT", [128, 8, S], BF16)
        gates = take("gates", [128, NT, 48], F32)
        bgb = take("bgb", [128, 48], F32)
        gq2 = take("gq2", [128, 2], F32)
        o_attn = o
        wst = take("nwst", [128, KC, 128], F32)
        wqb = [take(f"wqb{i}", [128, KC, 128], BF16) for i in range(2)]
        sqk = take("nsqk", [128, 512], BF16)
        kraw = take("nkraw", [128, 512], F32)
        rk_ = take("nrk", [128, 512], F32)
        wgs = take("wgs", [128, KC, 48], F32)
        wgb = take("wgb", [128, KC, 48], BF16)
        pkeys = ["nqT", "gates", "bgb", "gq2", "nwst", ("wqb", 0), ("wqb", 1), ("sqk", "q"), ("kraw", "q"), ("rk_", "q"), "wgs", "wgb"]
        self.use_scratch(pkeys)
        wq = self.w_b_q[j].rearrange("(c p) n -> p c n", p=128)
        hTk = lambda c, n: ("hT", c, n)
        for half in range(2):
            self.dma(gq2[half * 64:(half + 1) * 64, 0:1], self.g_qnorm[j:j + 1, :].rearrange("o d -> d o"), [], ["gq2"],
                     allow_slow_non_contiguous=True)
        self.ts(gq2[:, 1:2], gq2[:, 0:1], 0.125, ALU.mult, ["gq2"], ["gq2"])
        bg = self.b_b_gate[j:j + 1, :]
        self.dma(bgb[:], bass.AP(bg.tensor, bg.offset, [[0, 128], [1, 48]]), [], ["bgb"])
        for sl in range(8):
            gp, hh = sl // 4, sl % 4
            b = sl % 2
            for par in range(2):
                head = (2 * gp + par) * 4 + hh
                self.dma(wst[:, :, par * 64:(par + 1) * 64], wq[:, :, head * 64:(head + 1) * 64], [], ["nwst"])
            self.copy(wqb[b][:], wst[:], ["nwst"], [("wqb", b)], eng="pool")
            for n in range(NB):
                tsl = slice(n * 512, (n + 1) * 512)
                pb = n % 2
                for c in range(KC):
                    self.mm(ps[pb][:, :], wqb[b][:, c, :], hT[:, c, tsl], c == 0, c == KC - 1, [("wqb", b), hTk(c, n)],
                            [("ps", pb)])
                self.qknorm(pb, qT[:, sl, tsl], gq2[:, 1:2], sqk, kraw, rk_, 2 + pb, [("nqT", sl, n)], ["gq2"], sfx="q")
        self.dma(wgs[:], wq[:, :, 1024:1072], [], ["wgs"])
        self.copy(wgb[:], wgs[:], ["wgs"], ["wgb"])
        for t in range(NT):
            pb = 4 + t % 2
            for c in range(KC):
                self.mm(ps[pb][:, 0:48], hT[:, c, t * 128:(t + 1) * 128], wgb[:, c, :], c == 0, c == KC - 1,
                        ["wgb", hTk(c, t // 4)], [("ps", pb)])
            self.tt(gates[:, t, :], ps[pb][:, 0:48], bgb[:], ALU.add, [("ps", pb), "bgb"], [("gates", t)])
        self.act(gates[:], gates[:], AF.Sigmoid, [("gates", t) for t in range(NT)], [("gates", t) for t in range(NT)])
        o = o_attn
        cb = [take(f"cb{i}", [128, 4, 128], F32) for i in range(2)]
        ssum = take("ssum", [128, 512], F32)
        EcT = take("EcT", [128, 512], BF16)
        Et = [take(f"Et{i}", [128, 512], BF16) for i in range(3)]
        selT = [take(f"selT{i}", [32, 512], BF16) for i in range(2)]
        otok = [take(f"otok{i}", [128, D], BF16) for i in range(2)]
        acc = [take(f"acc{i}", [128, 4, 64], F32) for i in range(2)]
        ocs = [take(f"ocs{i}", [128, 4, 65], F32) for i in range(2)]
        zA = take("zA", [128, 8], F32)
        zB = take("zB", [128, 12], F32)
        cf = take("cf", [128, 12], F32)
        impt = take("impt", [128, 4, 32], F32)
        score = take("score", [128, 64], F32)
        top8 = take("top8", [128, 8], F32)
        akeys = [("cb", 0), ("cb", 1), "ssum", "EcT", ("Et", 0), ("Et", 1), ("Et", 2), ("selT", 0), ("selT", 1), ("otok", 0),
                 ("otok", 1), ("acc", 0), ("acc", 1), ("ocs", 0), ("ocs", 1), "zA", "zB", "cf", "impt", "score", "top8"]
        P.alias(akeys, ["nwst", ("wqb", 0), ("wqb", 1), ("sqk", "q"), ("kraw", "q"), ("rk_", "q"), "wgs", "wgb"])
        self._scratch_keys |= set(akeys)
        ksT, kwT, vs, vw, kcT, vcx = self.ksT, self.kwT, self.vs, self.vw, self.kcT, self.vcx
        ps6b = ps[6][:].bitcast(BF16)
        cnt = {"e": 0, "s": 0}

        def qview(qi, g):
            gp, par = g // 2, g % 2
            hs = slice(par * 64, (par + 1) * 64)
            qv = qT[hs, gp * 4:(gp + 1) * 4, qi * 128:(qi + 1) * 128]
            qk = [("nqT", gp * 4 + hh, qi // 4) for hh in range(4)]
            return gp, hs, qv, qk

        def A_ops(qi, g):
            gp, hs, qv, qk = qview(qi, g)
            b = (qi * 4 + g) % 2
            base = 1951 - 128 * qi
            src = bass.AP(self.fvd.tensor, (g * 4) * NF + base, [[16, 127], [NF, 4], [1, 128]])
            Oc = ps[3][:, 0:388].rearrange("p (h c) -> p h c", h=4)
            rzb = bass.AP(zA[:].tensor, zA[:, 4:8].offset, [[zA[:].ap[0][0], 128], [1, 4], [0, 32]])
            p7 = ps[6][0:32, 0:128]
            ops = []
            ops.append(lambda: self.dma(cb[b][0:127, :, :], src, ["fvd"], [("cb", b)]))
            ops.append(lambda: self.mm(ps[6][0:127, :], kcT[hs, gp, 0:127], qv, True, True, ["kcT"] + qk, [("ps", 6)]))
            ops.append(lambda: self.tt(ssum[0:127, :].rearrange("p (h t) -> p h t", h=4),
                                       ps[6][0:127, :].rearrange("p (h t) -> p h t", h=4), self.rev_ap(cb[b], 127, 4, 128),
                                       ALU.add, [("ps", 6), ("cb", b)], ["ssum"]))
            ops.append(lambda: self.act(EcT[0:127, :], ssum[0:127, :], AF.Exp, ["ssum"], ["EcT"]))

            def pvc():
                for hh in range(4):
                    self.mm(ps[3][:, hh * 97:(hh + 1) * 97], EcT[0:127, hh * 128:(hh + 1) * 128], vcx[0:127, g, :], True, True,
                            ["EcT", "vcx"], [("ps", 3)])
            ops.append(pvc)
            ops.append(lambda: self.copy(ocs[b][:], Oc[:, :, 0:65], [("ps", 3)], [("ocs", b)]))
            ops.append(lambda: self.ts(zA[:, 0:4], ocs[b][:, :, 64], self.tiny[:], ALU.max, [("ocs", b), "tiny"], ["zA"]))
            ops.append(lambda: P.op("dve", lambda e: e.reciprocal(out=zA[:, 4:8], in_=zA[:, 0:4]), ["zA"], ["zA"]))
            ops.append(lambda: self.tt(impt[:], Oc[:, :, 65:97], rzb, ALU.mult, [("ps", 3), "zA"], ["impt"]))
            ops.append(lambda: P.op("dve", lambda e: e.tensor_reduce(out=score[:, 0:32], in_=impt[:].rearrange("p h j -> p j h"),
                                                                     axis=AX.X, op=ALU.add), ["impt"], ["score"]))
            ops.append(lambda: self.tt(score[:, 0:32], score[:, 0:32], self.addc[:, qi, :], ALU.add, ["score", "addc"], ["score"]))
            ops.append(lambda: P.op("dve", lambda e: e.max(out=top8[:], in_=score[:, 0:32]), ["score"], ["top8"]))
            ops.append(lambda: self.ts(score[:, 32:64], score[:, 0:32], top8[:, 7:8], ALU.is_ge, ["score", "top8"], ["score"]))
            ops.append(lambda: self.ts(score[:, 32:64], score[:, 32:64], -1.0, ALU.add, ["score"], ["score"], s2=30000.0,
                                       op1=ALU.mult))
            ops.append(lambda: self.tr(ps[6][0:32, 0:128], score[:, 32:64], self.ident_f[:], ["score", "ident_f"], [("ps", 6)]))
            ops.append(lambda: self.copy(selT[b][:].rearrange("p (h t) -> p h t", h=4),
                                         bass.AP(p7.tensor, p7.offset, [[p7.ap[0][0], 32], [0, 4], [1, 128]]), [("ps", 6)],
                                         [("selT", b)]))
            return ops

        def B_emit(qi, g, a_next):
            gp, hs, qv, qk = qview(qi, g)
            b = (qi * 4 + g) % 2
            ob = qi % 2
            pO, pW = (4, 5) if g % 2 == 0 else (2, 7)
            a_next = list(a_next)
            self.mm(ps[pO][:, 0:260], self.zrow[0:1, 0:128], self.zrow[0:1, 0:260], True, False, ["zrow"], [("ps", pO)])
            self.mm(ps[pW][:, 0:260], self.zrow[0:1, 0:128], self.zrow[0:1, 0:260], True, False, ["zrow"], [("ps", pW)])
            units = [("s", kt) for kt in range(qi + 1)] + [("w", kt) for kt in range(max(0, qi - 4), qi + 1)]
            last = {"s": qi, "w": qi}

            def scores(u):
                br, kt = u
                dl = qi - kt
                pc = cnt["s"] % 2
                cnt["s"] += 1
                eb = cnt["e"] % 3
                cnt["e"] += 1
                if br == "s":
                    near = dl <= 1
                    self.mm(ps[pc][:, :], ksT[hs, gp, kt * 128:(kt + 1) * 128], qv, True, False,
                            [("ksT", gp, kt // 4)] + qk, [("ps", pc)])
                    self.mm(ps[pc][:, :], self.Expand[:, kt * 128:(kt + 1) * 128], selT[b][:], False, not near,
                            ["Expand", ("selT", b)], [("ps", pc)])
                    if near:
                        self.mm(ps[pc][:, :], self.ident_b[:], self.PatD[dl][:, g * 4:(g + 1) * 4, :], False, True,
                                ["ident_b", ("PatD", dl)], [("ps", pc)])
                else:
                    pat = dl in (0, 1, 4)
                    self.mm(ps[pc][:, :], kwT[hs, gp, kt * 128:(kt + 1) * 128], qv, True, not pat,
                            [("kwT", gp, kt // 4)] + qk, [("ps", pc)])
                    if dl <= 1:
                        self.mm(ps[pc][:, :], self.ident_b[:], self.PatD[dl][:, g * 4:(g + 1) * 4, :], False, True,
                                ["ident_b", ("PatD", dl)], [("ps", pc)])
                    elif dl == 4:
                        self.mm(ps[pc][:, :], self.ident_b[:], self.PatD4[:], False, True, ["ident_b", "PatD4"], [("ps", pc)])
                self.act(Et[eb][:], ps[pc][:, :], AF.Exp, [("ps", pc)], [("Et", eb)])
                return eb

            def pv(u, eb):
                br, kt = u
                po = pO if br == "s" else pW
                vv, vk = (vs, "vs") if br == "s" else (vw, "vw")
                for hh in range(4):
                    self.mm(ps[po][:, hh * 65:(hh + 1) * 65], Et[eb][:, hh * 128:(hh + 1) * 128], vv[:, kt, g, :], False,
                            kt == last[br] and hh == 3, [("Et", eb), vk, (vk, kt // 2)], [("ps", po)])

            nA = max(1, -(-len(a_next) // max(1, len(units))))
            prev = None
            for u in units:
                eb = scores(u)
                for _ in range(nA):
                    if a_next:
                        a_next.pop(0)()
                if prev is not None:
                    pv(*prev)
                prev = (u, eb)
            pv(*prev)
            while a_next:
                a_next.pop(0)()
            Os = ps[pO][:, 0:260].rearrange("p (h c) -> p h c", h=4)
            Ow = ps[pW][:, 0:260].rearrange("p (h c) -> p h c", h=4)
            self.ts(zB[:, 0:4], ocs[b][:, :, 64], self.tiny[:], ALU.max, [("ocs", b), "tiny"], ["zB"])
            self.ts(zB[:, 4:8], Os[:, :, 64], self.tiny[:], ALU.max, [("ps", pO), "tiny"], ["zB"])
            self.ts(zB[:, 8:12], Ow[:, :, 64], self.tiny[:], ALU.max, [("ps", pW), "tiny"], ["zB"])
            P.op("dve", lambda e: e.reciprocal(out=zB[:], in_=zB[:]), ["zB"], ["zB"])
            gv = gates[:, qi, g * 12:(g + 1) * 12].rearrange("p (h b) -> p b h", b=3)
            self.tt(cf[:].rearrange("p (b h) -> p b h", b=3), zB[:].rearrange("p (b h) -> p b h", b=3), gv, ALU.mult,
                    ["zB", ("gates", qi)], ["cf"])
            ab = g % 2
            for hh in range(4):
                self.ts(acc[ab][:, hh, :], ocs[b][:, hh, 0:64], cf[:, hh:hh + 1], ALU.mult, [("ocs", b), "cf"], [("acc", ab)])
                self.stt(acc[ab][:, hh, :], Os[:, hh, 0:64], cf[:, 4 + hh:5 + hh], acc[ab][:, hh, :], ALU.mult, ALU.add,
                         [("ps", pO), "cf", ("acc", ab)], [("acc", ab)])
                self.stt(otok[ob][:, (g * 4 + hh) * 64:(g * 4 + hh + 1) * 64], Ow[:, hh, 0:64], cf[:, 8 + hh:9 + hh],
                         acc[ab][:, hh, :], ALU.mult, ALU.add, [("ps", pW), "cf", ("acc", ab)], [("otok", ob)])

        order = [(qi, g) for qi in range(NT) for g in range(4)]
        for op_ in A_ops(*order[0]):
            op_()
        for i, (qi, g) in enumerate(order):
            nxt = A_ops(*order[i + 1]) if i + 1 < len(order) else []
            B_emit(qi, g, nxt)
            if g == 3:
                ob = qi % 2
                qsl = slice(qi * 128, (qi + 1) * 128)
                for c in range(KC):
                    self.tr(ps6b[:, c * 128:(c + 1) * 128], otok[ob][:, c * 128:(c + 1) * 128], self.ident_b[:],
                            [("otok", ob), "ident_b"], [("ps", 6)])
                self.copy(hT[:, :, qsl], ps6b[:, :].rearrange("p (c t) -> p c t", c=KC), [("ps", 6)],
                          [("hT", c, qi // 4) for c in range(KC)])
        wao = self.sb("nwao", [128, KC, D], BF16, off=self.scr0)
        wos = [self.sb(f"nwaos{i}", [128, D], F32, off=self.scr0 + 16384 + i * 4096) for i in range(2)]
        P.alias(["nwao", ("nwaos", 0), ("nwaos", 1)], ["nqT"] + [("nqT", sl, n) for sl in range(8) for n in range(NB)])
        self._scratch_keys |= {"nwao", ("nwaos", 0), ("nwaos", 1)} | set(("nqT", sl, n) for sl in range(8) for n in range(NB))
        for c in range(KC):
            b = c % 2
            self.dma(wos[b][:], self.w_b_out[j, c * 128:(c + 1) * 128, :], [], [("nwaos", b)])
            self.copy(wao[:, c, :], wos[b][:], [("nwaos", b)], ["nwao"], eng="pool")
        k = 0
        for dc in range(KC):
            for n in range(NB):
                tsl = slice(n * 512, (n + 1) * 512)
                pb = k % 2
                k += 1
                for c in range(KC):
                    self.mm(ps[pb][:, :], wao[:, c, dc * 128:(dc + 1) * 128], hT[:, c, tsl], c == 0, c == KC - 1,
                            ["nwao", ("hT", c, n)], [("ps", pb)])
                self.stt(xT[:, dc, tsl], ps[pb][:, :], self.mod[:, l, 16 + dc: 17 + dc], xT[:, dc, tsl], ALU.mult, ALU.add,
                         [("ps", pb), ("mod", l), ("xT", dc, n)], [("xT", dc, n)])

    def ffn(self, l, w_fi, w_fo):
        P, ps, xT, hT = self.P, self.ps, self.xT, self.hT
        scr0 = self.scr0
        groups = [(0, 6), (6, 6), (12, 5), (17, 5)]
        actT = self.sb("actT", [128, 6, S], BF16, off=scr0)
        wob = self.sb("wob", [128, 6, D], BF16, off=scr0 + 24576)
        wis = self.sb("wis", [128, KC, 256], F32, off=scr0 + 36864)
        wib = [self.sb(f"wib{i}", [128, KC, 256], BF16, off=scr0 + 45056 + i * 4096) for i in range(2)]
        wos = self.sb("wos", [128, D], F32, off=scr0 + 53248)
        sg = [self.sb(f"sg{i}", [128, 512], BF16, off=scr0 + 57344 + i * 1024) for i in range(2)]
        assert scr0 + 59392 <= self.scr_end
        keys = [("actT", j) for j in range(6)] + [("wob", j) for j in range(6)] + ["wis", ("wib", 0), ("wib", 1), "wos",
                                                                                 ("sg", 0), ("sg", 1)]
        self.use_scratch(keys)
        wiv = w_fi[l].rearrange("(c p) n -> p c n", p=128)
        ga = lambda c: self.mod[:, l, 40 + c: 41 + c]
        it = 0
        for (j0, nj) in groups:
            for jj in range(nj):
                j = j0 + jj
                wb = it % 2
                it += 1
                self.dma(wis[:, :, 0:128], wiv[:, :, j * 128:(j + 1) * 128], [], ["wis"])
                self.dma(wis[:, :, 128:256], wiv[:, :, DFF + j * 128: DFF + (j + 1) * 128], [], ["wis"])
                self.copy(wib[wb][:], wis[:], ["wis"], [("wib", wb)], eng="pool")
                for n in range(NB):
                    tsl = slice(n * 512, (n + 1) * 512)
                    pg, pu = (n % 2) * 2, (n % 2) * 2 + 1
                    for part, pb in ((0, pg), (1, pu)):
                        for c in range(KC):
                            self.mm(ps[pb][:, :], wib[wb][:, c, part * 128:(part + 1) * 128], hT[:, c, tsl],
                                    c == 0, c == KC - 1, [("wib", wb), ("hT", c, n)], [("ps", pb)])
                    sb_ = n % 2
                    self.act(sg[sb_][:], ps[pg][:, :], AF.Silu, [("ps", pg)], [("sg", sb_)])
                    self.tt(actT[:, jj, tsl], sg[sb_][:], ps[pu][:, :], ALU.mult,
                            [("sg", sb_), ("ps", pu)], [("actT", jj)])
                self.dma(wos[:], w_fo[l, j * 128:(j + 1) * 128, :], [], ["wos"])
                self.copy(wob[:, jj, :], wos[:], ["wos"], [("wob", jj)], eng="pool")
            k = 0
            for dc in range(KC):
                for n in range(NB):
                    tsl = slice(n * 512, (n + 1) * 512)
                    pb = 4 + k % 2
                    k += 1
                    for jj in range(nj):
                        self.mm(ps[pb][:, :], wob[:, jj, dc * 128:(dc + 1) * 128], actT[:, jj, tsl],
                                jj == 0, jj == nj - 1, [("wob", jj), ("actT", jj)], [("ps", pb)])
                    self.stt(xT[:, dc, tsl], ps[pb][:, :], ga(dc), xT[:, dc, tsl], ALU.mult, ALU.add,
                             [("ps", pb), ("mod", l), ("xT", dc, n)], [("xT", dc, n)])


_CACHE = {}


def _get_prog(key, **kw):
    if key not in _CACHE:
        b = B(**kw)
        b.build()
        _CACHE[key] = b
    return _CACHE[key]


def kernel(**inputs):
    b = _get_prog("full")
    hc = host_consts()
    in_maps = []
    for core in range(8):
        m = {}
        for name in b.din:
            shp = tuple(b.din[name].shape)
            if name == "x":
                m[name] = np.ascontiguousarray(inputs["x"][core])
            elif name == "c":
                m[name] = np.ascontiguousarray(inputs["c"][core:core + 1])
            elif name in hc:
                m[name] = hc[name]
            else:
                m[name] = np.ascontiguousarray(np.asarray(inputs[name], dtype=np.float32)).reshape(shp)
        in_maps.append(m)
    res = run_bass_kernel_spmd(b.nc, in_maps, core_ids=list(range(8)))
    return np.stack([r["out"] for r in res.results], axis=0)
```
